# Optimizing a Trainium2 kernel written in Bass

```python
import jax, jax.numpy as jnp
from jax import lax
import numpy as np

D_MODEL = 1024
BATCH = 2
SEQ = 8192
DEPTH = 2

MIX_WIDTH = 2 * D_MODEL
HEAD_DIM = 64
BLOCK = 128
SWA_HEADS = (MIX_WIDTH // 4) // HEAD_DIM
SWA_KV_HEADS = SWA_HEADS // 4
SWA_WINDOW = 128
FOX_HEADS = (MIX_WIDTH // 4) // HEAD_DIM
SSM_HEAD_DIM = 64
SSM_HEADS = (MIX_WIDTH // 2) // SSM_HEAD_DIM
SSM_GROUPS = 2
SSM_STATE = 128
SSM_CONV = 4
SSM_CHUNK = 128
MEM_TOKENS = 256
MEM_HEADS = 4
MEM_HEAD_DIM = D_MODEL // MEM_HEADS
ROPE_THETA = 10000.0
EPS = 1e-6
NEG_INF = -1e30

SWA_W = SWA_HEADS * HEAD_DIM
SWA_KV_W = SWA_KV_HEADS * HEAD_DIM
FOX_W = FOX_HEADS * HEAD_DIM
SSM_W = SSM_HEADS * SSM_HEAD_DIM
SSM_BC_W = SSM_GROUPS * SSM_STATE
SSM_CONV_W = SSM_W + 2 * SSM_BC_W
IN_SPLITS = (SWA_W, SWA_KV_W, SWA_KV_W, SWA_W,
             FOX_W, FOX_W, FOX_W, FOX_HEADS, FOX_W,
             SSM_W, SSM_CONV_W, SSM_HEADS)
IN_W = SWA_W + 2 * SWA_KV_W + SWA_W + 4 * FOX_W + FOX_HEADS + SSM_W + SSM_CONV_W + SSM_HEADS

kernel_name = "hymba_style_swa_fox_ssd_hybrid"


def _offsets(sizes):
    out, acc = [], 0
    for s in sizes[:-1]:
        acc += s
        out.append(acc)
    return out


def rms_norm(x, w):
    xf = x.astype(jnp.float32)
    y = xf * lax.rsqrt(jnp.mean(xf * xf, axis=-1, keepdims=True) + EPS)
    return (y * w.astype(jnp.float32)).astype(x.dtype)


def rope_tables(seq):
    pos = jnp.arange(seq, dtype=jnp.float32)
    inv = 1.0 / (ROPE_THETA ** (jnp.arange(0, HEAD_DIM, 2, dtype=jnp.float32) / HEAD_DIM))
    ang = pos[:, None] * inv[None, :]
    return jnp.cos(ang), jnp.sin(ang)


def apply_rope(x, cos, sin):
    xf = x.astype(jnp.float32)
    x1, x2 = jnp.split(xf, 2, axis=-1)
    c = cos[None, :, None, :]
    s = sin[None, :, None, :]
    return jnp.concatenate([x1 * c - x2 * s, x2 * c + x1 * s], axis=-1).astype(x.dtype)


def sliding_window_attention(q, k, v, sinks):
    b, s, hq, hd = q.shape
    hkv = k.shape[2]
    g = hq // hkv
    n = s // BLOCK
    qb = q.reshape(b, n, BLOCK, hkv, g, hd)
    kb = k.reshape(b, n, BLOCK, hkv, hd)
    vb = v.reshape(b, n, BLOCK, hkv, hd)
    pad = ((0, 0), (1, 0), (0, 0), (0, 0), (0, 0))
    kcat = jnp.concatenate([jnp.pad(kb, pad)[:, :-1], kb], axis=2)
    vcat = jnp.concatenate([jnp.pad(vb, pad)[:, :-1], vb], axis=2)
    scores = jnp.einsum('bnqkgd,bnskd->bnkgqs', qb, kcat).astype(jnp.float32) * (hd ** -0.5)
    qi = jnp.arange(BLOCK)[:, None]
    si = jnp.arange(2 * BLOCK)[None, :] - BLOCK
    rel = qi - si
    band = (rel >= 0) & (rel < SWA_WINDOW)
    valid = (jnp.arange(n)[:, None, None] * BLOCK + si[None]) >= 0
    mask = band[None] & valid
    scores = jnp.where(mask[None, :, None, None], scores, NEG_INF)
    sink = sinks.astype(jnp.float32).reshape(hkv, g)[None, None, :, :, None, None]
    m = jnp.maximum(jnp.max(scores, axis=-1, keepdims=True), sink)
    p = jnp.exp(scores - m)
    probs = (p / (jnp.sum(p, axis=-1, keepdims=True) + jnp.exp(sink - m))).astype(v.dtype)
    out = jnp.einsum('bnkgqs,bnskd->bnqkgd', probs, vcat)
    return out.reshape(b, s, hq, hd)


def forgetting_attention(q, k, v, log_f):
    b, s, h, hd = q.shape
    n = s // BLOCK
    c = jnp.cumsum(log_f, axis=1).transpose(0, 2, 1)
    qb = q.reshape(b, n, BLOCK, h, hd).swapaxes(0, 1)
    cq = c.reshape(b, h, n, BLOCK).transpose(2, 0, 1, 3)
    kpos = jnp.arange(s)
    scale = hd ** -0.5

    def one_block(args):
        qi, ci, i = args
        logits = jnp.einsum('bqhd,bkhd->bhqk', qi, k).astype(jnp.float32) * scale
        logits = logits + ci[..., :, None] - c[:, :, None, :]
        qpos = i * BLOCK + jnp.arange(BLOCK)
        mask = kpos[None, :] <= qpos[:, None]
        logits = jnp.where(mask, logits, NEG_INF)
        probs = jax.nn.softmax(logits, axis=-1).astype(v.dtype)
        return jnp.einsum('bhqk,bkhd->bqhd', probs, v)

    out = lax.map(one_block, (qb, cq, jnp.arange(n)))
    return out.swapaxes(0, 1).reshape(b, s, h, hd)


def causal_depthwise_conv(u, w, bias):
    kw = w.astype(u.dtype)[:, None, :]
    out = lax.conv_general_dilated(u, kw, window_strides=(1,), padding=[(SSM_CONV - 1, 0)],
                                   dimension_numbers=('NWC', 'WIO', 'NWC'),
                                   feature_group_count=u.shape[-1])
    return out + bias.astype(u.dtype)


def ssd_chunked(x, dt, a, bmat, cmat):
    b, s, h, p = x.shape
    g, nst = bmat.shape[2], bmat.shape[3]
    hg = h // g
    L = SSM_CHUNK
    nc = s // L
    xs = (x.astype(jnp.float32) * dt[..., None]).reshape(b, nc, L, g, hg, p)
    acs = jnp.cumsum((dt * a.astype(jnp.float32)).reshape(b, nc, L, g, hg), axis=2)
    bm = bmat.astype(jnp.float32).reshape(b, nc, L, g, nst)
    cm = cmat.astype(jnp.float32).reshape(b, nc, L, g, nst)
    diff = acs[:, :, :, None] - acs[:, :, None, :]
    causal = jnp.tril(jnp.ones((L, L), dtype=bool))
    decay = jnp.exp(jnp.where(causal[:, :, None, None], diff, NEG_INF))
    cb = jnp.einsum('bclgn,bcsgn->bclsg', cm, bm)
    y_diag = jnp.einsum('bclsg,bclsgh,bcsghp->bclghp', cb, decay, xs)
    decay_st = jnp.exp(acs[:, :, -1:] - acs)
    states = jnp.einsum('bclgn,bclgh,bclghp->bcghpn', bm, decay_st, xs)
    chunk_decay = jnp.exp(acs[:, :, -1])

    def step(carry, inp):
        st, dec = inp
        return carry * dec[..., None, None] + st, carry

    init = jnp.zeros((b, g, hg, p, nst), jnp.float32)
    _, prev = lax.scan(step, init, (states.swapaxes(0, 1), chunk_decay.swapaxes(0, 1)))
    prev = prev.swapaxes(0, 1)
    y_off = jnp.einsum('bclgn,bcghpn,bclgh->bclghp', cm, prev, jnp.exp(acs))
    return (y_diag + y_off).reshape(b, s, h, p)


def hybrid_mixer(h, w_in, b_forget, sinks, conv_w, conv_b, dt_bias, a_log, d_skip,
                 ssm_norm_w, w_out, cos, sin):
    b, s, _ = h.shape
    proj = h @ w_in.astype(h.dtype)
    (q_a, k_a, v_a, g_a, q_b, k_b, v_b, f_b, g_b, z_c, xbc_c, dt_c) = jnp.split(
        proj, _offsets(IN_SPLITS), axis=-1)
    qa = apply_rope(q_a.reshape(b, s, SWA_HEADS, HEAD_DIM), cos, sin)
    ka = apply_rope(k_a.reshape(b, s, SWA_KV_HEADS, HEAD_DIM), cos, sin)
    va = v_a.reshape(b, s, SWA_KV_HEADS, HEAD_DIM)
    ya = sliding_window_attention(qa, ka, va, sinks).reshape(b, s, SWA_W)
    ya = ya * jax.nn.silu(g_a)
    log_f = jax.nn.log_sigmoid(f_b.astype(jnp.float32) + b_forget.astype(jnp.float32))
    yb = forgetting_attention(q_b.reshape(b, s, FOX_HEADS, HEAD_DIM),
                              k_b.reshape(b, s, FOX_HEADS, HEAD_DIM),
                              v_b.reshape(b, s, FOX_HEADS, HEAD_DIM), log_f).reshape(b, s, FOX_W)
    yb = yb * jax.nn.silu(g_b)
    xbc = jax.nn.silu(causal_depthwise_conv(xbc_c, conv_w, conv_b))
    xs, bm, cm = jnp.split(xbc, [SSM_W, SSM_W + SSM_BC_W], axis=-1)
    xs = xs.reshape(b, s, SSM_HEADS, SSM_HEAD_DIM)
    bm = bm.reshape(b, s, SSM_GROUPS, SSM_STATE)
    cm = cm.reshape(b, s, SSM_GROUPS, SSM_STATE)
    dt = jax.nn.softplus(dt_c.astype(jnp.float32) + dt_bias.astype(jnp.float32))
    a = -jnp.exp(a_log.astype(jnp.float32))
    yc = ssd_chunked(xs, dt, a, bm, cm) + d_skip.astype(jnp.float32)[:, None] * xs.astype(jnp.float32)
    yc = yc.reshape(b, s, SSM_W).astype(h.dtype)
    yc = rms_norm(yc * jax.nn.silu(z_c), ssm_norm_w)
    y = jnp.concatenate([ya, yb, yc], axis=-1)
    return y @ w_out.astype(h.dtype)


def memory_cross_attention(h, mem_n, wq, wk, wv, wo):
    b, s, _ = h.shape
    m = mem_n.shape[1]
    q = (h @ wq.astype(h.dtype)).reshape(b, s, MEM_HEADS, MEM_HEAD_DIM)
    k = (mem_n @ wk.astype(h.dtype)).reshape(b, m, MEM_HEADS, MEM_HEAD_DIM)
    v = (mem_n @ wv.astype(h.dtype)).reshape(b, m, MEM_HEADS, MEM_HEAD_DIM)
    logits = jnp.einsum('bqhd,bkhd->bhqk', q, k).astype(jnp.float32) * (MEM_HEAD_DIM ** -0.5)
    probs = jax.nn.softmax(logits, axis=-1).astype(v.dtype)
    out = jnp.einsum('bhqk,bkhd->bqhd', probs, v).reshape(b, s, D_MODEL)
    return out @ wo.astype(h.dtype)


def setup_inputs(seed: int = 0) -> dict:
    key = jax.random.key(seed)
    ks = jax.random.split(key, 20)
    f32 = jnp.float32

    def nrm(k, shape, scale):
        return jax.random.normal(k, shape, f32) * scale

    dt0 = jnp.exp(jax.random.uniform(ks[7], (DEPTH, SSM_HEADS), f32, np.log(1e-3), np.log(1e-1)))
    return {
        "x": nrm(ks[0], (BATCH, SEQ, D_MODEL), 1.0),
        "mem": nrm(ks[1], (BATCH, MEM_TOKENS, D_MODEL), 1.0),
        "norm_mix_w": 1.0 + nrm(ks[2], (DEPTH, D_MODEL), 0.02),
        "w_in": nrm(ks[3], (DEPTH, D_MODEL, IN_W), D_MODEL ** -0.5),
        "b_forget": jax.random.uniform(ks[4], (DEPTH, FOX_HEADS), f32, 1.0, 6.0),
        "swa_sinks": nrm(ks[5], (DEPTH, SWA_HEADS), 0.5),
        "conv_w": nrm(ks[6], (DEPTH, SSM_CONV, SSM_CONV_W), SSM_CONV ** -0.5),
        "conv_b": nrm(ks[8], (DEPTH, SSM_CONV_W), 0.02),
        "dt_bias": dt0 + jnp.log(-jnp.expm1(-dt0)),
        "a_log": jnp.log(jax.random.uniform(ks[9], (DEPTH, SSM_HEADS), f32, 1.0, 16.0)),
        "d_skip": 1.0 + nrm(ks[10], (DEPTH, SSM_HEADS), 0.1),
        "ssm_norm_w": 1.0 + nrm(ks[11], (DEPTH, SSM_W), 0.02),
        "w_out": nrm(ks[12], (DEPTH, MIX_WIDTH, D_MODEL), MIX_WIDTH ** -0.5),
        "norm_xq_w": 1.0 + nrm(ks[13], (DEPTH, D_MODEL), 0.02),
        "norm_mem_w": 1.0 + nrm(ks[14], (DEPTH, D_MODEL), 0.02),
        "w_mq": nrm(ks[15], (DEPTH, D_MODEL, D_MODEL), D_MODEL ** -0.5),
        "w_mk": nrm(ks[16], (DEPTH, D_MODEL, D_MODEL), D_MODEL ** -0.5),
        "w_mv": nrm(ks[17], (DEPTH, D_MODEL, D_MODEL), D_MODEL ** -0.5),
        "w_mo": nrm(ks[18], (DEPTH, D_MODEL, D_MODEL), D_MODEL ** -0.5),
        "final_norm_w": 1.0 + nrm(ks[19], (D_MODEL,), 0.02),
    }


def reference(x, mem, norm_mix_w, w_in, b_forget, swa_sinks, conv_w, conv_b, dt_bias, a_log,
              d_skip, ssm_norm_w, w_out, norm_xq_w, norm_mem_w, w_mq, w_mk, w_mv, w_mo,
              final_norm_w):
    cos, sin = rope_tables(x.shape[1])
    for l in range(DEPTH):
        h = rms_norm(x, norm_mix_w[l])
        x = x + hybrid_mixer(h, w_in[l], b_forget[l], swa_sinks[l], conv_w[l], conv_b[l],
                             dt_bias[l], a_log[l], d_skip[l], ssm_norm_w[l], w_out[l], cos, sin)
        hq = rms_norm(x, norm_xq_w[l])
        mem_n = rms_norm(mem, norm_mem_w[l])
        x = x + memory_cross_attention(hq, mem_n, w_mq[l], w_mk[l], w_mv[l], w_mo[l])
    return rms_norm(x, final_norm_w)
```

```python
import numpy as np
import ml_dtypes
import concourse.bass as bass
import concourse.mybir as mybir
from concourse.bass_utils import run_bass_kernel_spmd
from contextlib import ExitStack
import types
import os

F32 = mybir.dt.float32
BF16 = mybir.dt.bfloat16
AF = mybir.ActivationFunctionType
ALU = mybir.AluOpType
AX = mybir.AxisListType

SEM_EPOCH = 30000
NO_SELF_SYNC = tuple(os.environ.get("MK_NOSELF", "pe").split(","))
SKIP_SAME_ENGINE_WAW = os.environ.get("MK_WAW", "1") == "1"
EMBED_WAIT = os.environ.get("MK_EMBED", "1") == "1"
SAME_ENGINE_SYNC = os.environ.get("MK_SES", "1") == "1"


class Res:
    __slots__ = ("name", "last_w", "readers", "excl")

    def __init__(self, name):
        self.name = name
        self.last_w = None
        self.readers = []
        self.excl = False


class Tile:
    def __init__(self, t, name):
        self.t = t
        self.r = Res(name)

    def __getitem__(self, idx):
        return self.t[idx]


def _res(x):
    return x.r if isinstance(x, Tile) else x


def _freeze(fn):
    if fn.__closure__ is None:
        return fn
    cells = []
    for c in fn.__closure__:
        try:
            cells.append(types.CellType(c.cell_contents))
        except ValueError:
            cells.append(c)
    return types.FunctionType(fn.__code__, fn.__globals__, fn.__name__, fn.__defaults__, tuple(cells))


class Op:
    __slots__ = ("idx", "eng", "fn", "deps", "is_dma", "sig", "dma_sem", "dma_val", "ndma", "needs_sig", "dma_phys")


class Prog:
    def __init__(self, nc, stack):
        self.nc = nc
        self.stack = stack
        self.ops = []
        self.dma_sems = {}
        self.phys = []
        self.free_phys = {}

    def sbuf(self, name, shape, dtype):
        t = self.stack.enter_context(self.nc.sbuf_tensor(name, list(shape), dtype))
        return Tile(t, name)

    def psum(self, name, shape, dtype):
        t = self.stack.enter_context(self.nc.psum_tensor(name, list(shape), dtype))
        tl = Tile(t, name)
        tl.r.excl = True
        return tl

    def dram(self, name, shape, dtype, kind):
        t = self.nc.dram_tensor(name, list(shape), dtype, kind=kind)
        return Tile(t.ap(), name)

    def _record(self, eng, fn, reads, writes, is_dma=False, ndma=1, sem_key=None):
        op = Op()
        op.idx = len(self.ops)
        op.eng = eng
        op.fn = fn
        op.is_dma = is_dma
        op.ndma = ndma
        op.sig = None
        op.needs_sig = False
        op.dma_sem = None
        op.dma_val = None
        deps = set()
        waw = set()
        reads = [_res(r) for r in reads]
        writes = [_res(w) for w in writes]
        for r in reads:
            if r.last_w is not None:
                deps.add(r.last_w)
            if r.excl:
                for rd in r.readers:
                    if self.ops[rd].eng != eng:
                        deps.add(rd)
        for w in writes:
            if w.last_w is not None:
                if SKIP_SAME_ENGINE_WAW and w.last_w not in deps and self.ops[w.last_w].eng == eng and not is_dma and not self.ops[w.last_w].is_dma:
                    waw.add(w.last_w)
                else:
                    deps.add(w.last_w)
            for rd in w.readers:
                deps.add(rd)
        waw -= deps
        deps.discard(op.idx)
        op.deps = sorted(deps)
        for r in reads:
            r.readers.append(op.idx)
        for w in writes:
            w.last_w = op.idx
            w.readers = []
        if is_dma:
            key = sem_key if sem_key is not None else (writes[0] if writes else reads[0])
            key = (_res(key), eng)
            if key not in self.dma_sems:
                fl = self.free_phys.setdefault(eng, [])
                if fl:
                    pi = fl.pop()
                else:
                    pi = len(self.phys)
                    self.phys.append([None, 0, eng])
                self.dma_sems[key] = [pi, self.phys[pi][1]]
            ent = self.dma_sems[key]
            ent[1] += 16 * ndma
            self.phys[ent[0]][1] = ent[1]
            op.dma_sem = key
            op.dma_val = ent[1]
            op.dma_phys = ent[0]
        self.ops.append(op)
        return op

    def op(self, eng, fn, reads=(), writes=()):
        return self._record(eng, _freeze(fn), reads, writes)

    def barrier(self):
        last = {}
        for o in self.ops:
            if o.is_dma:
                last[("dma", o.dma_sem)] = o.idx
            else:
                last[o.eng] = o.idx
        extra = sorted(set(last.values()))
        for eng in ("sp", "pe", "act", "dve", "pool"):
            o = self._record(eng, lambda e: e.nop(), [], [])
            o.deps = sorted(set(o.deps) | set(extra))
        self.retired = getattr(self, "retired", {})
        for key, ent in self.dma_sems.items():
            self.retired[key] = ent
            self.free_phys.setdefault(self.phys[ent[0]][2], []).append(ent[0])
        self.dma_sems = {}

    def dma(self, q, fns, reads=(), writes=(), sem_key=None):
        if callable(fns):
            fns = [fns]
        fns = [_freeze(f) for f in fns]
        return self._record(q, fns, reads, writes, is_dma=True, ndma=len(fns), sem_key=sem_key)

    def emit(self):
        nc = self.nc
        ops = self.ops
        for op in ops:
            for d in op.deps:
                dop = ops[d]
                if dop.is_dma:
                    continue
                if dop.eng == op.eng and (dop.eng in NO_SELF_SYNC or not SAME_ENGINE_SYNC) and not op.is_dma:
                    continue
                dop.needs_sig = True
        counters = {}
        for op in ops:
            if op.is_dma or not op.needs_sig:
                continue
            c = counters.get(op.eng, 0) + 1
            counters[op.eng] = c
            op.sig = c
        eng_sems = {}
        for eng, c in counters.items():
            n_ep = (c + SEM_EPOCH - 1) // SEM_EPOCH
            eng_sems[eng] = [self.stack.enter_context(nc.semaphore(f"s_{eng}_{i}")) for i in range(n_ep)]
        for i, ph in enumerate(self.phys):
            ph[0] = self.stack.enter_context(nc.semaphore(f"d{i}"))
        self.n_sems = sum(len(v) for v in eng_sems.values()) + len(self.phys)
        self.max_dma_total = max([ph[1] for ph in self.phys] + [0])
        allk = dict(getattr(self, "retired", {}))
        allk.update(self.dma_sems)

        def dsem(key):
            return self.phys[allk[key][0]][0]

        def sem_of(eng, sig):
            ep = (sig - 1) // SEM_EPOCH
            return eng_sems[eng][ep], sig - ep * SEM_EPOCH

        per_eng = {}
        for op in ops:
            per_eng.setdefault(op.eng, []).append(op)
        handles = {"pe": "tensor", "act": "scalar", "dve": "vector", "pool": "gpsimd", "sp": "sync"}
        final_dma = [(ph[0], ph[1]) for ph in self.phys]
        stats = {"waits": 0, "instr": 0}

        def run_engine(engname, e):
            seen_eng = {}
            seen_dma = {}
            for op in per_eng.get(engname, []):
                need_eng = {}
                need_dma = {}
                for d in op.deps:
                    dop = ops[d]
                    if dop.is_dma:
                        k = dop.dma_phys
                        if dop.dma_val > need_dma.get(k, 0):
                            need_dma[k] = dop.dma_val
                    else:
                        if dop.sig is None:
                            continue
                        if dop.eng == engname and not op.is_dma and (engname in NO_SELF_SYNC or not SAME_ENGINE_SYNC):
                            continue
                        if dop.sig > need_eng.get(dop.eng, 0):
                            need_eng[dop.eng] = dop.sig
                wl = []
                for se, sig in need_eng.items():
                    if seen_eng.get(se, 0) >= sig:
                        continue
                    seen_eng[se] = sig
                    wl.append(sem_of(se, sig))
                for k, val in need_dma.items():
                    if seen_dma.get(k, 0) >= val:
                        continue
                    seen_dma[k] = val
                    wl.append((self.phys[k][0], val))
                emb = None
                if wl and EMBED_WAIT and not op.is_dma:
                    emb = wl.pop()
                for s_, v_ in wl:
                    e.wait_ge(s_, v_)
                    stats["waits"] += 1
                if op.is_dma:
                    s = self.phys[op.dma_phys][0]
                    for f in op.fn:
                        f(e).then_inc(s, 16)
                        stats["instr"] += 1
                else:
                    ins = op.fn(e)
                    stats["instr"] += 1
                    if emb is not None:
                        ins._wait_ge(emb[0], emb[1])
                    if op.sig is not None:
                        s, _ = sem_of(op.eng, op.sig)
                        ins.then_inc(s, 1)
            if engname == "sp":
                for s, v in final_dma:
                    e.wait_ge(s, v)

        with nc.Block() as block:
            for engname in ("sp", "pe", "act", "dve", "pool"):
                if engname not in per_eng and engname != "sp":
                    continue
                deco = getattr(block, handles[engname])

                def body(e, engname=engname):
                    run_engine(engname, e)

                deco(body)
        self.stats = stats
        return stats


D = 1024
SEQ = 8192
NB_ = 2
DEPTH = 2
HD = 64
MEMT = 256
EPS = 1e-6
O_QA, O_KA, O_VA, O_GA = 0, 512, 640, 768
O_QB, O_KB, O_VB, O_FB, O_GB = 1280, 1792, 2304, 2816, 2824
O_Z, O_XBC, O_DT = 3336, 4360, 5896
FM_GROUPS = [("qa", 128), ("qas", 128), ("ka", 128), ("kas", 128), ("ga0", 64), ("ga1", 64),
             ("qb0", 64), ("qb1", 64), ("kb0", 64), ("kb1", 64), ("gb0", 64), ("gb1", 64), ("fb", 2),
             ("z0", 128), ("z1", 128), ("x0", 128), ("x1", 128), ("Bm", 128), ("Cm", 128)]
FM_OFF = {}
_o = 0
for _n, _m in FM_GROUPS:
    FM_OFF[_n] = (_o, _m)
    _o += _m
NFM = _o
NTM = 196
NPV = 72
TT = 512
FOX_D = int(os.environ.get("MK_FOXD", "3"))


def fm_cols(hg):
    h0, h1 = 2 * hg, 2 * hg + 1
    kv = hg // 2
    g = hg // 2
    r = np.arange(64)
    sw = np.concatenate([np.arange(32, 64), np.arange(0, 32)])
    a128 = np.arange(128)
    cols = {
        "qa": np.concatenate([O_QA + h0 * 64 + r, O_QA + h1 * 64 + r]),
        "qas": np.concatenate([O_QA + h0 * 64 + sw, O_QA + h1 * 64 + sw]),
        "ka": np.concatenate([O_KA + kv * 64 + r, O_KA + kv * 64 + r]),
        "kas": np.concatenate([O_KA + kv * 64 + sw, O_KA + kv * 64 + sw]),
        "ga0": O_GA + h0 * 64 + r, "ga1": O_GA + h1 * 64 + r,
        "qb0": O_QB + h0 * 64 + r, "qb1": O_QB + h1 * 64 + r,
        "kb0": O_KB + h0 * 64 + r, "kb1": O_KB + h1 * 64 + r,
        "gb0": O_GB + h0 * 64 + r, "gb1": O_GB + h1 * 64 + r,
        "fb": np.array([O_FB + h0, O_FB + h1]),
        "z0": O_Z + 256 * hg + a128, "z1": O_Z + 256 * hg + 128 + a128,
        "x0": O_XBC + 256 * hg + a128, "x1": O_XBC + 256 * hg + 128 + a128,
        "Bm": O_XBC + 1024 + 128 * g + a128, "Cm": O_XBC + 1280 + 128 * g + a128,
    }
    fm = np.concatenate([cols[n] for n, _ in FM_GROUPS])
    tm = np.concatenate([O_VB + h0 * 64 + r, O_VB + h1 * 64 + r, O_VA + kv * 64 + r, O_DT + 4 * hg + np.arange(4)])
    return fm, tm


def host_pvec(inp, l, hg):
    pv = np.zeros((128, NPV), np.float32)
    pv[:, 0:8] = inp["norm_mix_w"][l].reshape(8, 128).T
    g = hg // 2
    a128 = np.arange(128)
    chans = [256 * hg + a128, 256 * hg + 128 + a128, 1024 + 128 * g + a128, 1280 + 128 * g + a128]
    for c, ch in enumerate(chans):
        pv[:, 8 + 4 * c:12 + 4 * c] = inp["conv_w"][l][:, ch].T
        pv[:, 24 + c] = inp["conv_b"][l][ch]
    for j in range(2):
        pv[:, 28 + j] = np.repeat(inp["d_skip"][l][4 * hg + 2 * j:4 * hg + 2 * j + 2], 64)
    pv[:, 30:46] = np.tile(inp["dt_bias"][l][4 * hg:4 * hg + 4], 4)[None, :]
    pv[:, 46:62] = np.tile(inp["a_log"][l][4 * hg:4 * hg + 4], 4)[None, :]
    pv[0:2, 62] = inp["b_forget"][l][2 * hg:2 * hg + 2]
    pv[:, 63] = inp["swa_sinks"][l][2 * hg]
    pv[:, 64] = inp["swa_sinks"][l][2 * hg + 1]
    return pv


def host_consts(T):
    c = {}
    c["ident"] = np.eye(128, dtype=np.float32)
    s = np.arange(128)[:, None]
    t = np.arange(128)[None, :]
    triu = (s <= t).astype(np.float32)
    c["triu"] = triu
    c["maskS"] = (np.concatenate([triu, 1.0 - triu], axis=1) * np.float32(-240000.0)).astype(np.float32)
    pos = np.arange(T, dtype=np.float32)
    inv = (1.0 / (np.float32(10000.0) ** (np.arange(0, HD, 2, dtype=np.float32) / np.float32(HD)))).astype(np.float32)
    ang = (pos[:, None] * inv[None, :]).astype(np.float32)
    cosv = np.cos(ang).astype(np.float32).T
    sinv = np.sin(ang).astype(np.float32).T
    c["cos2"] = np.concatenate([cosv, cosv, cosv, cosv], axis=0)
    c["sin2"] = np.concatenate([-sinv, sinv, -sinv, sinv], axis=0)
    return c


class Ctx:
    pass


def alloc_common(P, T):
    C = Ctx()
    C.T = T
    C.ident_d = P.dram("ident", [128, 128], F32, "ExternalInput")
    C.triu_d = P.dram("triu", [128, 128], F32, "ExternalInput")
    C.identf = P.sbuf("identf", [128, 128], F32)
    C.identb = P.sbuf("identb", [128, 128], BF16)
    C.triuf = P.sbuf("triuf", [128, 128], F32)
    C.triub = P.sbuf("triub", [128, 128], BF16)
    C.onesf = P.sbuf("onesf", [128, 512], F32)
    C.onesb = P.sbuf("onesb", [128, 128], BF16)
    P.dma("sp", lambda e: e.dma_start(out=C.identf[:], in_=C.ident_d[:]), writes=[C.identf])
    P.dma("sp", lambda e: e.dma_start(out=C.triuf[:], in_=C.triu_d[:]), writes=[C.triuf])
    P.op("dve", lambda e: e.tensor_copy(out=C.identb[:], in_=C.identf[:]), reads=[C.identf], writes=[C.identb])
    P.op("dve", lambda e: e.tensor_copy(out=C.triub[:], in_=C.triuf[:]), reads=[C.triuf], writes=[C.triub])
    P.op("pool", lambda e: e.memset(C.onesf[:], 1.0), writes=[C.onesf])
    P.op("pool", lambda e: e.memset(C.onesb[:], 1.0), writes=[C.onesb])
    C.PJ = [P.psum(f"PJ{i}", [128, 512], F32) for i in range(2)]
    C.SC = [P.psum(f"SC{i}", [128, 512], F32) for i in range(2)]
    C.OT = [P.psum(f"OT{i}", [128, 512], F32) for i in range(2)]
    C.MS = [P.psum(f"MS{i}", [128, 512], F32) for i in range(2)]
    return C


import os
PARTS = set(os.environ.get("MK_PARTS", "swa,fox,ssd,f").split(","))


def emit_phaseA(P, C, hT_d, wfm_d, wtm_d, pvec_d, cos_d, sin_d, maskS_d, yT_d, tag="A"):
    T = C.T
    NT = T // TT
    NBLK = T // 128
    PJ, SC, OT = C.PJ, C.SC, C.OT
    MS0, MS1 = C.MS
    CBp = Tile(MS0[:, 0:128], tag + "CBp")
    XTp = Tile(MS0[:, 128:384], tag + "XTp")
    ACp = Tile(MS0[:, 384:400], tag + "ACp")
    STp = Tile(MS1[:, 0:256], tag + "STp")
    BTp = Tile(MS1[:, 256:320].bitcast(BF16), tag + "BTp")
    CBp.r = XTp.r = ACp.r = MS0.r
    XTs = [Tile(PJ[0][:, 0:256], tag + "XT0"), Tile(PJ[1][:, 0:256], tag + "XT1")]
    XTs[0].r = PJ[0].r
    XTs[1].r = PJ[1].r
    STp.r = BTp.r = MS1.r

    pv = P.sbuf(tag + "pv", [128, NPV], F32)
    P.dma("sp", lambda e: e.dma_start(out=pv[:], in_=pvec_d[:]), writes=[pv])
    aneg = P.sbuf(tag + "aneg", [128, 16], F32)
    P.op("act", lambda e: e.activation(out=aneg[:], in_=pv[:, 46:62], func=AF.Exp), reads=[pv], writes=[aneg])
    P.op("dve", lambda e: e.tensor_scalar(out=aneg[:], in0=aneg[:], scalar1=-1.0, scalar2=None, op0=ALU.mult), reads=[aneg], writes=[aneg])
    negb = P.sbuf(tag + "negb", [2, 1], F32)
    P.op("dve", lambda e: e.tensor_scalar(out=negb[:], in0=pv[0:2, 62:63], scalar1=-1.0, scalar2=None, op0=ALU.mult), reads=[pv], writes=[negb])
    esink = P.sbuf(tag + "esink", [128, 2], F32)
    P.op("act", lambda e: e.activation(out=esink[:], in_=pv[:, 63:65], func=AF.Exp), reads=[pv], writes=[esink])
    maskS = P.sbuf(tag + "maskS", [128, 256], BF16)
    maskSf = P.sbuf(tag + "maskSf", [128, 256], F32)
    P.dma("sp", lambda e: e.dma_start(out=maskSf[:], in_=maskS_d[:]), writes=[maskSf])
    P.op("dve", lambda e: e.tensor_copy(out=maskS[:], in_=maskSf[:]), reads=[maskSf], writes=[maskS])

    Wb = P.sbuf(tag + "Wb", [128, 8, NFM], BF16)
    Wt = P.sbuf(tag + "Wt", [128, 8, NTM], BF16)
    stg = [P.sbuf(tag + "stg", [128, NFM], F32)] * 2
    wfm_v = wfm_d.t.rearrange("(k p) n -> p k n", p=128)
    for k in range(8):
        s_ = stg[k % 2]
        P.dma("sp", lambda e, k=k, s_=s_: e.dma_start(out=s_[:], in_=wfm_v[:, k, :]), writes=[s_])
        if k % 2 == 0:
            P.op("dve", lambda e, k=k, s_=s_: e.tensor_scalar(out=Wb[:, k, :], in0=s_[:], scalar1=pv[:, k:k + 1], scalar2=None, op0=ALU.mult),
                 reads=[s_, pv], writes=[Wb])
        else:
            P.op("act", lambda e, k=k, s_=s_: e.activation(out=Wb[:, k, :], in_=s_[:], func=AF.Copy, scale=pv[:, k:k + 1]),
                 reads=[s_, pv], writes=[Wb])
    s_ = stg[0]
    P.dma("sp", lambda e: e.dma_start(out=s_[:, 0:8 * NTM].rearrange("p (k n) -> p k n", k=8), in_=wtm_d.t.rearrange("(k p) n -> p k n", p=128)), writes=[s_])
    for k in range(8):
        P.op("dve", lambda e, k=k: e.tensor_scalar(out=Wt[:, k, :], in0=s_[:, k * NTM:(k + 1) * NTM], scalar1=pv[:, k:k + 1], scalar2=None, op0=ALU.mult),
             reads=[s_, pv], writes=[Wt])

    KA = [P.sbuf(tag + f"KA{h}", [70, T], BF16) for h in range(2)]
    KAq = [[Res(f"KAq{h}_{t}") for t in range(NT)] for h in range(2)]
    KAc = [[Res(f"KAc{h}_{t}") for t in range(NT)] for h in range(2)]
    VF = P.sbuf(tag + "VF", [128, NBLK, 2, 72], BF16)
    VFr = [Res(f"VF_{t}") for t in range(NT)]
    KR = P.sbuf(tag + "KR", [128, T], BF16)
    KRr = [Res(f"KR_{t}") for t in range(NT)]
    VS = P.sbuf(tag + "VS", [128, NBLK, 72], BF16)
    VSr = [Res(f"VS_{t}") for t in range(NT)]
    QA = [P.sbuf(tag + f"QA{h}", [70, TT], BF16) for h in range(2)]
    QAc = [Res(f"QAc{h}") for h in range(2)]
    ST = P.sbuf(tag + "ST", [128, 4, 64], F32)
    STb = P.sbuf(tag + "STb", [128, 4, 64], BF16)
    ub = [P.sbuf(tag + f"ub{c}", [128, 3 + TT], F32) for c in range(4)]
    carry = P.sbuf(tag + "carry", [2, 1], F32)

    initr = Res("init")
    for h in range(2):
        P.op("pool", lambda e, h=h: e.memset(KA[h][64:70, :], 1.0), writes=[KAc[h][t] for t in range(NT)])
        P.op("pool", lambda e, h=h: e.memset(QA[h][64:70, :], 1.0), writes=[QAc[h]])
    P.op("pool", lambda e: e.memset(VF[:], 1.0), writes=VFr)
    P.op("pool", lambda e: e.memset(VS[:], 1.0), writes=VSr)
    P.op("pool", lambda e: e.memset(ST[:], 0.0), writes=[ST])
    P.op("pool", lambda e: e.memset(STb[:], 0.0), writes=[STb])
    for c in range(4):
        P.op("pool", lambda e, c=c: e.memset(ub[c][:, 0:3], 0.0), writes=[ub[c]])
    P.op("pool", lambda e: e.memset(carry[:], 0.0), writes=[carry])

    hTt = [P.sbuf(tag + f"hTt{i}", [128, 8, TT], BF16) for i in range(2)]
    cosT = [P.sbuf(tag + "cosT", [128, TT], F32)] * 2
    sinT = [P.sbuf(tag + "sinT", [128, TT], F32)] * 2
    QR = P.sbuf(tag + "QR", [128, TT], BF16)
    sg = {n: P.sbuf(tag + "sg_" + n, [m, TT], F32) for n, m in (("ga0", 64), ("ga1", 64), ("gb0", 64), ("gb1", 64), ("z0", 128), ("z1", 128))}
    xc = [P.sbuf(tag + f"xc{j}", [128, TT], F32) for j in range(2)]
    acc = P.sbuf(tag + "acc", [128, TT], F32)
    BcT = P.sbuf(tag + "BcT", [128, TT], BF16)
    CcT = P.sbuf(tag + "CcT", [128, TT], BF16)
    fe = P.sbuf(tag + "fe", [2, TT], F32)
    cc = P.sbuf(tag + "cc", [2, TT], F32)
    cr = fe
    csp = [P.sbuf(tag + f"csp{i}", [2, TT], BF16) for i in range(3)]
    cspn = [P.sbuf(tag + "cspn", [2, TT], BF16)] * 3
    dtraw = P.sbuf(tag + "dtraw", [128, 16], F32)
    dtt = P.sbuf(tag + "dtt", [128, 16], F32)
    dA = P.sbuf(tag + "dA", [128, 16], F32)
    acsc = P.sbuf(tag + "acsc", [128, 16], F32)
    PT = [P.sbuf(tag + f"PT{i}", [128, TT], BF16) for i in range(5)]
    denr = P.sbuf(tag + "denr", [65, TT], F32)
    rbc = P.sbuf(tag + "rbc", [64, TT], F32)
    yo = [P.sbuf(tag + f"yo{i}", [64, TT], BF16) for i in range(4)]
    vo = [P.sbuf(tag + f"vo{j}", [128, TT], BF16) for j in range(2)]
    Dabs = P.sbuf(tag + "Dabs", [128, 4, 128], F32)
    Ee = Dabs
    eA = P.sbuf(tag + "eA", [128, 4, 128], F32)
    CBm = P.sbuf(tag + "CBm", [128, 128], F32)
    MT2 = [P.sbuf(tag + f"MT{i}", [128, 4, 128], BF16) for i in range(2)]
    Cs2 = [P.sbuf(tag + f"Cs{i}", [128, 4, 128], BF16) for i in range(2)]
    Bc2 = [P.sbuf(tag + f"Bc{i}", [128, 128], BF16) for i in range(2)]
    cd2 = [P.sbuf(tag + f"cd{i}", [128, 4], F32) for i in range(2)]
    d4 = P.sbuf(tag + "d4", [128, 4], F32)
    dtd = P.sbuf(tag + "dtd", [128, 4], F32)
    xs2 = [P.sbuf(tag + f"xs{i}", [128, 4, 64], BF16) for i in range(2)]
    xsd2 = [P.sbuf(tag + f"xsd{i}", [128, 4, 64], BF16) for i in range(2)]
    yf = P.sbuf(tag + "yf", [128, TT], F32)
    t1, t2 = yf, acc

    hT_v = hT_d.t.rearrange("(k p) t -> p k t", p=128)
    PJB = [PJ[0], PJ[1], C.MS[0], C.MS[1]] if os.environ.get("MK_PJ4", "1") == "1" else [PJ[0], PJ[1]]
    pj_i = [0]
    pt_i = [0]

    def load_tile(tt):
        b_ = tt % 2
        sl = slice(tt * TT, (tt + 1) * TT)
        P.dma("sp", lambda e: e.dma_start(out=hTt[b_][:], in_=hT_v[:, :, sl]), writes=[hTt[b_]])

    def load_tables(tt):
        sl = slice(tt * TT, (tt + 1) * TT)
        P.dma("sp", lambda e: e.dma_start(out=cosT[0][:], in_=cos_d[:, sl]), writes=[cosT[0]])
        P.dma("sp", lambda e: e.dma_start(out=sinT[0][:], in_=sin_d[:, sl]), writes=[sinT[0]])

    def proj_fm(tt, name):
        off, m = FM_OFF[name]
        bank = PJB[pj_i[0] % len(PJB)]
        pj_i[0] += 1
        h_ = hTt[tt % 2]
        for k in range(8):
            P.op("pe", lambda e, k=k: e.matmul(bank[0:m, :], lhsT=Wb[:, k, off:off + m], rhs=h_[:, k, :], start=(k == 0), stop=(k == 7)),
                 reads=[Wb, h_], writes=[bank])
        return bank, m

    def normalize(otb, sgt, sink_col, yout, gate_part):
        if sink_col is None:
            P.op("act", lambda e: e.activation(out=denr[64:65, :], in_=otb[64:65, :], func=AF.Ln), reads=[otb], writes=[denr])
        else:
            P.op("act", lambda e: e.activation(out=denr[64:65, :], in_=otb[64:65, :], func=AF.Ln, bias=esink[64:65, sink_col:sink_col + 1]),
                 reads=[otb, esink], writes=[denr])
        P.op("act", lambda e: e.activation(out=denr[64:65, :], in_=denr[64:65, :], func=AF.Exp, scale=-1.0), reads=[denr], writes=[denr])
        bcb = SC[0]
        P.op("pe", lambda e: e.matmul(bcb[0:64, :], lhsT=C.onesf[64:65, 0:64], rhs=denr[64:65, :], start=True, stop=True),
             reads=[C.onesf, denr], writes=[bcb])
        P.op("dve", lambda e: e.tensor_tensor(out=rbc[:], in0=bcb[0:64, :], in1=sgt[gate_part], op=ALU.mult), reads=[bcb, sgt], writes=[rbc])
        P.op("dve", lambda e: e.tensor_tensor(out=yout[:], in0=otb[0:64, :], in1=rbc[:], op=ALU.mult), reads=[otb, rbc], writes=[yout])

    for tt in range(NT):
        if tt == 0:
            load_tile(0)
            load_tables(0)
        if tt + 1 < NT:
            load_tile(tt + 1)
        b_ = tt % 2
        sl = slice(tt * TT, (tt + 1) * TT)
        cs_, sn_ = cosT[b_], sinT[b_]

        for (nm, nms, dst, dres) in (("qa", "qas", QR, QR), ("ka", "kas", KR, KRr[tt])):
            bank, _ = proj_fm(tt, nm)
            P.op("dve", lambda e, bank=bank: e.tensor_tensor(out=t1[:], in0=bank[:], in1=cs_[:], op=ALU.mult), reads=[bank, cs_], writes=[t1])
            bank2, _ = proj_fm(tt, nms)
            P.op("dve", lambda e, bank2=bank2: e.tensor_tensor(out=t2[:], in0=bank2[:], in1=sn_[:], op=ALU.mult), reads=[bank2, sn_], writes=[t2])
            if dst is QR:
                P.op("pool", lambda e: e.tensor_tensor(out=QR[:], in0=t1[:], in1=t2[:], op=ALU.add), reads=[t1, t2], writes=[QR])
            else:
                P.op("pool", lambda e: e.tensor_tensor(out=KR[:, sl], in0=t1[:], in1=t2[:], op=ALU.add), reads=[t1, t2], writes=[KRr[tt]])
        if tt + 1 < NT:
            load_tables(tt + 1)
        for nm in ("ga0", "ga1", "gb0", "gb1", "z0", "z1"):
            bank, m = proj_fm(tt, nm)
            P.op("act", lambda e, bank=bank, m=m, nm=nm: e.activation(out=sg[nm][:], in_=bank[0:m, :], func=AF.Silu), reads=[bank], writes=[sg[nm]])
        for h in range(2):
            bank, _ = proj_fm(tt, f"qb{h}")
            P.op("act", lambda e, bank=bank, h=h: e.activation(out=QA[h][0:64, :], in_=bank[0:64, :], func=AF.Copy, scale=0.125), reads=[bank], writes=[QA[h]])
            bank, _ = proj_fm(tt, f"kb{h}")
            P.op("act", lambda e, bank=bank, h=h: e.copy(out=KA[h][0:64, sl], in_=bank[0:64, :]), reads=[bank], writes=[KAq[h][tt]])
        bank, _ = proj_fm(tt, "fb")
        P.op("act", lambda e, bank=bank: e.activation(out=fe[:], in_=bank[0:2, :], func=AF.Exp, scale=-1.0, bias=negb[:]), reads=[bank, negb], writes=[fe])
        P.op("act", lambda e: e.activation(out=fe[:], in_=fe[:], func=AF.Ln, bias=1.0), reads=[fe], writes=[fe])
        P.op("dve", lambda e: e.tensor_tensor_scan(out=cc[:], data0=C.onesf[0:2, 0:TT], data1=fe[:], initial=carry[:], op0=ALU.mult, op1=ALU.subtract),
             reads=[fe, carry, C.onesf], writes=[cc])
        P.op("dve", lambda e: e.tensor_copy(out=carry[:], in_=cc[:, TT - 1:TT]), reads=[cc], writes=[carry])
        P.op("act", lambda e: e.copy(out=csp[0][:], in_=cc[:]), reads=[cc], writes=[csp[0]])
        P.op("dve", lambda e: e.tensor_tensor(out=cr[:], in0=cc[:], in1=csp[0][:], op=ALU.subtract), reads=[cc, csp[0]], writes=[cr])
        P.op("act", lambda e: e.copy(out=csp[1][:], in_=cr[:]), reads=[cr], writes=[csp[1]])
        P.op("dve", lambda e: e.tensor_tensor(out=cr[:], in0=cr[:], in1=csp[1][:], op=ALU.subtract), reads=[cr, csp[1]], writes=[cr])
        P.op("act", lambda e: e.copy(out=csp[2][:], in_=cr[:]), reads=[cr], writes=[csp[2]])
        for h in range(2):
            P.dma("sp", [lambda e, h=h, i=i: e.dma_start(out=QA[h][64 + i:65 + i, :], in_=csp[i][h:h + 1, :]) for i in range(3)],
                  reads=csp, writes=[QAc[h]])
        for i in range(3):
            P.op("act", lambda e, i=i: e.activation(out=cspn[i][:], in_=csp[i][:], func=AF.Copy, scale=-1.0), reads=[csp[i]], writes=[cspn[i]])
            P.dma("sp", [lambda e, h=h, i=i: e.dma_start(out=KA[h][67 + i:68 + i, sl], in_=cspn[i][h:h + 1, :]) for h in range(2)],
                  reads=[cspn[i]], writes=[KAc[0][tt], KAc[1][tt]])
        for c, nm in enumerate(("x0", "x1", "Bm", "Cm")):
            bank, _ = proj_fm(tt, nm)
            u = ub[c]
            P.op("act", lambda e, bank=bank, u=u: e.copy(out=u[:, 3:3 + TT], in_=bank[:]), reads=[bank], writes=[u])
        for i in range(4):
            blk = tt * 4 + i
            bank = PJB[pj_i[0] % len(PJB)]
            pj_i[0] += 1
            h_ = hTt[b_]
            for k in range(8):
                P.op("pe", lambda e, k=k, i=i, bank=bank: e.matmul(bank[:, 0:NTM], lhsT=h_[:, k, i * 128:(i + 1) * 128], rhs=Wt[:, k, :], start=(k == 0), stop=(k == 7)),
                     reads=[Wt, h_], writes=[bank])
            P.op("act", lambda e, blk=blk, bank=bank: e.copy(out=VF[:, blk, :, 0:64], in_=bank[:, 0:128].rearrange("p (h d) -> p h d", h=2)), reads=[bank], writes=[VFr[tt]])
            if os.environ.get("MK_DBG") == "1" and i == 2:
                P.op("dve", lambda e, blk=blk, bank=bank: e.tensor_copy(out=t1[:, 0:64], in_=bank[:, 128:192]), reads=[bank], writes=[VSr[tt]])
            elif os.environ.get("MK_DBG") == "2" and i == 2:
                P.op("dve", lambda e, blk=blk, bank=bank: e.tensor_copy(out=VS[:, blk, 0:64], in_=t2[:, 128:192]), reads=[bank], writes=[VSr[tt]])
            else:
                P.op("dve", lambda e, blk=blk, bank=bank: e.tensor_copy(out=VS[:, blk, 0:64], in_=bank[:, 128:192]), reads=[bank], writes=[VSr[tt]])
            P.op("dve", lambda e, i=i, bank=bank: e.tensor_copy(out=dtraw[:, 4 * i:4 * i + 4], in_=bank[:, 192:196]), reads=[bank], writes=[dtraw])

        accs = [acc, yf]
        for c in range(4):
            u = ub[c]
            ac = accs[c % 2]
            P.op("act", lambda e, u=u, c=c, ac=ac: e.activation(out=ac[:], in_=u[:, 3:3 + TT], func=AF.Identity, scale=pv[:, 8 + 4 * c + 3:8 + 4 * c + 4], bias=pv[:, 24 + c:25 + c]),
                 reads=[u, pv], writes=[ac])
            for k in range(3):
                P.op("dve", lambda e, u=u, c=c, k=k, ac=ac: e.scalar_tensor_tensor(out=ac[:], in0=u[:, k:k + TT], scalar=pv[:, 8 + 4 * c + k:8 + 4 * c + k + 1], in1=ac[:], op0=ALU.mult, op1=ALU.add),
                     reads=[u, pv, ac], writes=[ac])
            P.op("pool", lambda e, u=u: e.tensor_copy(out=u[:, 0:3], in_=u[:, TT:TT + 3]), reads=[u], writes=[u])
            dst = xc[c] if c < 2 else (BcT if c == 2 else CcT)
            P.op("act", lambda e, dst=dst, ac=ac: e.activation(out=dst[:], in_=ac[:], func=AF.Silu), reads=[ac], writes=[dst])

        if "swa" in PARTS:
            sitems = [(hh, i) for i in range(4) for hh in range(2)]
            sbanks = [SC[0], SC[1], PJ[0], PJ[1]]
            spts = {}

            def swa_s1(idx):
                hh, i = sitems[idx]
                pb = 64 * hh
                qb = tt * 4 + i
                scb = sbanks[idx % 4]
                pt = PT[pt_i[0] % len(PT)]
                pt_i[0] += 1
                spts[idx] = pt
                kres = [KRr[tt]] if i > 0 else ([KRr[tt], KRr[tt - 1]] if tt > 0 else [KRr[tt]])
                qcols = slice(i * 128, (i + 1) * 128)
                if qb > 0:
                    P.op("pe", lambda e: e.matmul(scb[:, 0:128], lhsT=KR[pb:pb + 64, (qb - 1) * 128:qb * 128], rhs=QR[pb:pb + 64, qcols], start=True, stop=False),
                         reads=kres + [QR], writes=[scb])
                    P.op("pe", lambda e: e.matmul(scb[:, 0:128], lhsT=C.identb[:], rhs=maskS[:, 0:128], start=False, stop=True),
                         reads=[C.identb, maskS], writes=[scb])
                P.op("pe", lambda e: e.matmul(scb[:, 128:256], lhsT=KR[pb:pb + 64, qb * 128:(qb + 1) * 128], rhs=QR[pb:pb + 64, qcols], start=True, stop=False),
                     reads=kres + [QR], writes=[scb])
                P.op("pe", lambda e: e.matmul(scb[:, 128:256], lhsT=C.identb[:], rhs=maskS[:, 128:256], start=False, stop=True),
                     reads=[C.identb, maskS], writes=[scb])
                lo = 0 if qb > 0 else 128
                P.op("act", lambda e: e.activation(out=pt[:, lo:256], in_=scb[:, lo:256], func=AF.Exp, scale=0.125), reads=[scb], writes=[pt])

            def swa_s2(idx):
                hh, i = sitems[idx]
                qb = tt * 4 + i
                pt = spts[idx]
                otb = OT[hh]
                vres = [VSr[tt]] if i > 0 else ([VSr[tt], VSr[tt - 1]] if tt > 0 else [VSr[tt]])
                qcols = slice(i * 128, (i + 1) * 128)
                if qb > 0:
                    P.op("pe", lambda e: e.matmul(otb[0:65, qcols], lhsT=VS[:, qb - 1, 0:65], rhs=pt[:, 0:128], start=True, stop=False),
                         reads=vres + [pt], writes=[otb])
                P.op("pe", lambda e: e.matmul(otb[0:65, qcols], lhsT=VS[:, qb, 0:65], rhs=pt[:, 128:256], start=(qb == 0), stop=True),
                     reads=vres + [pt], writes=[otb])

            SD = 3
            for idx in range(len(sitems) + SD):
                if idx < len(sitems):
                    swa_s1(idx)
                if idx >= SD:
                    swa_s2(idx - SD)
            for hh in range(2):
                normalize(OT[hh], sg[f"ga{hh}"], hh, yo[hh], slice(None))

        if "fox" in PARTS:
            nkb = 4 * tt + 4
            items = [(h, j) for j in range(nkb) for h in range(2)]
            fbanks = [SC[0], SC[1], PJ[0], PJ[1]]
            NPT = len(PT)

            def fox_s1(idx):
                h, j = items[idx]
                i = j - 4 * tt
                c0 = 0 if i <= 0 else i * 128
                cols = slice(c0, TT)
                scb = fbanks[idx % 4]
                pt = PT[idx % NPT]
                jt = j // 4
                P.op("pe", lambda e: e.matmul(scb[:, cols], lhsT=KA[h][0:70, j * 128:(j + 1) * 128], rhs=QA[h][0:70, cols], start=True, stop=(i < 0)),
                     reads=[KAq[h][jt], KAc[h][jt], QA[h], QAc[h]], writes=[scb])
                if i >= 0:
                    dcol = slice(i * 128, (i + 1) * 128)
                    P.op("pe", lambda e: e.matmul(scb[:, dcol], lhsT=C.identb[:], rhs=maskS[:, 128:256], start=False, stop=True),
                         reads=[C.identb, maskS], writes=[scb])
                P.op("act", lambda e: e.activation(out=pt[:, cols], in_=scb[:, cols], func=AF.Exp), reads=[scb], writes=[pt])

            def fox_s2(idx):
                h, j = items[idx]
                i = j - 4 * tt
                c0 = 0 if i <= 0 else i * 128
                cols = slice(c0, TT)
                pt = PT[idx % NPT]
                jt = j // 4
                otb = OT[h]
                P.op("pe", lambda e: e.matmul(otb[0:65, cols], lhsT=VF[:, j, h, 0:65], rhs=pt[:, cols], start=(j == 0), stop=(j == nkb - 1)),
                     reads=[VFr[jt], pt], writes=[otb])

            for idx in range(len(items) + FOX_D):
                if idx < len(items):
                    fox_s1(idx)
                if idx >= FOX_D:
                    fox_s2(idx - FOX_D)
            for h in range(2):
                normalize(OT[h], sg[f"gb{h}"], None, yo[2 + h], slice(None))

        P.op("dve", lambda e: e.tensor_tensor(out=dtt[:], in0=dtraw[:], in1=pv[:, 30:46], op=ALU.add), reads=[dtraw, pv], writes=[dtt])
        P.op("act", lambda e: e.activation(out=dtt[:], in_=dtt[:], func=AF.Exp), reads=[dtt], writes=[dtt])
        P.op("act", lambda e: e.activation(out=dtt[:], in_=dtt[:], func=AF.Ln, bias=1.0), reads=[dtt], writes=[dtt])
        P.op("dve", lambda e: e.tensor_tensor(out=dA[:], in0=dtt[:], in1=aneg[:], op=ALU.mult), reads=[dtt, aneg], writes=[dA])
        P.op("pe", lambda e: e.matmul(ACp[:], lhsT=C.triuf[:], rhs=dA[:], start=True, stop=True), reads=[C.triuf, dA], writes=[ACp])
        P.op("dve", lambda e: e.tensor_copy(out=acsc[:], in_=ACp[:]), reads=[ACp], writes=[acsc])
        def ssd_A(c):
            ccols = slice(c * 128, (c + 1) * 128)
            par = c % 2
            AB = SC[par]
            MT_, Cs_, xs_, xsd_, Bc_, cd_ = MT2[par], Cs2[par], xs2[par], xsd2[par], Bc2[par], cd2[par]
            for h in range(4):
                ch = 4 * c + h
                P.op("pe", lambda e, h=h, ch=ch: e.matmul(AB[:, h * 128:(h + 1) * 128], lhsT=dA[:, ch:ch + 1].to_broadcast([128, 128]), rhs=C.triuf[:], start=True, stop=True),
                     reads=[dA, C.triuf], writes=[AB])
            for h in range(4):
                ch = 4 * c + h
                P.op("dve", lambda e, h=h, ch=ch: e.tensor_scalar(out=Dabs[:, h, :], in0=AB[:, h * 128:(h + 1) * 128], scalar1=acsc[:, ch:ch + 1], scalar2=0.0, op0=ALU.subtract, op1=ALU.min),
                     reads=[AB, acsc], writes=[Dabs])
            P.op("act", lambda e: e.activation(out=Ee[:].rearrange("p h l -> p (h l)"), in_=Dabs[:].rearrange("p h l -> p (h l)"), func=AF.Exp), reads=[Dabs], writes=[Ee])
            P.op("act", lambda e: e.activation(out=eA[:].rearrange("p h l -> p (h l)"), in_=AB[:], func=AF.Exp), reads=[AB], writes=[eA])
            P.op("dve", lambda e: e.tensor_tensor(out=d4[:], in0=AB[:].rearrange("p (h l) -> p h l", h=4)[:, :, 127], in1=acsc[:, 4 * c:4 * c + 4], op=ALU.subtract),
                 reads=[AB, acsc], writes=[d4])
            P.op("act", lambda e: e.activation(out=d4[:], in_=d4[:], func=AF.Exp), reads=[d4], writes=[d4])
            P.op("dve", lambda e: e.tensor_tensor(out=dtd[:], in0=d4[:], in1=dtt[:, 4 * c:4 * c + 4], op=ALU.mult), reads=[d4, dtt], writes=[dtd])
            P.op("dve", lambda e: e.tensor_copy(out=cd_[:], in_=eA[:, :, 127]), reads=[eA], writes=[cd_])
            P.op("pe", lambda e: e.matmul(CBp[:], lhsT=BcT[:, ccols], rhs=CcT[:, ccols], start=True, stop=True), reads=[BcT, CcT], writes=[CBp])
            P.op("dve", lambda e: e.tensor_tensor(out=CBm[:], in0=CBp[:], in1=C.triuf[:], op=ALU.mult), reads=[CBp, C.triuf], writes=[CBm])
            P.op("dve", lambda e: e.tensor_tensor(out=MT_[:], in0=Ee[:], in1=CBm[:].unsqueeze(1).to_broadcast([128, 4, 128]), op=ALU.mult), reads=[Ee, CBm], writes=[MT_])
            P.op("pool", lambda e: e.tensor_tensor(out=Cs_[:], in0=eA[:], in1=CcT[:, ccols].unsqueeze(1).to_broadcast([128, 4, 128]), op=ALU.mult), reads=[eA, CcT], writes=[Cs_])
            XTp = XTs[par]
            for j in range(2):
                P.op("pe", lambda e, j=j: e.transpose(out=XTp[:, j * 128:(j + 1) * 128], in_=xc[j][:, ccols], identity=C.identf[:]), reads=[xc[j], C.identf], writes=[XTp])
            P.op("pe", lambda e: e.transpose(out=BTp[:], in_=BcT[:, ccols], identity=C.identb[:]), reads=[BcT, C.identb], writes=[BTp])
            P.op("act", lambda e: e.copy(out=Bc_[:], in_=BTp[:]), reads=[BTp], writes=[Bc_])
            P.op("dve", lambda e: e.tensor_tensor(out=xs_[:], in0=XTp[:].rearrange("p (h d) -> p h d", h=4), in1=dtt[:, 4 * c:4 * c + 4].unsqueeze(2).to_broadcast([128, 4, 64]), op=ALU.mult),
                 reads=[XTp, dtt], writes=[xs_])
            P.op("dve", lambda e: e.tensor_tensor(out=xsd_[:], in0=XTp[:].rearrange("p (h d) -> p h d", h=4), in1=dtd[:].unsqueeze(2).to_broadcast([128, 4, 64]), op=ALU.mult),
                 reads=[XTp, dtd], writes=[xsd_])

        def ssd_B(c):
            ccols = slice(c * 128, (c + 1) * 128)
            par = c % 2
            MT_, Cs_, xs_, xsd_, Bc_, cd_ = MT2[par], Cs2[par], xs2[par], xsd2[par], Bc2[par], cd2[par]
            for h in range(4):
                yb_ = OT[h // 2]
                po = (h % 2) * 64
                P.op("pe", lambda e, h=h: e.matmul(yb_[po:po + 64, ccols], lhsT=xs_[:, h, :], rhs=MT_[:, h, :], start=True, stop=False),
                     reads=[xs_, MT_], writes=[yb_])
                P.op("pe", lambda e, h=h: e.matmul(yb_[po:po + 64, ccols], lhsT=STb[:, h, :], rhs=Cs_[:, h, :], start=False, stop=True),
                     reads=[STb, Cs_], writes=[yb_])
            P.op("pe", lambda e: e.matmul(STp[:], lhsT=Bc_[:], rhs=xsd_[:].rearrange("p h d -> p (h d)"), start=True, stop=True), reads=[Bc_, xsd_], writes=[STp])
            P.op("dve", lambda e: e.tensor_tensor(out=ST[:], in0=ST[:], in1=cd_[:].unsqueeze(2).to_broadcast([128, 4, 64]), op=ALU.mult), reads=[ST, cd_], writes=[ST])
            P.op("dve", lambda e: e.tensor_tensor(out=ST[:], in0=ST[:], in1=STp[:].rearrange("p (h d) -> p h d", h=4), op=ALU.add), reads=[ST, STp], writes=[ST])
            P.op("act", lambda e: e.copy(out=STb[:], in_=ST[:]), reads=[ST], writes=[STb])

        if "ssd" in PARTS:
            ssd_A(0)
            for c in range(4):
                if c + 1 < 4:
                    ssd_A(c + 1)
                ssd_B(c)
        for j in range(2):
            P.op("dve", lambda e, j=j: e.scalar_tensor_tensor(out=yf[:], in0=xc[j][:], scalar=pv[:, 28 + j:29 + j], in1=OT[j][:], op0=ALU.mult, op1=ALU.add),
                 reads=[xc[j], pv, OT[j]], writes=[yf])
            P.op("pool", lambda e, j=j: e.tensor_tensor(out=vo[j][:], in0=yf[:], in1=sg[f"z{j}"][:], op=ALU.mult), reads=[yf, sg[f"z{j}"]], writes=[vo[j]])

        for i in range(4):
            P.dma("pool", lambda e, i=i: e.dma_start(out=yT_d[i * 64:(i + 1) * 64, sl], in_=yo[i][:]), reads=[yo[i]])
        for j in range(2):
            P.dma("pool", lambda e, j=j: e.dma_start(out=yT_d[256 + j * 128:256 + (j + 1) * 128, sl], in_=vo[j][:]), reads=[vo[j]])


def build_A(T):
    nc = bass.Bass("TRN2", target_bir_lowering=False)
    with ExitStack() as st:
        P = Prog(nc, st)
        C = alloc_common(P, T)
        hT_d = P.dram("hT", [D, T], BF16, "ExternalInput")
        wfm_d = P.dram("wfm", [D, NFM], F32, "ExternalInput")
        wtm_d = P.dram("wtm", [D, NTM], F32, "ExternalInput")
        pvec_d = P.dram("pvec", [128, NPV], F32, "ExternalInput")
        cos_d = P.dram("cos2", [128, T], F32, "ExternalInput")
        sin_d = P.dram("sin2", [128, T], F32, "ExternalInput")
        maskS_d = P.dram("maskS", [128, 256], F32, "ExternalInput")
        yT_d = P.dram("yT", [512, T], BF16, "ExternalOutput")
        emit_phaseA(P, C, hT_d, wfm_d, wtm_d, pvec_d, cos_d, sin_d, maskS_d, yT_d)
        stats = P.emit()
        print("phaseA ops", len(P.ops), stats, "sems", P.n_sems)
    return nc


NPVB = 32


def host_pvecB(inp, l):
    pv = np.zeros((128, NPVB), np.float32)
    pv[:, 0:8] = inp["norm_xq_w"][l].reshape(8, 128).T
    pv[:, 8:16] = inp["norm_mem_w"][l].reshape(8, 128).T
    nw = inp["ssm_norm_w"][l]
    for hg in range(4):
        pv[:, 16 + 4 * hg + 0] = 1.0
        pv[:, 16 + 4 * hg + 1] = 1.0
        pv[:, 16 + 4 * hg + 2] = nw[256 * hg:256 * hg + 128]
        pv[:, 16 + 4 * hg + 3] = nw[256 * hg + 128:256 * hg + 256]
    return pv


def wout_row_perm():
    rows = []
    for hg in range(4):
        rows.append(np.arange(2 * hg * 64, (2 * hg + 2) * 64))
        rows.append(512 + np.arange(2 * hg * 64, (2 * hg + 2) * 64))
        rows.append(1024 + np.arange(256 * hg, 256 * (hg + 1)))
    return np.concatenate(rows)


def load_weight(P, stg, dst, src_d, KC, N, scale_fn, eng_cycle=("dve", "act")):
    src_v = src_d.t.rearrange("(k p) n -> p k n", p=128)
    for k in range(KC):
        s_ = stg[k % 2]
        P.dma("sp", lambda e: e.dma_start(out=s_[:, 0:N], in_=src_v[:, k, :]), writes=[s_])
        sc = scale_fn(k) if scale_fn is not None else None
        eng = eng_cycle[k % len(eng_cycle)]
        if sc is None:
            if eng == "act":
                P.op(eng, lambda e: e.copy(out=dst[:, k, :], in_=s_[:, 0:N]), reads=[s_], writes=[dst])
            else:
                P.op(eng, lambda e: e.tensor_copy(out=dst[:, k, :], in_=s_[:, 0:N]), reads=[s_], writes=[dst])
        else:
            tl, ap = sc
            if eng == "act":
                P.op(eng, lambda e: e.activation(out=dst[:, k, :], in_=s_[:, 0:N], func=AF.Copy, scale=ap), reads=[s_, tl], writes=[dst])
            else:
                P.op(eng, lambda e: e.tensor_scalar(out=dst[:, k, :], in0=s_[:, 0:N], scalar1=ap, scalar2=None, op0=ALU.mult), reads=[s_, tl], writes=[dst])


def emit_norm_T(P, C, xin, xin_res, hT_out, col0, sq, ss, hb, want_T=True):
    P.op("act", lambda e: e.activation(out=sq[:], in_=xin, func=AF.Square, accum_out=ss[:]), reads=[xin_res], writes=[sq, ss])
    P.op("act", lambda e: e.activation(out=ss[:], in_=ss[:], func=AF.Ln, scale=1.0 / D, bias=EPS), reads=[ss], writes=[ss])
    P.op("act", lambda e: e.activation(out=ss[:], in_=ss[:], func=AF.Exp, scale=-0.5), reads=[ss], writes=[ss])
    if not want_T:
        return
    P.op("dve", lambda e: e.tensor_scalar(out=hb[:], in0=xin, scalar1=ss[:], scalar2=None, op0=ALU.mult), reads=[xin_res, ss], writes=[hb])
    tp = C.TPb
    for k in range(8):
        P.op("pe", lambda e, k=k: e.transpose(out=tp[:, k * 128:(k + 1) * 128], in_=hb[:, k * 128:(k + 1) * 128], identity=C.identb[:]), reads=[hb, C.identb], writes=[tp])
    P.op("act", lambda e: e.copy(out=hT_out[:, :, col0:col0 + 128], in_=tp[:].rearrange("p (k t) -> p k t", k=8)), reads=[tp], writes=[hT_out])


def emit_phaseB(P, C, TQ, x_d, yT_d, wout_d, pvB_d, wmq_d, wmk_d, wmv_d, wmo_d, mem_d, xout_d, hTn_d, out_d, fnw_d, final, tag="B"):
    NTB = TQ // TT
    PJ, SC, OT, MS = C.PJ, C.SC, C.OT, C.MS
    C.TPb = Tile(MS[1][:].bitcast(BF16), "TPb")
    C.TPb.r = MS[1].r
    pvB = P.sbuf(tag + "pvB", [128, NPVB], F32)
    P.dma("sp", lambda e: e.dma_start(out=pvB[:], in_=pvB_d[:]), writes=[pvB])
    stg = [P.sbuf(tag + f"stg{i}", [128, 1024], F32) for i in range(2)]
    WO = P.sbuf(tag + "WO", [128, 16, 1024], BF16)
    WQ = P.sbuf(tag + "WQ", [128, 8, 1024], BF16)
    WMO = P.sbuf(tag + "WMO", [128, 8, 1024], BF16)
    WT = P.sbuf(tag + "WT", [128, 8, 1024], BF16)
    KmT = P.sbuf(tag + "KmT", [128, 8, 256], BF16)
    Vm = P.sbuf(tag + "Vm", [128, 2, 1024], BF16)
    memnT = P.sbuf(tag + "memnT", [128, 8, 256], BF16)
    xt = P.sbuf(tag + "xt", [128, 4, 1024], F32)
    yt = P.sbuf(tag + "yt", [128, 16, TT], BF16)
    sq8 = P.sbuf(tag + "sq8", [128, 8, TT], BF16)
    sq = P.sbuf(tag + "sq", [128, 1024], F32)
    ss = P.sbuf(tag + "ss", [128, 1], F32)
    rs = P.sbuf(tag + "rs", [128, 4], F32)
    hb = P.sbuf(tag + "hb", [128, 1024], BF16)
    hqT = P.sbuf(tag + "hqT", [128, 8, TT], BF16)
    qT = P.sbuf(tag + "qT", [128, 8, TT], BF16)
    oT = P.sbuf(tag + "oT", [128, 8, TT], BF16)
    hTn = P.sbuf(tag + "hTn", [128, 8, TT], BF16)
    ptm = [P.sbuf(tag + f"ptm{i}", [128, TT], BF16) for i in range(2)]
    rden = P.sbuf(tag + "rden", [128, TT], F32)
    if final:
        fnw = P.sbuf(tag + "fnw", [128, 1024], F32)
        P.dma("sp", lambda e: e.dma_start(out=fnw[:], in_=fnw_d[:]), writes=[fnw])
        ob = P.sbuf(tag + "ob", [128, 1024], F32)

    for mb in range(2):
        P.dma("sp", lambda e: e.dma_start(out=xt[:, mb, :], in_=mem_d[mb * 128:(mb + 1) * 128, :]), writes=[xt])
        emit_norm_T(P, C, xt[:, mb, :], xt, memnT, mb * 128, sq, ss, hb)
    load_weight(P, stg, WT, wmk_d, 8, 1024, lambda k: (pvB, pvB[:, 8 + k:9 + k]))
    for c in range(8):
        bank = PJ[c % 2]
        for k in range(8):
            P.op("pe", lambda e, k=k: e.matmul(bank[:, 0:256], lhsT=WT[:, k, c * 128:(c + 1) * 128], rhs=memnT[:, k, :], start=(k == 0), stop=(k == 7)), reads=[WT, memnT], writes=[bank])
        P.op("act", lambda e: e.copy(out=KmT[:, c, :], in_=bank[:, 0:256]), reads=[bank], writes=[KmT])
    load_weight(P, stg, WT, wmv_d, 8, 1024, lambda k: (pvB, pvB[:, 8 + k:9 + k]))
    for mc in range(2):
        for half in range(2):
            bank = PJ[(mc * 2 + half) % 2]
            for k in range(8):
                P.op("pe", lambda e, k=k: e.matmul(bank[:], lhsT=memnT[:, k, mc * 128:(mc + 1) * 128], rhs=WT[:, k, half * 512:(half + 1) * 512], start=(k == 0), stop=(k == 7)), reads=[WT, memnT], writes=[bank])
            P.op("act", lambda e: e.copy(out=Vm[:, mc, half * 512:(half + 1) * 512], in_=bank[:]), reads=[bank], writes=[Vm])
    load_weight(P, stg, WO, wout_d, 16, 1024, lambda k: (pvB, pvB[:, 16 + k:17 + k]))
    load_weight(P, stg, WQ, wmq_d, 8, 1024, lambda k: (pvB, pvB[:, k:k + 1]))
    load_weight(P, stg, WMO, wmo_d, 8, 1024, None)

    yT_v = yT_d.t.rearrange("(c p) t -> p c t", p=128)
    x_v = x_d.t.rearrange("(n p) d -> p n d", p=128)
    xo_v = xout_d.t.rearrange("(n p) d -> p n d", p=128) if xout_d is not None else None
    out_v = out_d.t.rearrange("(n p) d -> p n d", p=128) if out_d is not None else None
    attn_ch = [c for c in range(16) if c % 4 < 2]
    ssm_ch = [c for c in range(16) if c % 4 >= 2]
    SSp = Tile(MS[0][:, 0:4], "SSp")
    SSp.r = MS[0].r
    for tb in range(NTB):
        sl = slice(tb * TT, (tb + 1) * TT)
        P.dma("sp", lambda e: e.dma_start(out=yt[:], in_=yT_v[:, :, sl]), writes=[yt])
        P.dma("sp", lambda e: e.dma_start(out=xt[:], in_=x_v[:, tb * 4:(tb + 1) * 4, :]), writes=[xt])
        for j, c in enumerate(ssm_ch):
            P.op("act", lambda e: e.activation(out=sq8[:, j, :], in_=yt[:, c, :], func=AF.Square), reads=[yt], writes=[sq8])
        for i in range(4):
            for j in range(8):
                P.op("pe", lambda e: e.matmul(SSp[:, i:i + 1], lhsT=sq8[:, j, i * 128:(i + 1) * 128], rhs=C.onesb[:, 0:1], start=(j == 0), stop=(j == 7)), reads=[sq8, C.onesb], writes=[SSp])
        P.op("act", lambda e: e.activation(out=rs[:], in_=SSp[:], func=AF.Ln, scale=1.0 / 1024, bias=EPS), reads=[SSp], writes=[rs])
        P.op("act", lambda e: e.activation(out=rs[:], in_=rs[:], func=AF.Exp, scale=-0.5), reads=[rs], writes=[rs])
        for i in range(4):
            for half in range(2):
                hs = slice(half * 512, (half + 1) * 512)
                A, S = PJ[0], PJ[1]
                for n_, c in enumerate(attn_ch):
                    P.op("pe", lambda e: e.matmul(A[:], lhsT=yt[:, c, i * 128:(i + 1) * 128], rhs=WO[:, c, hs], start=(n_ == 0), stop=(n_ == 7)), reads=[yt, WO], writes=[A])
                for n_, c in enumerate(ssm_ch):
                    P.op("pe", lambda e: e.matmul(S[:], lhsT=yt[:, c, i * 128:(i + 1) * 128], rhs=WO[:, c, hs], start=(n_ == 0), stop=(n_ == 7)), reads=[yt, WO], writes=[S])
                P.op("dve", lambda e: e.tensor_tensor(out=xt[:, i, hs], in0=A[:], in1=xt[:, i, hs], op=ALU.add), reads=[A, xt], writes=[xt])
                P.op("dve", lambda e: e.scalar_tensor_tensor(out=xt[:, i, hs], in0=S[:], scalar=rs[:, i:i + 1], in1=xt[:, i, hs], op0=ALU.mult, op1=ALU.add), reads=[S, rs, xt], writes=[xt])
            emit_norm_T(P, C, xt[:, i, :], xt, hqT, i * 128, sq, ss, hb)
        for c in range(8):
            bank = PJ[c % 2]
            for k in range(8):
                P.op("pe", lambda e, k=k: e.matmul(bank[:], lhsT=WQ[:, k, c * 128:(c + 1) * 128], rhs=hqT[:, k, :], start=(k == 0), stop=(k == 7)), reads=[WQ, hqT], writes=[bank])
            P.op("act", lambda e: e.activation(out=qT[:, c, :], in_=bank[:], func=AF.Copy, scale=1.0 / 16), reads=[bank], writes=[qT])
        for h in range(4):
            for mc in range(2):
                scb = SC[mc]
                for dc in range(2):
                    P.op("pe", lambda e: e.matmul(scb[:], lhsT=KmT[:, 2 * h + dc, mc * 128:(mc + 1) * 128], rhs=qT[:, 2 * h + dc, :], start=(dc == 0), stop=(dc == 1)), reads=[KmT, qT], writes=[scb])
                P.op("act", lambda e: e.activation(out=ptm[mc][:], in_=scb[:], func=AF.Exp), reads=[scb], writes=[ptm[mc]])
            den = MS[0]
            for mc in range(2):
                P.op("pe", lambda e: e.matmul(den[:], lhsT=C.onesb[:], rhs=ptm[mc][:], start=(mc == 0), stop=(mc == 1)), reads=[C.onesb, ptm[mc]], writes=[den])
            P.op("dve", lambda e: e.reciprocal(out=rden[:], in_=den[:]), reads=[den], writes=[rden])
            for dc in range(2):
                ob_ = OT[dc]
                for mc in range(2):
                    P.op("pe", lambda e: e.matmul(ob_[:], lhsT=Vm[:, mc, (2 * h + dc) * 128:(2 * h + dc + 1) * 128], rhs=ptm[mc][:], start=(mc == 0), stop=(mc == 1)), reads=[Vm, ptm[mc]], writes=[ob_])
                P.op("dve", lambda e: e.tensor_tensor(out=oT[:, 2 * h + dc, :], in0=ob_[:], in1=rden[:], op=ALU.mult), reads=[ob_, rden], writes=[oT])
        for i in range(4):
            for half in range(2):
                hs = slice(half * 512, (half + 1) * 512)
                bank = PJ[half]
                for k in range(8):
                    P.op("pe", lambda e, k=k: e.matmul(bank[:], lhsT=oT[:, k, i * 128:(i + 1) * 128], rhs=WMO[:, k, hs], start=(k == 0), stop=(k == 7)), reads=[oT, WMO], writes=[bank])
                P.op("dve", lambda e: e.tensor_tensor(out=xt[:, i, hs], in0=bank[:], in1=xt[:, i, hs], op=ALU.add), reads=[bank, xt], writes=[xt])
            if final:
                emit_norm_T(P, C, xt[:, i, :], xt, None, 0, sq, ss, hb, want_T=False)
                P.op("dve", lambda e: e.scalar_tensor_tensor(out=ob[:], in0=xt[:, i, :], scalar=ss[:], in1=fnw[:], op0=ALU.mult, op1=ALU.mult), reads=[xt, ss, fnw], writes=[ob])
                P.dma("pool", lambda e: e.dma_start(out=out_v[:, tb * 4 + i, :], in_=ob[:]), reads=[ob])
            else:
                emit_norm_T(P, C, xt[:, i, :], xt, hTn, i * 128, sq, ss, hb)
        if not final:
            P.dma("pool", lambda e: e.dma_start(out=xo_v[:, tb * 4:(tb + 1) * 4, :], in_=xt[:]), reads=[xt])
            P.dma("pool", lambda e: e.dma_start(out=hTn_d.t.rearrange("(k p) t -> p k t", p=128)[:, :, sl], in_=hTn[:]), reads=[hTn])


def build_B(TQ, final):
    nc = bass.Bass("TRN2", target_bir_lowering=False)
    with ExitStack() as st:
        P = Prog(nc, st)
        C = alloc_common(P, TQ)
        x_d = P.dram("x", [TQ, D], F32, "ExternalInput")
        yT_d = P.dram("yT", [2048, TQ], BF16, "ExternalInput")
        wout_d = P.dram("wout", [2048, D], F32, "ExternalInput")
        pvB_d = P.dram("pvB", [128, NPVB], F32, "ExternalInput")
        wmq_d = P.dram("wmq", [D, D], F32, "ExternalInput")
        wmk_d = P.dram("wmk", [D, D], F32, "ExternalInput")
        wmv_d = P.dram("wmv", [D, D], F32, "ExternalInput")
        wmo_d = P.dram("wmo", [D, D], F32, "ExternalInput")
        mem_d = P.dram("mem", [MEMT, D], F32, "ExternalInput")
        if final:
            fnw_d = P.dram("fnw", [128, D], F32, "ExternalInput")
            out_d = P.dram("out", [TQ, D], F32, "ExternalOutput")
            xout_d = hTn_d = None
        else:
            fnw_d = out_d = None
            xout_d = P.dram("xout", [TQ, D], F32, "ExternalOutput")
            hTn_d = P.dram("hTn", [D, TQ], BF16, "ExternalOutput")
        emit_phaseB(P, C, TQ, x_d, yT_d, wout_d, pvB_d, wmq_d, wmk_d, wmv_d, wmo_d, mem_d, xout_d, hTn_d, out_d, fnw_d, final)
        stats = P.emit()
        print("phaseB ops", len(P.ops), stats, "sems", P.n_sems)
    return nc


def build_N(TQ):
    nc = bass.Bass("TRN2", target_bir_lowering=False)
    with ExitStack() as st:
        P = Prog(nc, st)
        C = alloc_common(P, TQ)
        C.TPb = Tile(C.MS[1][:].bitcast(BF16), "TPb")
        C.TPb.r = C.MS[1].r
        x_d = P.dram("x", [TQ, D], F32, "ExternalInput")
        hTn_d = P.dram("hTn", [D, TQ], BF16, "ExternalOutput")
        xt = P.sbuf("Nxt", [128, 4, 1024], F32)
        sq = P.sbuf("Nsq", [128, 1024], F32)
        ss = P.sbuf("Nss", [128, 1], F32)
        hb = P.sbuf("Nhb", [128, 1024], BF16)
        hTn = P.sbuf("NhTn", [128, 8, TT], BF16)
        x_v = x_d.t.rearrange("(n p) d -> p n d", p=128)
        for tb in range(TQ // TT):
            sl = slice(tb * TT, (tb + 1) * TT)
            P.dma("sp", lambda e: e.dma_start(out=xt[:], in_=x_v[:, tb * 4:(tb + 1) * 4, :]), writes=[xt])
            for i in range(4):
                emit_norm_T(P, C, xt[:, i, :], xt, hTn, i * 128, sq, ss, hb)
            P.dma("pool", lambda e: e.dma_start(out=hTn_d.t.rearrange("(k p) t -> p k t", p=128)[:, :, sl], in_=hTn[:]), reads=[hTn])
        stats = P.emit()
        print("phaseN ops", len(P.ops), stats, "sems", P.n_sems)
    return nc


def _run(nc, in_maps):
    res = run_bass_kernel_spmd(nc, in_maps, core_ids=list(range(8)))
    return res.results


def kernel_unfused(**inputs):
    inp = {k: np.asarray(v) for k, v in inputs.items()}
    T = SEQ
    TQ = T // 4
    x = inp["x"]
    consts = host_consts(T)
    common = {"ident": consts["ident"], "triu": consts["triu"]}
    cores = [(c // 4, c % 4) for c in range(8)]
    ncN = build_N(TQ)
    r = _run(ncN, [dict(x=np.ascontiguousarray(x[b, q * TQ:(q + 1) * TQ]), **common) for b, q in cores])
    hT_full = [np.concatenate([np.asarray(r[b * 4 + q]["hTn"]) for q in range(4)], axis=1) for b in range(NB_)]
    x_cur = [np.ascontiguousarray(x[b, q * TQ:(q + 1) * TQ]) for b, q in cores]
    ncA = build_A(T)
    perm = wout_row_perm()
    out = None
    for l in range(DEPTH):
        maps = []
        for b, hg in cores:
            fm, tm = fm_cols(hg)
            maps.append(dict(hT=hT_full[b], wfm=np.ascontiguousarray(inp["w_in"][l][:, fm]), wtm=np.ascontiguousarray(inp["w_in"][l][:, tm]),
                             pvec=host_pvec(inp, l, hg), cos2=consts["cos2"], sin2=consts["sin2"], maskS=consts["maskS"], **common))
        rA = _run(ncA, maps)
        final = (l == DEPTH - 1)
        ncB = build_B(TQ, final)
        maps = []
        wout_p = np.ascontiguousarray(inp["w_out"][l][perm])
        pvB = host_pvecB(inp, l)
        for ci, (b, q) in enumerate(cores):
            yT_own = np.concatenate([np.asarray(rA[b * 4 + hg]["yT"])[:, q * TQ:(q + 1) * TQ] for hg in range(4)], axis=0)
            m = dict(x=x_cur[ci], yT=np.ascontiguousarray(yT_own), wout=wout_p, pvB=pvB, wmq=inp["w_mq"][l], wmk=inp["w_mk"][l],
                     wmv=inp["w_mv"][l], wmo=inp["w_mo"][l], mem=np.ascontiguousarray(inp["mem"][b]), **common)
            if final:
                m["fnw"] = np.ascontiguousarray(np.broadcast_to(inp["final_norm_w"][None, :], (128, D)))
            maps.append(m)
        rB = _run(ncB, maps)
        if final:
            out = np.stack([np.concatenate([np.asarray(rB[b * 4 + q]["out"]) for q in range(4)], axis=0) for b in range(NB_)], axis=0)
        else:
            x_cur = [np.asarray(rB[ci]["xout"]) for ci in range(8)]
            hT_full = [np.concatenate([np.asarray(rB[b * 4 + q]["hTn"]) for q in range(4)], axis=1) for b in range(NB_)]
    return out.astype(np.float32)


def _sub(tile_, ap, name):
    t = Tile(ap, name)
    return t


def build_fused(T, depth=DEPTH):
    nc = bass.Bass("TRN2", target_bir_lowering=False)
    with ExitStack() as st:
        P = Prog(nc, st)
        C = alloc_common(P, T)
        C.TPb = Tile(C.MS[1][:].bitcast(BF16), "TPb")
        C.TPb.r = C.MS[1].r
        x_d = P.dram("x", [T, D], F32, "ExternalInput")
        mem_d = P.dram("mem", [MEMT, D], F32, "ExternalInput")
        wfm_d = P.dram("wfm", [depth * 4 * D, NFM], F32, "ExternalInput")
        wtm_d = P.dram("wtm", [depth * 4 * D, NTM], F32, "ExternalInput")
        pvec_d = P.dram("pvec", [depth * 4 * 128, NPV], F32, "ExternalInput")
        wout_d = P.dram("wout", [depth * 2048, D], F32, "ExternalInput")
        pvB_d = P.dram("pvB", [depth * 128, NPVB], F32, "ExternalInput")
        wm_d = {n: P.dram(n, [depth * D, D], F32, "ExternalInput") for n in ("wmq", "wmk", "wmv", "wmo")}
        cos_d = P.dram("cos2", [128, T], F32, "ExternalInput")
        sin_d = P.dram("sin2", [128, T], F32, "ExternalInput")
        maskS_d = P.dram("maskS", [128, 256], F32, "ExternalInput")
        fnw_d = P.dram("fnw", [128, D], F32, "ExternalInput")
        out_d = P.dram("out", [T, D], F32, "ExternalOutput")
        hT_s = P.dram("hT_s", [D, T], BF16, "Internal")
        yT_s = P.dram("yT_s", [2048, T], BF16, "Internal")
        x1_s = P.dram("x1_s", [T, D], F32, "Internal")
        outer = P.stack

        def phase(fn):
            with ExitStack() as ph:
                P.stack = ph
                fn()
                P.barrier()
            P.stack = outer

        P.barrier()

        def phN():
            xt = P.sbuf("Nxt", [128, 4, 1024], F32)
            sq = P.sbuf("Nsq", [128, 1024], F32)
            ss = P.sbuf("Nss", [128, 1], F32)
            hb = P.sbuf("Nhb", [128, 1024], BF16)
            hTn = P.sbuf("NhTn", [128, 8, TT], BF16)
            x_v = x_d.t.rearrange("(n p) d -> p n d", p=128)
            for tb in range(T // TT):
                sl = slice(tb * TT, (tb + 1) * TT)
                P.dma("sp", lambda e: e.dma_start(out=xt[:], in_=x_v[:, tb * 4:(tb + 1) * 4, :]), writes=[xt])
                for i in range(4):
                    emit_norm_T(P, C, xt[:, i, :], xt, hTn, i * 128, sq, ss, hb)
                P.dma("pool", lambda e: e.dma_start(out=hT_s.t.rearrange("(k p) t -> p k t", p=128)[:, :, sl], in_=hTn[:]), reads=[hTn])
        phase(phN)
        for l in range(depth):
            for hg in range(4):
                i_ = l * 4 + hg
                phase(lambda: emit_phaseA(P, C, hT_s, Tile(wfm_d[i_ * D:(i_ + 1) * D, :], "wfm_s"), Tile(wtm_d[i_ * D:(i_ + 1) * D, :], "wtm_s"),
                                          Tile(pvec_d[i_ * 128:(i_ + 1) * 128, :], "pv_s"), cos_d, sin_d, maskS_d,
                                          Tile(yT_s[hg * 512:(hg + 1) * 512, :], "yT_sub"), tag=f"A{l}{hg}"))
            final = (l == depth - 1)
            wsl = lambda n: Tile(wm_d[n][l * D:(l + 1) * D, :], n + "_s")
            phase(lambda: emit_phaseB(P, C, T, x_d if l == 0 else x1_s, yT_s, Tile(wout_d[l * 2048:(l + 1) * 2048, :], "wo_s"),
                                      Tile(pvB_d[l * 128:(l + 1) * 128, :], "pvB_s"), wsl("wmq"), wsl("wmk"), wsl("wmv"), wsl("wmo"), mem_d,
                                      None if final else x1_s, None if final else hT_s, out_d if final else None, fnw_d if final else None, final, tag=f"B{l}"))
        stats = P.emit()
        print("fused ops", len(P.ops), stats, "sems", P.n_sems)
    return nc


def fused_inputs(inp, T, depth=DEPTH):
    consts = host_consts(T)
    perm = wout_row_perm()
    cols = [fm_cols(hg) for hg in range(4)]
    shared = dict(
        wfm=np.concatenate([inp["w_in"][l][:, cols[hg][0]] for l in range(depth) for hg in range(4)], axis=0),
        wtm=np.concatenate([inp["w_in"][l][:, cols[hg][1]] for l in range(depth) for hg in range(4)], axis=0),
        pvec=np.concatenate([host_pvec(inp, l, hg) for l in range(depth) for hg in range(4)], axis=0),
        wout=np.concatenate([inp["w_out"][l][perm] for l in range(depth)], axis=0),
        pvB=np.concatenate([host_pvecB(inp, l) for l in range(depth)], axis=0),
        wmq=np.concatenate([inp["w_mq"][l] for l in range(depth)], axis=0),
        wmk=np.concatenate([inp["w_mk"][l] for l in range(depth)], axis=0),
        wmv=np.concatenate([inp["w_mv"][l] for l in range(depth)], axis=0),
        wmo=np.concatenate([inp["w_mo"][l] for l in range(depth)], axis=0),
        cos2=consts["cos2"], sin2=consts["sin2"], maskS=consts["maskS"], ident=consts["ident"], triu=consts["triu"],
        fnw=np.ascontiguousarray(np.broadcast_to(inp["final_norm_w"][None, :], (128, D))),
    )
    return shared


def kernel_fused(**inputs):
    inp = {k: np.asarray(v) for k, v in inputs.items()}
    T = inp["x"].shape[1]
    shared = fused_inputs(inp, T)
    nc = build_fused(T)
    maps = []
    for c in range(8):
        b = c // 4
        maps.append(dict(x=np.ascontiguousarray(inp["x"][b]), mem=np.ascontiguousarray(inp["mem"][b]), **shared))
    r = _run(nc, maps)
    return np.stack([np.asarray(r[4 * b]["out"]) for b in range(NB_)], axis=0).astype(np.float32)


def kernel(**inputs):
    return kernel_fused(**inputs)
```

```python
import numpy as np
import ml_dtypes
import concourse.bass as bass
import concourse.mybir as mybir
from concourse.bass_utils import run_bass_kernel_spmd
from contextlib import ExitStack
import types
import os

F32 = mybir.dt.float32
BF16 = mybir.dt.bfloat16
AF = mybir.ActivationFunctionType
ALU = mybir.AluOpType
AX = mybir.AxisListType

SEM_EPOCH = 30000
NO_SELF_SYNC = tuple(os.environ.get("MK_NOSELF", "pe").split(","))
SKIP_SAME_ENGINE_WAW = os.environ.get("MK_WAW", "1") == "1"
EMBED_WAIT = os.environ.get("MK_EMBED", "1") == "1"
SAME_ENGINE_SYNC = os.environ.get("MK_SES", "1") == "1"


class Res:
    __slots__ = ("name", "last_w", "readers", "excl")

    def __init__(self, name):
        self.name = name
        self.last_w = None
        self.readers = []
        self.excl = False


class Tile:
    def __init__(self, t, name):
        self.t = t
        self.r = Res(name)

    def __getitem__(self, idx):
        return self.t[idx]


def _res(x):
    return x.r if isinstance(x, Tile) else x


def _freeze(fn):
    if fn.__closure__ is None:
        return fn
    cells = []
    for c in fn.__closure__:
        try:
            cells.append(types.CellType(c.cell_contents))
        except ValueError:
            cells.append(c)
    return types.FunctionType(fn.__code__, fn.__globals__, fn.__name__, fn.__defaults__, tuple(cells))


class Op:
    __slots__ = ("idx", "eng", "fn", "deps", "is_dma", "sig", "dma_sem", "dma_val", "ndma", "needs_sig", "dma_phys")


class Prog:
    def __init__(self, nc, stack):
        self.nc = nc
        self.stack = stack
        self.ops = []
        self.dma_sems = {}
        self.phys = []
        self.free_phys = {}

    def sbuf(self, name, shape, dtype):
        t = self.stack.enter_context(self.nc.sbuf_tensor(name, list(shape), dtype))
        return Tile(t, name)

    def psum(self, name, shape, dtype):
        t = self.stack.enter_context(self.nc.psum_tensor(name, list(shape), dtype))
        tl = Tile(t, name)
        tl.r.excl = True
        return tl

    def dram(self, name, shape, dtype, kind):
        t = self.nc.dram_tensor(name, list(shape), dtype, kind=kind)
        return Tile(t.ap(), name)

    def _record(self, eng, fn, reads, writes, is_dma=False, ndma=1, sem_key=None):
        op = Op()
        op.idx = len(self.ops)
        op.eng = eng
        op.fn = fn
        op.is_dma = is_dma
        op.ndma = ndma
        op.sig = None
        op.needs_sig = False
        op.dma_sem = None
        op.dma_val = None
        deps = set()
        waw = set()
        reads = [_res(r) for r in reads]
        writes = [_res(w) for w in writes]
        for r in reads:
            if r.last_w is not None:
                deps.add(r.last_w)
            if r.excl:
                for rd in r.readers:
                    if self.ops[rd].eng != eng:
                        deps.add(rd)
        for w in writes:
            if w.last_w is not None:
                if SKIP_SAME_ENGINE_WAW and w.last_w not in deps and self.ops[w.last_w].eng == eng and not is_dma and not self.ops[w.last_w].is_dma:
                    waw.add(w.last_w)
                else:
                    deps.add(w.last_w)
            for rd in w.readers:
                deps.add(rd)
        waw -= deps
        deps.discard(op.idx)
        op.deps = sorted(deps)
        for r in reads:
            r.readers.append(op.idx)
        for w in writes:
            w.last_w = op.idx
            w.readers = []
        if is_dma:
            key = sem_key if sem_key is not None else (writes[0] if writes else reads[0])
            key = (_res(key), eng)
            if key not in self.dma_sems:
                fl = self.free_phys.setdefault(eng, [])
                if fl:
                    pi = fl.pop()
                else:
                    pi = len(self.phys)
                    self.phys.append([None, 0, eng])
                self.dma_sems[key] = [pi, self.phys[pi][1]]
            ent = self.dma_sems[key]
            ent[1] += 16 * ndma
            self.phys[ent[0]][1] = ent[1]
            op.dma_sem = key
            op.dma_val = ent[1]
            op.dma_phys = ent[0]
        self.ops.append(op)
        return op

    def op(self, eng, fn, reads=(), writes=()):
        return self._record(eng, _freeze(fn), reads, writes)

    def barrier(self):
        last = {}
        for o in self.ops:
            if o.is_dma:
                last[("dma", o.dma_sem)] = o.idx
            else:
                last[o.eng] = o.idx
        extra = sorted(set(last.values()))
        for eng in ("sp", "pe", "act", "dve", "pool"):
            o = self._record(eng, lambda e: e.nop(), [], [])
            o.deps = sorted(set(o.deps) | set(extra))
        self.retired = getattr(self, "retired", {})
        for key, ent in self.dma_sems.items():
            self.retired[key] = ent
            self.free_phys.setdefault(self.phys[ent[0]][2], []).append(ent[0])
        self.dma_sems = {}

    def dma(self, q, fns, reads=(), writes=(), sem_key=None):
        if callable(fns):
            fns = [fns]
        fns = [_freeze(f) for f in fns]
        return self._record(q, fns, reads, writes, is_dma=True, ndma=len(fns), sem_key=sem_key)

    def emit(self):
        nc = self.nc
        ops = self.ops
        for op in ops:
            for d in op.deps:
                dop = ops[d]
                if dop.is_dma:
                    continue
                if dop.eng == op.eng and (dop.eng in NO_SELF_SYNC or not SAME_ENGINE_SYNC) and not op.is_dma:
                    continue
                dop.needs_sig = True
        counters = {}
        for op in ops:
            if op.is_dma or not op.needs_sig:
                continue
            c = counters.get(op.eng, 0) + 1
            counters[op.eng] = c
            op.sig = c
        eng_sems = {}
        for eng, c in counters.items():
            n_ep = (c + SEM_EPOCH - 1) // SEM_EPOCH
            eng_sems[eng] = [self.stack.enter_context(nc.semaphore(f"s_{eng}_{i}")) for i in range(n_ep)]
        for i, ph in enumerate(self.phys):
            ph[0] = self.stack.enter_context(nc.semaphore(f"d{i}"))
        self.n_sems = sum(len(v) for v in eng_sems.values()) + len(self.phys)
        self.max_dma_total = max([ph[1] for ph in self.phys] + [0])
        allk = dict(getattr(self, "retired", {}))
        allk.update(self.dma_sems)

        def dsem(key):
            return self.phys[allk[key][0]][0]

        def sem_of(eng, sig):
            ep = (sig - 1) // SEM_EPOCH
            return eng_sems[eng][ep], sig - ep * SEM_EPOCH

        per_eng = {}
        for op in ops:
            per_eng.setdefault(op.eng, []).append(op)
        handles = {"pe": "tensor", "act": "scalar", "dve": "vector", "pool": "gpsimd", "sp": "sync"}
        final_dma = [(ph[0], ph[1]) for ph in self.phys]
        stats = {"waits": 0, "instr": 0}

        def run_engine(engname, e):
            seen_eng = {}
            seen_dma = {}
            for op in per_eng.get(engname, []):
                need_eng = {}
                need_dma = {}
                for d in op.deps:
                    dop = ops[d]
                    if dop.is_dma:
                        k = dop.dma_phys
                        if dop.dma_val > need_dma.get(k, 0):
                            need_dma[k] = dop.dma_val
                    else:
                        if dop.sig is None:
                            continue
                        if dop.eng == engname and not op.is_dma and (engname in NO_SELF_SYNC or not SAME_ENGINE_SYNC):
                            continue
                        if dop.sig > need_eng.get(dop.eng, 0):
                            need_eng[dop.eng] = dop.sig
                wl = []
                for se, sig in need_eng.items():
                    if seen_eng.get(se, 0) >= sig:
                        continue
                    seen_eng[se] = sig
                    wl.append(sem_of(se, sig))
                for k, val in need_dma.items():
                    if seen_dma.get(k, 0) >= val:
                        continue
                    seen_dma[k] = val
                    wl.append((self.phys[k][0], val))
                emb = None
                if wl and EMBED_WAIT and not op.is_dma:
                    emb = wl.pop()
                for s_, v_ in wl:
                    e.wait_ge(s_, v_)
                    stats["waits"] += 1
                if op.is_dma:
                    s = self.phys[op.dma_phys][0]
                    for f in op.fn:
                        f(e).then_inc(s, 16)
                        stats["instr"] += 1
                else:
                    ins = op.fn(e)
                    stats["instr"] += 1
                    if emb is not None:
                        ins._wait_ge(emb[0], emb[1])
                    if op.sig is not None:
                        s, _ = sem_of(op.eng, op.sig)
                        ins.then_inc(s, 1)
            if engname == "sp":
                for s, v in final_dma:
                    e.wait_ge(s, v)

        with nc.Block() as block:
            for engname in ("sp", "pe", "act", "dve", "pool"):
                if engname not in per_eng and engname != "sp":
                    continue
                deco = getattr(block, handles[engname])

                def body(e, engname=engname):
                    run_engine(engname, e)

                deco(body)
        self.stats = stats
        return stats


D = 1024
SEQ = 8192
NB_ = 2
DEPTH = 2
HD = 64
MEMT = 256
EPS = 1e-6
O_QA, O_KA, O_VA, O_GA = 0, 512, 640, 768
O_QB, O_KB, O_VB, O_FB, O_GB = 1280, 1792, 2304, 2816, 2824
O_Z, O_XBC, O_DT = 3336, 4360, 5896
FM_GROUPS = [("qa", 128), ("qas", 128), ("ka", 128), ("kas", 128), ("ga0", 64), ("ga1", 64),
             ("qb0", 64), ("qb1", 64), ("kb0", 64), ("kb1", 64), ("gb0", 64), ("gb1", 64), ("fb", 2),
             ("z0", 128), ("z1", 128), ("x0", 128), ("x1", 128), ("Bm", 128), ("Cm", 128)]
FM_OFF = {}
_o = 0
for _n, _m in FM_GROUPS:
    FM_OFF[_n] = (_o, _m)
    _o += _m
NFM = _o
NTM = 196
NPV = 72
TT = 512
FOX_D = int(os.environ.get("MK_FOXD", "3"))


def fm_cols(hg):
    h0, h1 = 2 * hg, 2 * hg + 1
    kv = hg // 2
    g = hg // 2
    r = np.arange(64)
    sw = np.concatenate([np.arange(32, 64), np.arange(0, 32)])
    a128 = np.arange(128)
    cols = {
        "qa": np.concatenate([O_QA + h0 * 64 + r, O_QA + h1 * 64 + r]),
        "qas": np.concatenate([O_QA + h0 * 64 + sw, O_QA + h1 * 64 + sw]),
        "ka": np.concatenate([O_KA + kv * 64 + r, O_KA + kv * 64 + r]),
        "kas": np.concatenate([O_KA + kv * 64 + sw, O_KA + kv * 64 + sw]),
        "ga0": O_GA + h0 * 64 + r, "ga1": O_GA + h1 * 64 + r,
        "qb0": O_QB + h0 * 64 + r, "qb1": O_QB + h1 * 64 + r,
        "kb0": O_KB + h0 * 64 + r, "kb1": O_KB + h1 * 64 + r,
        "gb0": O_GB + h0 * 64 + r, "gb1": O_GB + h1 * 64 + r,
        "fb": np.array([O_FB + h0, O_FB + h1]),
        "z0": O_Z + 256 * hg + a128, "z1": O_Z + 256 * hg + 128 + a128,
        "x0": O_XBC + 256 * hg + a128, "x1": O_XBC + 256 * hg + 128 + a128,
        "Bm": O_XBC + 1024 + 128 * g + a128, "Cm": O_XBC + 1280 + 128 * g + a128,
    }
    fm = np.concatenate([cols[n] for n, _ in FM_GROUPS])
    tm = np.concatenate([O_VB + h0 * 64 + r, O_VB + h1 * 64 + r, O_VA + kv * 64 + r, O_DT + 4 * hg + np.arange(4)])
    return fm, tm


def host_pvec(inp, l, hg):
    pv = np.zeros((128, NPV), np.float32)
    pv[:, 0:8] = inp["norm_mix_w"][l].reshape(8, 128).T
    g = hg // 2
    a128 = np.arange(128)
    chans = [256 * hg + a128, 256 * hg + 128 + a128, 1024 + 128 * g + a128, 1280 + 128 * g + a128]
    for c, ch in enumerate(chans):
        pv[:, 8 + 4 * c:12 + 4 * c] = inp["conv_w"][l][:, ch].T
        pv[:, 24 + c] = inp["conv_b"][l][ch]
    for j in range(2):
        pv[:, 28 + j] = np.repeat(inp["d_skip"][l][4 * hg + 2 * j:4 * hg + 2 * j + 2], 64)
    pv[:, 30:46] = np.tile(inp["dt_bias"][l][4 * hg:4 * hg + 4], 4)[None, :]
    pv[:, 46:62] = np.tile(inp["a_log"][l][4 * hg:4 * hg + 4], 4)[None, :]
    pv[0:2, 62] = inp["b_forget"][l][2 * hg:2 * hg + 2]
    pv[:, 63] = inp["swa_sinks"][l][2 * hg]
    pv[:, 64] = inp["swa_sinks"][l][2 * hg + 1]
    return pv


def host_consts(T):
    c = {}
    c["ident"] = np.eye(128, dtype=np.float32)
    s = np.arange(128)[:, None]
    t = np.arange(128)[None, :]
    triu = (s <= t).astype(np.float32)
    c["triu"] = triu
    c["maskS"] = (np.concatenate([triu, 1.0 - triu], axis=1) * np.float32(-240000.0)).astype(np.float32)
    pos = np.arange(T, dtype=np.float32)
    inv = (1.0 / (np.float32(10000.0) ** (np.arange(0, HD, 2, dtype=np.float32) / np.float32(HD)))).astype(np.float32)
    ang = (pos[:, None] * inv[None, :]).astype(np.float32)
    cosv = np.cos(ang).astype(np.float32).T
    sinv = np.sin(ang).astype(np.float32).T
    c["cos2"] = np.concatenate([cosv, cosv, cosv, cosv], axis=0)
    c["sin2"] = np.concatenate([-sinv, sinv, -sinv, sinv], axis=0)
    return c


class Ctx:
    pass


def alloc_common(P, T):
    C = Ctx()
    C.T = T
    C.ident_d = P.dram("ident", [128, 128], F32, "ExternalInput")
    C.triu_d = P.dram("triu", [128, 128], F32, "ExternalInput")
    C.identf = P.sbuf("identf", [128, 128], F32)
    C.identb = P.sbuf("identb", [128, 128], BF16)
    C.triuf = P.sbuf("triuf", [128, 128], F32)
    C.triub = P.sbuf("triub", [128, 128], BF16)
    C.onesf = P.sbuf("onesf", [128, 512], F32)
    C.onesb = P.sbuf("onesb", [128, 128], BF16)
    P.dma("sp", lambda e: e.dma_start(out=C.identf[:], in_=C.ident_d[:]), writes=[C.identf])
    P.dma("sp", lambda e: e.dma_start(out=C.triuf[:], in_=C.triu_d[:]), writes=[C.triuf])
    P.op("dve", lambda e: e.tensor_copy(out=C.identb[:], in_=C.identf[:]), reads=[C.identf], writes=[C.identb])
    P.op("dve", lambda e: e.tensor_copy(out=C.triub[:], in_=C.triuf[:]), reads=[C.triuf], writes=[C.triub])
    P.op("pool", lambda e: e.memset(C.onesf[:], 1.0), writes=[C.onesf])
    P.op("pool", lambda e: e.memset(C.onesb[:], 1.0), writes=[C.onesb])
    C.PJ = [P.psum(f"PJ{i}", [128, 512], F32) for i in range(2)]
    C.SC = [P.psum(f"SC{i}", [128, 512], F32) for i in range(2)]
    C.OT = [P.psum(f"OT{i}", [128, 512], F32) for i in range(2)]
    C.MS = [P.psum(f"MS{i}", [128, 512], F32) for i in range(2)]
    return C


import os
PARTS = set(os.environ.get("MK_PARTS", "swa,fox,ssd,f").split(","))


def emit_phaseA(P, C, hT_d, wfm_d, wtm_d, pvec_d, cos_d, sin_d, maskS_d, yT_d, tag="A"):
    T = C.T
    NT = T // TT
    NBLK = T // 128
    PJ, SC, OT = C.PJ, C.SC, C.OT
    MS0, MS1 = C.MS
    CBp = Tile(MS0[:, 0:128], tag + "CBp")
    XTp = Tile(MS0[:, 128:384], tag + "XTp")
    ACp = Tile(MS0[:, 384:400], tag + "ACp")
    STp = Tile(MS1[:, 0:256], tag + "STp")
    BTp = Tile(MS1[:, 256:320].bitcast(BF16), tag + "BTp")
    CBp.r = XTp.r = ACp.r = MS0.r
    XTs = [Tile(PJ[0][:, 0:256], tag + "XT0"), Tile(PJ[1][:, 0:256], tag + "XT1")]
    XTs[0].r = PJ[0].r
    XTs[1].r = PJ[1].r
    STp.r = BTp.r = MS1.r

    pv = P.sbuf(tag + "pv", [128, NPV], F32)
    P.dma("sp", lambda e: e.dma_start(out=pv[:], in_=pvec_d[:]), writes=[pv])
    aneg = P.sbuf(tag + "aneg", [128, 16], F32)
    P.op("act", lambda e: e.activation(out=aneg[:], in_=pv[:, 46:62], func=AF.Exp), reads=[pv], writes=[aneg])
    P.op("dve", lambda e: e.tensor_scalar(out=aneg[:], in0=aneg[:], scalar1=-1.0, scalar2=None, op0=ALU.mult), reads=[aneg], writes=[aneg])
    negb = P.sbuf(tag + "negb", [2, 1], F32)
    P.op("dve", lambda e: e.tensor_scalar(out=negb[:], in0=pv[0:2, 62:63], scalar1=-1.0, scalar2=None, op0=ALU.mult), reads=[pv], writes=[negb])
    esink = P.sbuf(tag + "esink", [128, 2], F32)
    P.op("act", lambda e: e.activation(out=esink[:], in_=pv[:, 63:65], func=AF.Exp), reads=[pv], writes=[esink])
    maskS = P.sbuf(tag + "maskS", [128, 256], BF16)
    maskSf = P.sbuf(tag + "maskSf", [128, 256], F32)
    P.dma("sp", lambda e: e.dma_start(out=maskSf[:], in_=maskS_d[:]), writes=[maskSf])
    P.op("dve", lambda e: e.tensor_copy(out=maskS[:], in_=maskSf[:]), reads=[maskSf], writes=[maskS])

    Wb = P.sbuf(tag + "Wb", [128, 8, NFM], BF16)
    Wt = P.sbuf(tag + "Wt", [128, 8, NTM], BF16)
    stg = [P.sbuf(tag + "stg", [128, NFM], F32)] * 2
    wfm_v = wfm_d.t.rearrange("(k p) n -> p k n", p=128)
    for k in range(8):
        s_ = stg[k % 2]
        P.dma("sp", lambda e, k=k, s_=s_: e.dma_start(out=s_[:], in_=wfm_v[:, k, :]), writes=[s_])
        if k % 2 == 0:
            P.op("dve", lambda e, k=k, s_=s_: e.tensor_scalar(out=Wb[:, k, :], in0=s_[:], scalar1=pv[:, k:k + 1], scalar2=None, op0=ALU.mult),
                 reads=[s_, pv], writes=[Wb])
        else:
            P.op("act", lambda e, k=k, s_=s_: e.activation(out=Wb[:, k, :], in_=s_[:], func=AF.Copy, scale=pv[:, k:k + 1]),
                 reads=[s_, pv], writes=[Wb])
    s_ = stg[0]
    P.dma("sp", lambda e: e.dma_start(out=s_[:, 0:8 * NTM].rearrange("p (k n) -> p k n", k=8), in_=wtm_d.t.rearrange("(k p) n -> p k n", p=128)), writes=[s_])
    for k in range(8):
        P.op("dve", lambda e, k=k: e.tensor_scalar(out=Wt[:, k, :], in0=s_[:, k * NTM:(k + 1) * NTM], scalar1=pv[:, k:k + 1], scalar2=None, op0=ALU.mult),
             reads=[s_, pv], writes=[Wt])

    KA = [P.sbuf(tag + f"KA{h}", [70, T], BF16) for h in range(2)]
    KAq = [[Res(f"KAq{h}_{t}") for t in range(NT)] for h in range(2)]
    KAc = [[Res(f"KAc{h}_{t}") for t in range(NT)] for h in range(2)]
    VF = P.sbuf(tag + "VF", [128, NBLK, 2, 72], BF16)
    VFr = [Res(f"VF_{t}") for t in range(NT)]
    KR = P.sbuf(tag + "KR", [128, T], BF16)
    KRr = [Res(f"KR_{t}") for t in range(NT)]
    VS = P.sbuf(tag + "VS", [128, NBLK, 72], BF16)
    VSr = [Res(f"VS_{t}") for t in range(NT)]
    QA = [P.sbuf(tag + f"QA{h}", [70, TT], BF16) for h in range(2)]
    QAc = [Res(f"QAc{h}") for h in range(2)]
    ST = P.sbuf(tag + "ST", [128, 4, 64], F32)
    STb = P.sbuf(tag + "STb", [128, 4, 64], BF16)
    ub = [P.sbuf(tag + f"ub{c}", [128, 3 + TT], F32) for c in range(4)]
    carry = P.sbuf(tag + "carry", [2, 1], F32)

    initr = Res("init")
    for h in range(2):
        P.op("pool", lambda e, h=h: e.memset(KA[h][64:70, :], 1.0), writes=[KAc[h][t] for t in range(NT)])
        P.op("pool", lambda e, h=h: e.memset(QA[h][64:70, :], 1.0), writes=[QAc[h]])
    P.op("pool", lambda e: e.memset(VF[:], 1.0), writes=VFr)
    P.op("pool", lambda e: e.memset(VS[:], 1.0), writes=VSr)
    P.op("pool", lambda e: e.memset(ST[:], 0.0), writes=[ST])
    P.op("pool", lambda e: e.memset(STb[:], 0.0), writes=[STb])
    for c in range(4):
        P.op("pool", lambda e, c=c: e.memset(ub[c][:, 0:3], 0.0), writes=[ub[c]])
    P.op("pool", lambda e: e.memset(carry[:], 0.0), writes=[carry])

    hTt = [P.sbuf(tag + f"hTt{i}", [128, 8, TT], BF16) for i in range(2)]
    cosT = [P.sbuf(tag + "cosT", [128, TT], F32)] * 2
    sinT = [P.sbuf(tag + "sinT", [128, TT], F32)] * 2
    QR = P.sbuf(tag + "QR", [128, TT], BF16)
    sg = {n: P.sbuf(tag + "sg_" + n, [m, TT], F32) for n, m in (("ga0", 64), ("ga1", 64), ("gb0", 64), ("gb1", 64), ("z0", 128), ("z1", 128))}
    xc = [P.sbuf(tag + f"xc{j}", [128, TT], F32) for j in range(2)]
    acc = P.sbuf(tag + "acc", [128, TT], F32)
    BcT = P.sbuf(tag + "BcT", [128, TT], BF16)
    CcT = P.sbuf(tag + "CcT", [128, TT], BF16)
    fe = P.sbuf(tag + "fe", [2, TT], F32)
    cc = P.sbuf(tag + "cc", [2, TT], F32)
    cr = fe
    csp = [P.sbuf(tag + f"csp{i}", [2, TT], BF16) for i in range(3)]
    cspn = [P.sbuf(tag + "cspn", [2, TT], BF16)] * 3
    dtraw = P.sbuf(tag + "dtraw", [128, 16], F32)
    dtt = P.sbuf(tag + "dtt", [128, 16], F32)
    dA = P.sbuf(tag + "dA", [128, 16], F32)
    acsc = P.sbuf(tag + "acsc", [128, 16], F32)
    PT = [P.sbuf(tag + f"PT{i}", [128, TT], BF16) for i in range(5)]
    denr = P.sbuf(tag + "denr", [65, TT], F32)
    rbc = P.sbuf(tag + "rbc", [64, TT], F32)
    yo = [P.sbuf(tag + f"yo{i}", [64, TT], BF16) for i in range(4)]
    vo = [P.sbuf(tag + f"vo{j}", [128, TT], BF16) for j in range(2)]
    Dabs = P.sbuf(tag + "Dabs", [128, 4, 128], F32)
    Ee = Dabs
    eA = P.sbuf(tag + "eA", [128, 4, 128], F32)
    CBm = P.sbuf(tag + "CBm", [128, 128], F32)
    MT2 = [P.sbuf(tag + f"MT{i}", [128, 4, 128], BF16) for i in range(2)]
    Cs2 = [P.sbuf(tag + f"Cs{i}", [128, 4, 128], BF16) for i in range(2)]
    Bc2 = [P.sbuf(tag + f"Bc{i}", [128, 128], BF16) for i in range(2)]
    cd2 = [P.sbuf(tag + f"cd{i}", [128, 4], F32) for i in range(2)]
    d4 = P.sbuf(tag + "d4", [128, 4], F32)
    dtd = P.sbuf(tag + "dtd", [128, 4], F32)
    xs2 = [P.sbuf(tag + f"xs{i}", [128, 4, 64], BF16) for i in range(2)]
    xsd2 = [P.sbuf(tag + f"xsd{i}", [128, 4, 64], BF16) for i in range(2)]
    yf = P.sbuf(tag + "yf", [128, TT], F32)
    t1, t2 = yf, acc

    hT_v = hT_d.t.rearrange("(k p) t -> p k t", p=128)
    PJB = [PJ[0], PJ[1], C.MS[0], C.MS[1]] if os.environ.get("MK_PJ4", "1") == "1" else [PJ[0], PJ[1]]
    pj_i = [0]
    pt_i = [0]

    def load_tile(tt):
        b_ = tt % 2
        sl = slice(tt * TT, (tt + 1) * TT)
        P.dma("sp", lambda e: e.dma_start(out=hTt[b_][:], in_=hT_v[:, :, sl]), writes=[hTt[b_]])

    def load_tables(tt):
        sl = slice(tt * TT, (tt + 1) * TT)
        P.dma("sp", lambda e: e.dma_start(out=cosT[0][:], in_=cos_d[:, sl]), writes=[cosT[0]])
        P.dma("sp", lambda e: e.dma_start(out=sinT[0][:], in_=sin_d[:, sl]), writes=[sinT[0]])

    def proj_fm(tt, name):
        off, m = FM_OFF[name]
        bank = PJB[pj_i[0] % len(PJB)]
        pj_i[0] += 1
        h_ = hTt[tt % 2]
        for k in range(8):
            P.op("pe", lambda e, k=k: e.matmul(bank[0:m, :], lhsT=Wb[:, k, off:off + m], rhs=h_[:, k, :], start=(k == 0), stop=(k == 7)),
                 reads=[Wb, h_], writes=[bank])
        return bank, m

    def normalize(otb, sgt, sink_col, yout, gate_part):
        if sink_col is None:
            P.op("act", lambda e: e.activation(out=denr[64:65, :], in_=otb[64:65, :], func=AF.Ln), reads=[otb], writes=[denr])
        else:
            P.op("act", lambda e: e.activation(out=denr[64:65, :], in_=otb[64:65, :], func=AF.Ln, bias=esink[64:65, sink_col:sink_col + 1]),
                 reads=[otb, esink], writes=[denr])
        P.op("act", lambda e: e.activation(out=denr[64:65, :], in_=denr[64:65, :], func=AF.Exp, scale=-1.0), reads=[denr], writes=[denr])
        bcb = SC[0]
        P.op("pe", lambda e: e.matmul(bcb[0:64, :], lhsT=C.onesf[64:65, 0:64], rhs=denr[64:65, :], start=True, stop=True),
             reads=[C.onesf, denr], writes=[bcb])
        P.op("dve", lambda e: e.tensor_tensor(out=rbc[:], in0=bcb[0:64, :], in1=sgt[gate_part], op=ALU.mult), reads=[bcb, sgt], writes=[rbc])
        P.op("dve", lambda e: e.tensor_tensor(out=yout[:], in0=otb[0:64, :], in1=rbc[:], op=ALU.mult), reads=[otb, rbc], writes=[yout])

    for tt in range(NT):
        if tt == 0:
            load_tile(0)
            load_tables(0)
        if tt + 1 < NT:
            load_tile(tt + 1)
        b_ = tt % 2
        sl = slice(tt * TT, (tt + 1) * TT)
        cs_, sn_ = cosT[b_], sinT[b_]

        for (nm, nms, dst, dres) in (("qa", "qas", QR, QR), ("ka", "kas", KR, KRr[tt])):
            bank, _ = proj_fm(tt, nm)
            P.op("dve", lambda e, bank=bank: e.tensor_tensor(out=t1[:], in0=bank[:], in1=cs_[:], op=ALU.mult), reads=[bank, cs_], writes=[t1])
            bank2, _ = proj_fm(tt, nms)
            P.op("dve", lambda e, bank2=bank2: e.tensor_tensor(out=t2[:], in0=bank2[:], in1=sn_[:], op=ALU.mult), reads=[bank2, sn_], writes=[t2])
            if dst is QR:
                P.op("pool", lambda e: e.tensor_tensor(out=QR[:], in0=t1[:], in1=t2[:], op=ALU.add), reads=[t1, t2], writes=[QR])
            else:
                P.op("pool", lambda e: e.tensor_tensor(out=KR[:, sl], in0=t1[:], in1=t2[:], op=ALU.add), reads=[t1, t2], writes=[KRr[tt]])
        if tt + 1 < NT:
            load_tables(tt + 1)
        for nm in ("ga0", "ga1", "gb0", "gb1", "z0", "z1"):
            bank, m = proj_fm(tt, nm)
            P.op("act", lambda e, bank=bank, m=m, nm=nm: e.activation(out=sg[nm][:], in_=bank[0:m, :], func=AF.Silu), reads=[bank], writes=[sg[nm]])
        for h in range(2):
            bank, _ = proj_fm(tt, f"qb{h}")
            P.op("act", lambda e, bank=bank, h=h: e.activation(out=QA[h][0:64, :], in_=bank[0:64, :], func=AF.Copy, scale=0.125), reads=[bank], writes=[QA[h]])
            bank, _ = proj_fm(tt, f"kb{h}")
            P.op("act", lambda e, bank=bank, h=h: e.copy(out=KA[h][0:64, sl], in_=bank[0:64, :]), reads=[bank], writes=[KAq[h][tt]])
        bank, _ = proj_fm(tt, "fb")
        P.op("act", lambda e, bank=bank: e.activation(out=fe[:], in_=bank[0:2, :], func=AF.Exp, scale=-1.0, bias=negb[:]), reads=[bank, negb], writes=[fe])
        P.op("act", lambda e: e.activation(out=fe[:], in_=fe[:], func=AF.Ln, bias=1.0), reads=[fe], writes=[fe])
        P.op("dve", lambda e: e.tensor_tensor_scan(out=cc[:], data0=C.onesf[0:2, 0:TT], data1=fe[:], initial=carry[:], op0=ALU.mult, op1=ALU.subtract),
             reads=[fe, carry, C.onesf], writes=[cc])
        P.op("dve", lambda e: e.tensor_copy(out=carry[:], in_=cc[:, TT - 1:TT]), reads=[cc], writes=[carry])
        P.op("act", lambda e: e.copy(out=csp[0][:], in_=cc[:]), reads=[cc], writes=[csp[0]])
        P.op("dve", lambda e: e.tensor_tensor(out=cr[:], in0=cc[:], in1=csp[0][:], op=ALU.subtract), reads=[cc, csp[0]], writes=[cr])
        P.op("act", lambda e: e.copy(out=csp[1][:], in_=cr[:]), reads=[cr], writes=[csp[1]])
        P.op("dve", lambda e: e.tensor_tensor(out=cr[:], in0=cr[:], in1=csp[1][:], op=ALU.subtract), reads=[cr, csp[1]], writes=[cr])
        P.op("act", lambda e: e.copy(out=csp[2][:], in_=cr[:]), reads=[cr], writes=[csp[2]])
        for h in range(2):
            P.dma("sp", [lambda e, h=h, i=i: e.dma_start(out=QA[h][64 + i:65 + i, :], in_=csp[i][h:h + 1, :]) for i in range(3)],
                  reads=csp, writes=[QAc[h]])
        for i in range(3):
            P.op("act", lambda e, i=i: e.activation(out=cspn[i][:], in_=csp[i][:], func=AF.Copy, scale=-1.0), reads=[csp[i]], writes=[cspn[i]])
            P.dma("sp", [lambda e, h=h, i=i: e.dma_start(out=KA[h][67 + i:68 + i, sl], in_=cspn[i][h:h + 1, :]) for h in range(2)],
                  reads=[cspn[i]], writes=[KAc[0][tt], KAc[1][tt]])
        for c, nm in enumerate(("x0", "x1", "Bm", "Cm")):
            bank, _ = proj_fm(tt, nm)
            u = ub[c]
            P.op("act", lambda e, bank=bank, u=u: e.copy(out=u[:, 3:3 + TT], in_=bank[:]), reads=[bank], writes=[u])
        for i in range(4):
            blk = tt * 4 + i
            bank = PJB[pj_i[0] % len(PJB)]
            pj_i[0] += 1
            h_ = hTt[b_]
            for k in range(8):
                P.op("pe", lambda e, k=k, i=i, bank=bank: e.matmul(bank[:, 0:NTM], lhsT=h_[:, k, i * 128:(i + 1) * 128], rhs=Wt[:, k, :], start=(k == 0), stop=(k == 7)),
                     reads=[Wt, h_], writes=[bank])
            P.op("act", lambda e, blk=blk, bank=bank: e.copy(out=VF[:, blk, :, 0:64], in_=bank[:, 0:128].rearrange("p (h d) -> p h d", h=2)), reads=[bank], writes=[VFr[tt]])
            if os.environ.get("MK_DBG") == "1" and i == 2:
                P.op("dve", lambda e, blk=blk, bank=bank: e.tensor_copy(out=t1[:, 0:64], in_=bank[:, 128:192]), reads=[bank], writes=[VSr[tt]])
            elif os.environ.get("MK_DBG") == "2" and i == 2:
                P.op("dve", lambda e, blk=blk, bank=bank: e.tensor_copy(out=VS[:, blk, 0:64], in_=t2[:, 128:192]), reads=[bank], writes=[VSr[tt]])
            else:
                P.op("dve", lambda e, blk=blk, bank=bank: e.tensor_copy(out=VS[:, blk, 0:64], in_=bank[:, 128:192]), reads=[bank], writes=[VSr[tt]])
            P.op("dve", lambda e, i=i, bank=bank: e.tensor_copy(out=dtraw[:, 4 * i:4 * i + 4], in_=bank[:, 192:196]), reads=[bank], writes=[dtraw])

        accs = [acc, yf]
        for c in range(4):
            u = ub[c]
            ac = accs[c % 2]
            P.op("act", lambda e, u=u, c=c, ac=ac: e.activation(out=ac[:], in_=u[:, 3:3 + TT], func=AF.Identity, scale=pv[:, 8 + 4 * c + 3:8 + 4 * c + 4], bias=pv[:, 24 + c:25 + c]),
                 reads=[u, pv], writes=[ac])
            for k in range(3):
                P.op("dve", lambda e, u=u, c=c, k=k, ac=ac: e.scalar_tensor_tensor(out=ac[:], in0=u[:, k:k + TT], scalar=pv[:, 8 + 4 * c + k:8 + 4 * c + k + 1], in1=ac[:], op0=ALU.mult, op1=ALU.add),
                     reads=[u, pv, ac], writes=[ac])
            P.op("pool", lambda e, u=u: e.tensor_copy(out=u[:, 0:3], in_=u[:, TT:TT + 3]), reads=[u], writes=[u])
            dst = xc[c] if c < 2 else (BcT if c == 2 else CcT)
            P.op("act", lambda e, dst=dst, ac=ac: e.activation(out=dst[:], in_=ac[:], func=AF.Silu), reads=[ac], writes=[dst])

        if "swa" in PARTS:
            sitems = [(hh, i) for i in range(4) for hh in range(2)]
            sbanks = [SC[0], SC[1], PJ[0], PJ[1]]
            spts = {}

            def swa_s1(idx):
                hh, i = sitems[idx]
                pb = 64 * hh
                qb = tt * 4 + i
                scb = sbanks[idx % 4]
                pt = PT[pt_i[0] % len(PT)]
                pt_i[0] += 1
                spts[idx] = pt
                kres = [KRr[tt]] if i > 0 else ([KRr[tt], KRr[tt - 1]] if tt > 0 else [KRr[tt]])
                qcols = slice(i * 128, (i + 1) * 128)
                if qb > 0:
                    P.op("pe", lambda e: e.matmul(scb[:, 0:128], lhsT=KR[pb:pb + 64, (qb - 1) * 128:qb * 128], rhs=QR[pb:pb + 64, qcols], start=True, stop=False),
                         reads=kres + [QR], writes=[scb])
                    P.op("pe", lambda e: e.matmul(scb[:, 0:128], lhsT=C.identb[:], rhs=maskS[:, 0:128], start=False, stop=True),
                         reads=[C.identb, maskS], writes=[scb])
                P.op("pe", lambda e: e.matmul(scb[:, 128:256], lhsT=KR[pb:pb + 64, qb * 128:(qb + 1) * 128], rhs=QR[pb:pb + 64, qcols], start=True, stop=False),
                     reads=kres + [QR], writes=[scb])
                P.op("pe", lambda e: e.matmul(scb[:, 128:256], lhsT=C.identb[:], rhs=maskS[:, 128:256], start=False, stop=True),
                     reads=[C.identb, maskS], writes=[scb])
                lo = 0 if qb > 0 else 128
                P.op("act", lambda e: e.activation(out=pt[:, lo:256], in_=scb[:, lo:256], func=AF.Exp, scale=0.125), reads=[scb], writes=[pt])

            def swa_s2(idx):
                hh, i = sitems[idx]
                qb = tt * 4 + i
                pt = spts[idx]
                otb = OT[hh]
                vres = [VSr[tt]] if i > 0 else ([VSr[tt], VSr[tt - 1]] if tt > 0 else [VSr[tt]])
                qcols = slice(i * 128, (i + 1) * 128)
                if qb > 0:
                    P.op("pe", lambda e: e.matmul(otb[0:65, qcols], lhsT=VS[:, qb - 1, 0:65], rhs=pt[:, 0:128], start=True, stop=False),
                         reads=vres + [pt], writes=[otb])
                P.op("pe", lambda e: e.matmul(otb[0:65, qcols], lhsT=VS[:, qb, 0:65], rhs=pt[:, 128:256], start=(qb == 0), stop=True),
                     reads=vres + [pt], writes=[otb])

            SD = 3
            for idx in range(len(sitems) + SD):
                if idx < len(sitems):
                    swa_s1(idx)
                if idx >= SD:
                    swa_s2(idx - SD)
            for hh in range(2):
                normalize(OT[hh], sg[f"ga{hh}"], hh, yo[hh], slice(None))

        if "fox" in PARTS:
            nkb = 4 * tt + 4
            items = [(h, j) for j in range(nkb) for h in range(2)]
            fbanks = [SC[0], SC[1], PJ[0], PJ[1]]
            NPT = len(PT)

            def fox_s1(idx):
                h, j = items[idx]
                i = j - 4 * tt
                c0 = 0 if i <= 0 else i * 128
                cols = slice(c0, TT)
                scb = fbanks[idx % 4]
                pt = PT[idx % NPT]
                jt = j // 4
                P.op("pe", lambda e: e.matmul(scb[:, cols], lhsT=KA[h][0:70, j * 128:(j + 1) * 128], rhs=QA[h][0:70, cols], start=True, stop=(i < 0)),
                     reads=[KAq[h][jt], KAc[h][jt], QA[h], QAc[h]], writes=[scb])
                if i >= 0:
                    dcol = slice(i * 128, (i + 1) * 128)
                    P.op("pe", lambda e: e.matmul(scb[:, dcol], lhsT=C.identb[:], rhs=maskS[:, 128:256], start=False, stop=True),
                         reads=[C.identb, maskS], writes=[scb])
                P.op("act", lambda e: e.activation(out=pt[:, cols], in_=scb[:, cols], func=AF.Exp), reads=[scb], writes=[pt])

            def fox_s2(idx):
                h, j = items[idx]
                i = j - 4 * tt
                c0 = 0 if i <= 0 else i * 128
                cols = slice(c0, TT)
                pt = PT[idx % NPT]
                jt = j // 4
                otb = OT[h]
                P.op("pe", lambda e: e.matmul(otb[0:65, cols], lhsT=VF[:, j, h, 0:65], rhs=pt[:, cols], start=(j == 0), stop=(j == nkb - 1)),
                     reads=[VFr[jt], pt], writes=[otb])

            for idx in range(len(items) + FOX_D):
                if idx < len(items):
                    fox_s1(idx)
                if idx >= FOX_D:
                    fox_s2(idx - FOX_D)
            for h in range(2):
                normalize(OT[h], sg[f"gb{h}"], None, yo[2 + h], slice(None))

        P.op("dve", lambda e: e.tensor_tensor(out=dtt[:], in0=dtraw[:], in1=pv[:, 30:46], op=ALU.add), reads=[dtraw, pv], writes=[dtt])
        P.op("act", lambda e: e.activation(out=dtt[:], in_=dtt[:], func=AF.Exp), reads=[dtt], writes=[dtt])
        P.op("act", lambda e: e.activation(out=dtt[:], in_=dtt[:], func=AF.Ln, bias=1.0), reads=[dtt], writes=[dtt])
        P.op("dve", lambda e: e.tensor_tensor(out=dA[:], in0=dtt[:], in1=aneg[:], op=ALU.mult), reads=[dtt, aneg], writes=[dA])
        P.op("pe", lambda e: e.matmul(ACp[:], lhsT=C.triuf[:], rhs=dA[:], start=True, stop=True), reads=[C.triuf, dA], writes=[ACp])
        P.op("dve", lambda e: e.tensor_copy(out=acsc[:], in_=ACp[:]), reads=[ACp], writes=[acsc])
        def ssd_A(c):
            ccols = slice(c * 128, (c + 1) * 128)
            par = c % 2
            AB = SC[par]
            MT_, Cs_, xs_, xsd_, Bc_, cd_ = MT2[par], Cs2[par], xs2[par], xsd2[par], Bc2[par], cd2[par]
            for h in range(4):
                ch = 4 * c + h
                P.op("pe", lambda e, h=h, ch=ch: e.matmul(AB[:, h * 128:(h + 1) * 128], lhsT=dA[:, ch:ch + 1].to_broadcast([128, 128]), rhs=C.triuf[:], start=True, stop=True),
                     reads=[dA, C.triuf], writes=[AB])
            for h in range(4):
                ch = 4 * c + h
                P.op("dve", lambda e, h=h, ch=ch: e.tensor_scalar(out=Dabs[:, h, :], in0=AB[:, h * 128:(h + 1) * 128], scalar1=acsc[:, ch:ch + 1], scalar2=0.0, op0=ALU.subtract, op1=ALU.min),
                     reads=[AB, acsc], writes=[Dabs])
            P.op("act", lambda e: e.activation(out=Ee[:].rearrange("p h l -> p (h l)"), in_=Dabs[:].rearrange("p h l -> p (h l)"), func=AF.Exp), reads=[Dabs], writes=[Ee])
            P.op("act", lambda e: e.activation(out=eA[:].rearrange("p h l -> p (h l)"), in_=AB[:], func=AF.Exp), reads=[AB], writes=[eA])
            P.op("dve", lambda e: e.tensor_tensor(out=d4[:], in0=AB[:].rearrange("p (h l) -> p h l", h=4)[:, :, 127], in1=acsc[:, 4 * c:4 * c + 4], op=ALU.subtract),
                 reads=[AB, acsc], writes=[d4])
            P.op("act", lambda e: e.activation(out=d4[:], in_=d4[:], func=AF.Exp), reads=[d4], writes=[d4])
            P.op("dve", lambda e: e.tensor_tensor(out=dtd[:], in0=d4[:], in1=dtt[:, 4 * c:4 * c + 4], op=ALU.mult), reads=[d4, dtt], writes=[dtd])
            P.op("dve", lambda e: e.tensor_copy(out=cd_[:], in_=eA[:, :, 127]), reads=[eA], writes=[cd_])
            P.op("pe", lambda e: e.matmul(CBp[:], lhsT=BcT[:, ccols], rhs=CcT[:, ccols], start=True, stop=True), reads=[BcT, CcT], writes=[CBp])
            P.op("dve", lambda e: e.tensor_tensor(out=CBm[:], in0=CBp[:], in1=C.triuf[:], op=ALU.mult), reads=[CBp, C.triuf], writes=[CBm])
            P.op("dve", lambda e: e.tensor_tensor(out=MT_[:], in0=Ee[:], in1=CBm[:].unsqueeze(1).to_broadcast([128, 4, 128]), op=ALU.mult), reads=[Ee, CBm], writes=[MT_])
            P.op("pool", lambda e: e.tensor_tensor(out=Cs_[:], in0=eA[:], in1=CcT[:, ccols].unsqueeze(1).to_broadcast([128, 4, 128]), op=ALU.mult), reads=[eA, CcT], writes=[Cs_])
            XTp = XTs[par]
            for j in range(2):
                P.op("pe", lambda e, j=j: e.transpose(out=XTp[:, j * 128:(j + 1) * 128], in_=xc[j][:, ccols], identity=C.identf[:]), reads=[xc[j], C.identf], writes=[XTp])
            P.op("pe", lambda e: e.transpose(out=BTp[:], in_=BcT[:, ccols], identity=C.identb[:]), reads=[BcT, C.identb], writes=[BTp])
            P.op("act", lambda e: e.copy(out=Bc_[:], in_=BTp[:]), reads=[BTp], writes=[Bc_])
            P.op("dve", lambda e: e.tensor_tensor(out=xs_[:], in0=XTp[:].rearrange("p (h d) -> p h d", h=4), in1=dtt[:, 4 * c:4 * c + 4].unsqueeze(2).to_broadcast([128, 4, 64]), op=ALU.mult),
                 reads=[XTp, dtt], writes=[xs_])
            P.op("dve", lambda e: e.tensor_tensor(out=xsd_[:], in0=XTp[:].rearrange("p (h d) -> p h d", h=4), in1=dtd[:].unsqueeze(2).to_broadcast([128, 4, 64]), op=ALU.mult),
                 reads=[XTp, dtd], writes=[xsd_])

        def ssd_B(c):
            ccols = slice(c * 128, (c + 1) * 128)
            par = c % 2
            MT_, Cs_, xs_, xsd_, Bc_, cd_ = MT2[par], Cs2[par], xs2[par], xsd2[par], Bc2[par], cd2[par]
            for h in range(4):
                yb_ = OT[h // 2]
                po = (h % 2) * 64
                P.op("pe", lambda e, h=h: e.matmul(yb_[po:po + 64, ccols], lhsT=xs_[:, h, :], rhs=MT_[:, h, :], start=True, stop=False),
                     reads=[xs_, MT_], writes=[yb_])
                P.op("pe", lambda e, h=h: e.matmul(yb_[po:po + 64, ccols], lhsT=STb[:, h, :], rhs=Cs_[:, h, :], start=False, stop=True),
                     reads=[STb, Cs_], writes=[yb_])
            P.op("pe", lambda e: e.matmul(STp[:], lhsT=Bc_[:], rhs=xsd_[:].rearrange("p h d -> p (h d)"), start=True, stop=True), reads=[Bc_, xsd_], writes=[STp])
            P.op("dve", lambda e: e.tensor_tensor(out=ST[:], in0=ST[:], in1=cd_[:].unsqueeze(2).to_broadcast([128, 4, 64]), op=ALU.mult), reads=[ST, cd_], writes=[ST])
            P.op("dve", lambda e: e.tensor_tensor(out=ST[:], in0=ST[:], in1=STp[:].rearrange("p (h d) -> p h d", h=4), op=ALU.add), reads=[ST, STp], writes=[ST])
            P.op("act", lambda e: e.copy(out=STb[:], in_=ST[:]), reads=[ST], writes=[STb])

        if "ssd" in PARTS:
            ssd_A(0)
            for c in range(4):
                if c + 1 < 4:
                    ssd_A(c + 1)
                ssd_B(c)
        for j in range(2):
            P.op("dve", lambda e, j=j: e.scalar_tensor_tensor(out=yf[:], in0=xc[j][:], scalar=pv[:, 28 + j:29 + j], in1=OT[j][:], op0=ALU.mult, op1=ALU.add),
                 reads=[xc[j], pv, OT[j]], writes=[yf])
            P.op("pool", lambda e, j=j: e.tensor_tensor(out=vo[j][:], in0=yf[:], in1=sg[f"z{j}"][:], op=ALU.mult), reads=[yf, sg[f"z{j}"]], writes=[vo[j]])

        for i in range(4):
            P.dma("pool", lambda e, i=i: e.dma_start(out=yT_d[i * 64:(i + 1) * 64, sl], in_=yo[i][:]), reads=[yo[i]])
        for j in range(2):
            P.dma("pool", lambda e, j=j: e.dma_start(out=yT_d[256 + j * 128:256 + (j + 1) * 128, sl], in_=vo[j][:]), reads=[vo[j]])


def build_A(T):
    nc = bass.Bass("TRN2", target_bir_lowering=False)
    with ExitStack() as st:
        P = Prog(nc, st)
        C = alloc_common(P, T)
        hT_d = P.dram("hT", [D, T], BF16, "ExternalInput")
        wfm_d = P.dram("wfm", [D, NFM], F32, "ExternalInput")
        wtm_d = P.dram("wtm", [D, NTM], F32, "ExternalInput")
        pvec_d = P.dram("pvec", [128, NPV], F32, "ExternalInput")
        cos_d = P.dram("cos2", [128, T], F32, "ExternalInput")
        sin_d = P.dram("sin2", [128, T], F32, "ExternalInput")
        maskS_d = P.dram("maskS", [128, 256], F32, "ExternalInput")
        yT_d = P.dram("yT", [512, T], BF16, "ExternalOutput")
        emit_phaseA(P, C, hT_d, wfm_d, wtm_d, pvec_d, cos_d, sin_d, maskS_d, yT_d)
        stats = P.emit()
        print("phaseA ops", len(P.ops), stats, "sems", P.n_sems)
    return nc


NPVB = 32


def host_pvecB(inp, l):
    pv = np.zeros((128, NPVB), np.float32)
    pv[:, 0:8] = inp["norm_xq_w"][l].reshape(8, 128).T
    pv[:, 8:16] = inp["norm_mem_w"][l].reshape(8, 128).T
    nw = inp["ssm_norm_w"][l]
    for hg in range(4):
        pv[:, 16 + 4 * hg + 0] = 1.0
        pv[:, 16 + 4 * hg + 1] = 1.0
        pv[:, 16 + 4 * hg + 2] = nw[256 * hg:256 * hg + 128]
        pv[:, 16 + 4 * hg + 3] = nw[256 * hg + 128:256 * hg + 256]
    return pv


def wout_row_perm():
    rows = []
    for hg in range(4):
        rows.append(np.arange(2 * hg * 64, (2 * hg + 2) * 64))
        rows.append(512 + np.arange(2 * hg * 64, (2 * hg + 2) * 64))
        rows.append(1024 + np.arange(256 * hg, 256 * (hg + 1)))
    return np.concatenate(rows)


def load_weight(P, stg, dst, src_d, KC, N, scale_fn, eng_cycle=("dve", "act")):
    src_v = src_d.t.rearrange("(k p) n -> p k n", p=128)
    for k in range(KC):
        s_ = stg[k % 2]
        P.dma("sp", lambda e: e.dma_start(out=s_[:, 0:N], in_=src_v[:, k, :]), writes=[s_])
        sc = scale_fn(k) if scale_fn is not None else None
        eng = eng_cycle[k % len(eng_cycle)]
        if sc is None:
            if eng == "act":
                P.op(eng, lambda e: e.copy(out=dst[:, k, :], in_=s_[:, 0:N]), reads=[s_], writes=[dst])
            else:
                P.op(eng, lambda e: e.tensor_copy(out=dst[:, k, :], in_=s_[:, 0:N]), reads=[s_], writes=[dst])
        else:
            tl, ap = sc
            if eng == "act":
                P.op(eng, lambda e: e.activation(out=dst[:, k, :], in_=s_[:, 0:N], func=AF.Copy, scale=ap), reads=[s_, tl], writes=[dst])
            else:
                P.op(eng, lambda e: e.tensor_scalar(out=dst[:, k, :], in0=s_[:, 0:N], scalar1=ap, scalar2=None, op0=ALU.mult), reads=[s_, tl], writes=[dst])


def emit_norm_T(P, C, xin, xin_res, hT_out, col0, sq, ss, hb, want_T=True):
    P.op("act", lambda e: e.activation(out=sq[:], in_=xin, func=AF.Square, accum_out=ss[:]), reads=[xin_res], writes=[sq, ss])
    P.op("act", lambda e: e.activation(out=ss[:], in_=ss[:], func=AF.Ln, scale=1.0 / D, bias=EPS), reads=[ss], writes=[ss])
    P.op("act", lambda e: e.activation(out=ss[:], in_=ss[:], func=AF.Exp, scale=-0.5), reads=[ss], writes=[ss])
    if not want_T:
        return
    P.op("dve", lambda e: e.tensor_scalar(out=hb[:], in0=xin, scalar1=ss[:], scalar2=None, op0=ALU.mult), reads=[xin_res, ss], writes=[hb])
    tp = C.TPb
    for k in range(8):
        P.op("pe", lambda e, k=k: e.transpose(out=tp[:, k * 128:(k + 1) * 128], in_=hb[:, k * 128:(k + 1) * 128], identity=C.identb[:]), reads=[hb, C.identb], writes=[tp])
    P.op("act", lambda e: e.copy(out=hT_out[:, :, col0:col0 + 128], in_=tp[:].rearrange("p (k t) -> p k t", k=8)), reads=[tp], writes=[hT_out])


def emit_phaseB(P, C, TQ, x_d, yT_d, wout_d, pvB_d, wmq_d, wmk_d, wmv_d, wmo_d, mem_d, xout_d, hTn_d, out_d, fnw_d, final, tag="B"):
    NTB = TQ // TT
    PJ, SC, OT, MS = C.PJ, C.SC, C.OT, C.MS
    C.TPb = Tile(MS[1][:].bitcast(BF16), "TPb")
    C.TPb.r = MS[1].r
    pvB = P.sbuf(tag + "pvB", [128, NPVB], F32)
    P.dma("sp", lambda e: e.dma_start(out=pvB[:], in_=pvB_d[:]), writes=[pvB])
    stg = [P.sbuf(tag + f"stg{i}", [128, 1024], F32) for i in range(2)]
    WO = P.sbuf(tag + "WO", [128, 16, 1024], BF16)
    WQ = P.sbuf(tag + "WQ", [128, 8, 1024], BF16)
    WMO = P.sbuf(tag + "WMO", [128, 8, 1024], BF16)
    WT = P.sbuf(tag + "WT", [128, 8, 1024], BF16)
    KmT = P.sbuf(tag + "KmT", [128, 8, 256], BF16)
    Vm = P.sbuf(tag + "Vm", [128, 2, 1024], BF16)
    memnT = P.sbuf(tag + "memnT", [128, 8, 256], BF16)
    xt = P.sbuf(tag + "xt", [128, 4, 1024], F32)
    yt = P.sbuf(tag + "yt", [128, 16, TT], BF16)
    sq8 = P.sbuf(tag + "sq8", [128, 8, TT], BF16)
    sq = P.sbuf(tag + "sq", [128, 1024], F32)
    ss = P.sbuf(tag + "ss", [128, 1], F32)
    rs = P.sbuf(tag + "rs", [128, 4], F32)
    hb = P.sbuf(tag + "hb", [128, 1024], BF16)
    hqT = P.sbuf(tag + "hqT", [128, 8, TT], BF16)
    qT = P.sbuf(tag + "qT", [128, 8, TT], BF16)
    oT = P.sbuf(tag + "oT", [128, 8, TT], BF16)
    hTn = P.sbuf(tag + "hTn", [128, 8, TT], BF16)
    ptm = [P.sbuf(tag + f"ptm{i}", [128, TT], BF16) for i in range(2)]
    rden = P.sbuf(tag + "rden", [128, TT], F32)
    if final:
        fnw = P.sbuf(tag + "fnw", [128, 1024], F32)
        P.dma("sp", lambda e: e.dma_start(out=fnw[:], in_=fnw_d[:]), writes=[fnw])
        ob = P.sbuf(tag + "ob", [128, 1024], F32)

    for mb in range(2):
        P.dma("sp", lambda e: e.dma_start(out=xt[:, mb, :], in_=mem_d[mb * 128:(mb + 1) * 128, :]), writes=[xt])
        emit_norm_T(P, C, xt[:, mb, :], xt, memnT, mb * 128, sq, ss, hb)
    load_weight(P, stg, WT, wmk_d, 8, 1024, lambda k: (pvB, pvB[:, 8 + k:9 + k]))
    for c in range(8):
        bank = PJ[c % 2]
        for k in range(8):
            P.op("pe", lambda e, k=k: e.matmul(bank[:, 0:256], lhsT=WT[:, k, c * 128:(c + 1) * 128], rhs=memnT[:, k, :], start=(k == 0), stop=(k == 7)), reads=[WT, memnT], writes=[bank])
        P.op("act", lambda e: e.copy(out=KmT[:, c, :], in_=bank[:, 0:256]), reads=[bank], writes=[KmT])
    load_weight(P, stg, WT, wmv_d, 8, 1024, lambda k: (pvB, pvB[:, 8 + k:9 + k]))
    for mc in range(2):
        for half in range(2):
            bank = PJ[(mc * 2 + half) % 2]
            for k in range(8):
                P.op("pe", lambda e, k=k: e.matmul(bank[:], lhsT=memnT[:, k, mc * 128:(mc + 1) * 128], rhs=WT[:, k, half * 512:(half + 1) * 512], start=(k == 0), stop=(k == 7)), reads=[WT, memnT], writes=[bank])
            P.op("act", lambda e: e.copy(out=Vm[:, mc, half * 512:(half + 1) * 512], in_=bank[:]), reads=[bank], writes=[Vm])
    load_weight(P, stg, WO, wout_d, 16, 1024, lambda k: (pvB, pvB[:, 16 + k:17 + k]))
    load_weight(P, stg, WQ, wmq_d, 8, 1024, lambda k: (pvB, pvB[:, k:k + 1]))
    load_weight(P, stg, WMO, wmo_d, 8, 1024, None)

    yT_v = yT_d.t.rearrange("(c p) t -> p c t", p=128)
    x_v = x_d.t.rearrange("(n p) d -> p n d", p=128)
    xo_v = xout_d.t.rearrange("(n p) d -> p n d", p=128) if xout_d is not None else None
    out_v = out_d.t.rearrange("(n p) d -> p n d", p=128) if out_d is not None else None
    attn_ch = [c for c in range(16) if c % 4 < 2]
    ssm_ch = [c for c in range(16) if c % 4 >= 2]
    SSp = Tile(MS[0][:, 0:4], "SSp")
    SSp.r = MS[0].r
    for tb in range(NTB):
        sl = slice(tb * TT, (tb + 1) * TT)
        P.dma("sp", lambda e: e.dma_start(out=yt[:], in_=yT_v[:, :, sl]), writes=[yt])
        P.dma("sp", lambda e: e.dma_start(out=xt[:], in_=x_v[:, tb * 4:(tb + 1) * 4, :]), writes=[xt])
        for j, c in enumerate(ssm_ch):
            P.op("act", lambda e: e.activation(out=sq8[:, j, :], in_=yt[:, c, :], func=AF.Square), reads=[yt], writes=[sq8])
        for i in range(4):
            for j in range(8):
                P.op("pe", lambda e: e.matmul(SSp[:, i:i + 1], lhsT=sq8[:, j, i * 128:(i + 1) * 128], rhs=C.onesb[:, 0:1], start=(j == 0), stop=(j == 7)), reads=[sq8, C.onesb], writes=[SSp])
        P.op("act", lambda e: e.activation(out=rs[:], in_=SSp[:], func=AF.Ln, scale=1.0 / 1024, bias=EPS), reads=[SSp], writes=[rs])
        P.op("act", lambda e: e.activation(out=rs[:], in_=rs[:], func=AF.Exp, scale=-0.5), reads=[rs], writes=[rs])
        for i in range(4):
            for half in range(2):
                hs = slice(half * 512, (half + 1) * 512)
                A, S = PJ[0], PJ[1]
                for n_, c in enumerate(attn_ch):
                    P.op("pe", lambda e: e.matmul(A[:], lhsT=yt[:, c, i * 128:(i + 1) * 128], rhs=WO[:, c, hs], start=(n_ == 0), stop=(n_ == 7)), reads=[yt, WO], writes=[A])
                for n_, c in enumerate(ssm_ch):
                    P.op("pe", lambda e: e.matmul(S[:], lhsT=yt[:, c, i * 128:(i + 1) * 128], rhs=WO[:, c, hs], start=(n_ == 0), stop=(n_ == 7)), reads=[yt, WO], writes=[S])
                P.op("dve", lambda e: e.tensor_tensor(out=xt[:, i, hs], in0=A[:], in1=xt[:, i, hs], op=ALU.add), reads=[A, xt], writes=[xt])
                P.op("dve", lambda e: e.scalar_tensor_tensor(out=xt[:, i, hs], in0=S[:], scalar=rs[:, i:i + 1], in1=xt[:, i, hs], op0=ALU.mult, op1=ALU.add), reads=[S, rs, xt], writes=[xt])
            emit_norm_T(P, C, xt[:, i, :], xt, hqT, i * 128, sq, ss, hb)
        for c in range(8):
            bank = PJ[c % 2]
            for k in range(8):
                P.op("pe", lambda e, k=k: e.matmul(bank[:], lhsT=WQ[:, k, c * 128:(c + 1) * 128], rhs=hqT[:, k, :], start=(k == 0), stop=(k == 7)), reads=[WQ, hqT], writes=[bank])
            P.op("act", lambda e: e.activation(out=qT[:, c, :], in_=bank[:], func=AF.Copy, scale=1.0 / 16), reads=[bank], writes=[qT])
        for h in range(4):
            for mc in range(2):
                scb = SC[mc]
                for dc in range(2):
                    P.op("pe", lambda e: e.matmul(scb[:], lhsT=KmT[:, 2 * h + dc, mc * 128:(mc + 1) * 128], rhs=qT[:, 2 * h + dc, :], start=(dc == 0), stop=(dc == 1)), reads=[KmT, qT], writes=[scb])
                P.op("act", lambda e: e.activation(out=ptm[mc][:], in_=scb[:], func=AF.Exp), reads=[scb], writes=[ptm[mc]])
            den = MS[0]
            for mc in range(2):
                P.op("pe", lambda e: e.matmul(den[:], lhsT=C.onesb[:], rhs=ptm[mc][:], start=(mc == 0), stop=(mc == 1)), reads=[C.onesb, ptm[mc]], writes=[den])
            P.op("act", lambda e: e.activation(out=rden[:], in_=den[:], func=AF.Ln), reads=[den], writes=[rden])
            P.op("act", lambda e: e.activation(out=rden[:], in_=rden[:], func=AF.Exp, scale=-1.0), reads=[rden], writes=[rden])
            for dc in range(2):
                ob_ = OT[dc]
                for mc in range(2):
                    P.op("pe", lambda e: e.matmul(ob_[:], lhsT=Vm[:, mc, (2 * h + dc) * 128:(2 * h + dc + 1) * 128], rhs=ptm[mc][:], start=(mc == 0), stop=(mc == 1)), reads=[Vm, ptm[mc]], writes=[ob_])
                P.op("dve", lambda e: e.tensor_tensor(out=oT[:, 2 * h + dc, :], in0=ob_[:], in1=rden[:], op=ALU.mult), reads=[ob_, rden], writes=[oT])
        for i in range(4):
            for half in range(2):
                hs = slice(half * 512, (half + 1) * 512)
                bank = PJ[half]
                for k in range(8):
                    P.op("pe", lambda e, k=k: e.matmul(bank[:], lhsT=oT[:, k, i * 128:(i + 1) * 128], rhs=WMO[:, k, hs], start=(k == 0), stop=(k == 7)), reads=[oT, WMO], writes=[bank])
                P.op("dve", lambda e: e.tensor_tensor(out=xt[:, i, hs], in0=bank[:], in1=xt[:, i, hs], op=ALU.add), reads=[bank, xt], writes=[xt])
            if final:
                emit_norm_T(P, C, xt[:, i, :], xt, None, 0, sq, ss, hb, want_T=False)
                P.op("dve", lambda e: e.scalar_tensor_tensor(out=ob[:], in0=xt[:, i, :], scalar=ss[:], in1=fnw[:], op0=ALU.mult, op1=ALU.mult), reads=[xt, ss, fnw], writes=[ob])
                P.dma("pool", lambda e: e.dma_start(out=out_v[:, tb * 4 + i, :], in_=ob[:]), reads=[ob])
            else:
                emit_norm_T(P, C, xt[:, i, :], xt, hTn, i * 128, sq, ss, hb)
        if not final:
            P.dma("pool", lambda e: e.dma_start(out=xo_v[:, tb * 4:(tb + 1) * 4, :], in_=xt[:]), reads=[xt])
            P.dma("pool", lambda e: e.dma_start(out=hTn_d.t.rearrange("(k p) t -> p k t", p=128)[:, :, sl], in_=hTn[:]), reads=[hTn])


def build_B(TQ, final):
    nc = bass.Bass("TRN2", target_bir_lowering=False)
    with ExitStack() as st:
        P = Prog(nc, st)
        C = alloc_common(P, TQ)
        x_d = P.dram("x", [TQ, D], F32, "ExternalInput")
        yT_d = P.dram("yT", [2048, TQ], BF16, "ExternalInput")
        wout_d = P.dram("wout", [2048, D], F32, "ExternalInput")
        pvB_d = P.dram("pvB", [128, NPVB], F32, "ExternalInput")
        wmq_d = P.dram("wmq", [D, D], F32, "ExternalInput")
        wmk_d = P.dram("wmk", [D, D], F32, "ExternalInput")
        wmv_d = P.dram("wmv", [D, D], F32, "ExternalInput")
        wmo_d = P.dram("wmo", [D, D], F32, "ExternalInput")
        mem_d = P.dram("mem", [MEMT, D], F32, "ExternalInput")
        if final:
            fnw_d = P.dram("fnw", [128, D], F32, "ExternalInput")
            out_d = P.dram("out", [TQ, D], F32, "ExternalOutput")
            xout_d = hTn_d = None
        else:
            fnw_d = out_d = None
            xout_d = P.dram("xout", [TQ, D], F32, "ExternalOutput")
            hTn_d = P.dram("hTn", [D, TQ], BF16, "ExternalOutput")
        emit_phaseB(P, C, TQ, x_d, yT_d, wout_d, pvB_d, wmq_d, wmk_d, wmv_d, wmo_d, mem_d, xout_d, hTn_d, out_d, fnw_d, final)
        stats = P.emit()
        print("phaseB ops", len(P.ops), stats, "sems", P.n_sems)
    return nc


def build_N(TQ):
    nc = bass.Bass("TRN2", target_bir_lowering=False)
    with ExitStack() as st:
        P = Prog(nc, st)
        C = alloc_common(P, TQ)
        C.TPb = Tile(C.MS[1][:].bitcast(BF16), "TPb")
        C.TPb.r = C.MS[1].r
        x_d = P.dram("x", [TQ, D], F32, "ExternalInput")
        hTn_d = P.dram("hTn", [D, TQ], BF16, "ExternalOutput")
        xt = P.sbuf("Nxt", [128, 4, 1024], F32)
        sq = P.sbuf("Nsq", [128, 1024], F32)
        ss = P.sbuf("Nss", [128, 1], F32)
        hb = P.sbuf("Nhb", [128, 1024], BF16)
        hTn = P.sbuf("NhTn", [128, 8, TT], BF16)
        x_v = x_d.t.rearrange("(n p) d -> p n d", p=128)
        for tb in range(TQ // TT):
            sl = slice(tb * TT, (tb + 1) * TT)
            P.dma("sp", lambda e: e.dma_start(out=xt[:], in_=x_v[:, tb * 4:(tb + 1) * 4, :]), writes=[xt])
            for i in range(4):
                emit_norm_T(P, C, xt[:, i, :], xt, hTn, i * 128, sq, ss, hb)
            P.dma("pool", lambda e: e.dma_start(out=hTn_d.t.rearrange("(k p) t -> p k t", p=128)[:, :, sl], in_=hTn[:]), reads=[hTn])
        stats = P.emit()
        print("phaseN ops", len(P.ops), stats, "sems", P.n_sems)
    return nc


def _run(nc, in_maps):
    res = run_bass_kernel_spmd(nc, in_maps, core_ids=list(range(8)))
    return res.results


def kernel_unfused(**inputs):
    inp = {k: np.asarray(v) for k, v in inputs.items()}
    T = SEQ
    TQ = T // 4
    x = inp["x"]
    consts = host_consts(T)
    common = {"ident": consts["ident"], "triu": consts["triu"]}
    cores = [(c // 4, c % 4) for c in range(8)]
    ncN = build_N(TQ)
    r = _run(ncN, [dict(x=np.ascontiguousarray(x[b, q * TQ:(q + 1) * TQ]), **common) for b, q in cores])
    hT_full = [np.concatenate([np.asarray(r[b * 4 + q]["hTn"]) for q in range(4)], axis=1) for b in range(NB_)]
    x_cur = [np.ascontiguousarray(x[b, q * TQ:(q + 1) * TQ]) for b, q in cores]
    ncA = build_A(T)
    perm = wout_row_perm()
    out = None
    for l in range(DEPTH):
        maps = []
        for b, hg in cores:
            fm, tm = fm_cols(hg)
            maps.append(dict(hT=hT_full[b], wfm=np.ascontiguousarray(inp["w_in"][l][:, fm]), wtm=np.ascontiguousarray(inp["w_in"][l][:, tm]),
                             pvec=host_pvec(inp, l, hg), cos2=consts["cos2"], sin2=consts["sin2"], maskS=consts["maskS"], **common))
        rA = _run(ncA, maps)
        final = (l == DEPTH - 1)
        ncB = build_B(TQ, final)
        maps = []
        wout_p = np.ascontiguousarray(inp["w_out"][l][perm])
        pvB = host_pvecB(inp, l)
        for ci, (b, q) in enumerate(cores):
            yT_own = np.concatenate([np.asarray(rA[b * 4 + hg]["yT"])[:, q * TQ:(q + 1) * TQ] for hg in range(4)], axis=0)
            m = dict(x=x_cur[ci], yT=np.ascontiguousarray(yT_own), wout=wout_p, pvB=pvB, wmq=inp["w_mq"][l], wmk=inp["w_mk"][l],
                     wmv=inp["w_mv"][l], wmo=inp["w_mo"][l], mem=np.ascontiguousarray(inp["mem"][b]), **common)
            if final:
                m["fnw"] = np.ascontiguousarray(np.broadcast_to(inp["final_norm_w"][None, :], (128, D)))
            maps.append(m)
        rB = _run(ncB, maps)
        if final:
            out = np.stack([np.concatenate([np.asarray(rB[b * 4 + q]["out"]) for q in range(4)], axis=0) for b in range(NB_)], axis=0)
        else:
            x_cur = [np.asarray(rB[ci]["xout"]) for ci in range(8)]
            hT_full = [np.concatenate([np.asarray(rB[b * 4 + q]["hTn"]) for q in range(4)], axis=1) for b in range(NB_)]
    return out.astype(np.float32)


def _sub(tile_, ap, name):
    t = Tile(ap, name)
    return t


def build_fused(T, depth=DEPTH):
    nc = bass.Bass("TRN2", target_bir_lowering=False)
    with ExitStack() as st:
        P = Prog(nc, st)
        C = alloc_common(P, T)
        C.TPb = Tile(C.MS[1][:].bitcast(BF16), "TPb")
        C.TPb.r = C.MS[1].r
        x_d = P.dram("x", [T, D], F32, "ExternalInput")
        mem_d = P.dram("mem", [MEMT, D], F32, "ExternalInput")
        wfm_d = P.dram("wfm", [depth * 4 * D, NFM], F32, "ExternalInput")
        wtm_d = P.dram("wtm", [depth * 4 * D, NTM], F32, "ExternalInput")
        pvec_d = P.dram("pvec", [depth * 4 * 128, NPV], F32, "ExternalInput")
        wout_d = P.dram("wout", [depth * 2048, D], F32, "ExternalInput")
        pvB_d = P.dram("pvB", [depth * 128, NPVB], F32, "ExternalInput")
        wm_d = {n: P.dram(n, [depth * D, D], F32, "ExternalInput") for n in ("wmq", "wmk", "wmv", "wmo")}
        cos_d = P.dram("cos2", [128, T], F32, "ExternalInput")
        sin_d = P.dram("sin2", [128, T], F32, "ExternalInput")
        maskS_d = P.dram("maskS", [128, 256], F32, "ExternalInput")
        fnw_d = P.dram("fnw", [128, D], F32, "ExternalInput")
        out_d = P.dram("out", [T, D], F32, "ExternalOutput")
        hT_s = P.dram("hT_s", [D, T], BF16, "Internal")
        yT_s = P.dram("yT_s", [2048, T], BF16, "Internal")
        x1_s = P.dram("x1_s", [T, D], F32, "Internal")
        outer = P.stack

        def phase(fn):
            with ExitStack() as ph:
                P.stack = ph
                fn()
                P.barrier()
            P.stack = outer

        P.barrier()

        def phN():
            xt = P.sbuf("Nxt", [128, 4, 1024], F32)
            sq = P.sbuf("Nsq", [128, 1024], F32)
            ss = P.sbuf("Nss", [128, 1], F32)
            hb = P.sbuf("Nhb", [128, 1024], BF16)
            hTn = P.sbuf("NhTn", [128, 8, TT], BF16)
            x_v = x_d.t.rearrange("(n p) d -> p n d", p=128)
            for tb in range(T // TT):
                sl = slice(tb * TT, (tb + 1) * TT)
                P.dma("sp", lambda e: e.dma_start(out=xt[:], in_=x_v[:, tb * 4:(tb + 1) * 4, :]), writes=[xt])
                for i in range(4):
                    emit_norm_T(P, C, xt[:, i, :], xt, hTn, i * 128, sq, ss, hb)
                P.dma("pool", lambda e: e.dma_start(out=hT_s.t.rearrange("(k p) t -> p k t", p=128)[:, :, sl], in_=hTn[:]), reads=[hTn])
        phase(phN)
        for l in range(depth):
            for hg in range(4):
                i_ = l * 4 + hg
                phase(lambda: emit_phaseA(P, C, hT_s, Tile(wfm_d[i_ * D:(i_ + 1) * D, :], "wfm_s"), Tile(wtm_d[i_ * D:(i_ + 1) * D, :], "wtm_s"),
                                          Tile(pvec_d[i_ * 128:(i_ + 1) * 128, :], "pv_s"), cos_d, sin_d, maskS_d,
                                          Tile(yT_s[hg * 512:(hg + 1) * 512, :], "yT_sub"), tag=f"A{l}{hg}"))
            final = (l == depth - 1)
            wsl = lambda n: Tile(wm_d[n][l * D:(l + 1) * D, :], n + "_s")
            phase(lambda: emit_phaseB(P, C, T, x_d if l == 0 else x1_s, yT_s, Tile(wout_d[l * 2048:(l + 1) * 2048, :], "wo_s"),
                                      Tile(pvB_d[l * 128:(l + 1) * 128, :], "pvB_s"), wsl("wmq"), wsl("wmk"), wsl("wmv"), wsl("wmo"), mem_d,
                                      None if final else x1_s, None if final else hT_s, out_d if final else None, fnw_d if final else None, final, tag=f"B{l}"))
        stats = P.emit()
        print("fused ops", len(P.ops), stats, "sems", P.n_sems)
    return nc


def fused_inputs(inp, T, depth=DEPTH):
    consts = host_consts(T)
    perm = wout_row_perm()
    cols = [fm_cols(hg) for hg in range(4)]
    shared = dict(
        wfm=np.concatenate([inp["w_in"][l][:, cols[hg][0]] for l in range(depth) for hg in range(4)], axis=0),
        wtm=np.concatenate([inp["w_in"][l][:, cols[hg][1]] for l in range(depth) for hg in range(4)], axis=0),
        pvec=np.concatenate([host_pvec(inp, l, hg) for l in range(depth) for hg in range(4)], axis=0),
        wout=np.concatenate([inp["w_out"][l][perm] for l in range(depth)], axis=0),
        pvB=np.concatenate([host_pvecB(inp, l) for l in range(depth)], axis=0),
        wmq=np.concatenate([inp["w_mq"][l] for l in range(depth)], axis=0),
        wmk=np.concatenate([inp["w_mk"][l] for l in range(depth)], axis=0),
        wmv=np.concatenate([inp["w_mv"][l] for l in range(depth)], axis=0),
        wmo=np.concatenate([inp["w_mo"][l] for l in range(depth)], axis=0),
        cos2=consts["cos2"], sin2=consts["sin2"], maskS=consts["maskS"], ident=consts["ident"], triu=consts["triu"],
        fnw=np.ascontiguousarray(np.broadcast_to(inp["final_norm_w"][None, :], (128, D))),
    )
    return shared


def kernel_fused(**inputs):
    inp = {k: np.asarray(v) for k, v in inputs.items()}
    T = inp["x"].shape[1]
    shared = fused_inputs(inp, T)
    nc = build_fused(T)
    maps = []
    for c in range(8):
        b = c // 4
        maps.append(dict(x=np.ascontiguousarray(inp["x"][b]), mem=np.ascontiguousarray(inp["mem"][b]), **shared))
    r = _run(nc, maps)
    return np.stack([np.asarray(r[4 * b]["out"]) for b in range(NB_)], axis=0).astype(np.float32)


def kernel(**inputs):
    return kernel_fused(**inputs)
```

```python
import numpy as np
import ml_dtypes
import concourse.bass as bass
import concourse.mybir as mybir
from concourse.bass_utils import run_bass_kernel_spmd
from contextlib import ExitStack
import types
import os

F32 = mybir.dt.float32
BF16 = mybir.dt.bfloat16
AF = mybir.ActivationFunctionType
ALU = mybir.AluOpType
AX = mybir.AxisListType

SEM_EPOCH = 30000
NO_SELF_SYNC = tuple(os.environ.get("MK_NOSELF", "pe").split(","))
SKIP_SAME_ENGINE_WAW = os.environ.get("MK_WAW", "1") == "1"
EMBED_WAIT = os.environ.get("MK_EMBED", "1") == "1"
SAME_ENGINE_SYNC = os.environ.get("MK_SES", "1") == "1"


class Res:
    __slots__ = ("name", "last_w", "readers", "excl")

    def __init__(self, name):
        self.name = name
        self.last_w = None
        self.readers = []
        self.excl = False


class Tile:
    def __init__(self, t, name):
        self.t = t
        self.r = Res(name)

    def __getitem__(self, idx):
        return self.t[idx]


def _res(x):
    return x.r if isinstance(x, Tile) else x


def _freeze(fn):
    if fn.__closure__ is None:
        return fn
    cells = []
    for c in fn.__closure__:
        try:
            cells.append(types.CellType(c.cell_contents))
        except ValueError:
            cells.append(c)
    return types.FunctionType(fn.__code__, fn.__globals__, fn.__name__, fn.__defaults__, tuple(cells))


class Op:
    __slots__ = ("idx", "eng", "fn", "deps", "is_dma", "sig", "dma_sem", "dma_val", "ndma", "needs_sig", "dma_phys")


class Prog:
    def __init__(self, nc, stack):
        self.nc = nc
        self.stack = stack
        self.ops = []
        self.dma_sems = {}
        self.phys = []
        self.free_phys = {}

    def sbuf(self, name, shape, dtype):
        t = self.stack.enter_context(self.nc.sbuf_tensor(name, list(shape), dtype))
        return Tile(t, name)

    def psum(self, name, shape, dtype):
        t = self.stack.enter_context(self.nc.psum_tensor(name, list(shape), dtype))
        tl = Tile(t, name)
        tl.r.excl = True
        return tl

    def dram(self, name, shape, dtype, kind):
        t = self.nc.dram_tensor(name, list(shape), dtype, kind=kind)
        return Tile(t.ap(), name)

    def _record(self, eng, fn, reads, writes, is_dma=False, ndma=1, sem_key=None):
        op = Op()
        op.idx = len(self.ops)
        op.eng = eng
        op.fn = fn
        op.is_dma = is_dma
        op.ndma = ndma
        op.sig = None
        op.needs_sig = False
        op.dma_sem = None
        op.dma_val = None
        deps = set()
        waw = set()
        reads = [_res(r) for r in reads]
        writes = [_res(w) for w in writes]
        for r in reads:
            if r.last_w is not None:
                deps.add(r.last_w)
            if r.excl:
                for rd in r.readers:
                    if self.ops[rd].eng != eng:
                        deps.add(rd)
        for w in writes:
            if w.last_w is not None:
                if SKIP_SAME_ENGINE_WAW and w.last_w not in deps and self.ops[w.last_w].eng == eng and not is_dma and not self.ops[w.last_w].is_dma:
                    waw.add(w.last_w)
                else:
                    deps.add(w.last_w)
            for rd in w.readers:
                deps.add(rd)
        waw -= deps
        deps.discard(op.idx)
        op.deps = sorted(deps)
        for r in reads:
            r.readers.append(op.idx)
        for w in writes:
            w.last_w = op.idx
            w.readers = []
        if is_dma:
            key = sem_key if sem_key is not None else (writes[0] if writes else reads[0])
            key = (_res(key), eng)
            if key not in self.dma_sems:
                fl = self.free_phys.setdefault(eng, [])
                if fl:
                    pi = fl.pop()
                else:
                    pi = len(self.phys)
                    self.phys.append([None, 0, eng])
                self.dma_sems[key] = [pi, self.phys[pi][1]]
            ent = self.dma_sems[key]
            ent[1] += 16 * ndma
            self.phys[ent[0]][1] = ent[1]
            op.dma_sem = key
            op.dma_val = ent[1]
            op.dma_phys = ent[0]
        self.ops.append(op)
        return op

    def op(self, eng, fn, reads=(), writes=()):
        return self._record(eng, _freeze(fn), reads, writes)

    def barrier(self):
        last = {}
        for o in self.ops:
            if o.is_dma:
                last[("dma", o.dma_sem)] = o.idx
            else:
                last[o.eng] = o.idx
        extra = sorted(set(last.values()))
        for eng in ("sp", "pe", "act", "dve", "pool"):
            o = self._record(eng, lambda e: e.nop(), [], [])
            o.deps = sorted(set(o.deps) | set(extra))
        self.retired = getattr(self, "retired", {})
        for key, ent in self.dma_sems.items():
            self.retired[key] = ent
            self.free_phys.setdefault(self.phys[ent[0]][2], []).append(ent[0])
        self.dma_sems = {}

    def dma(self, q, fns, reads=(), writes=(), sem_key=None):
        if callable(fns):
            fns = [fns]
        fns = [_freeze(f) for f in fns]
        return self._record(q, fns, reads, writes, is_dma=True, ndma=len(fns), sem_key=sem_key)

    def emit(self):
        nc = self.nc
        ops = self.ops
        for op in ops:
            for d in op.deps:
                dop = ops[d]
                if dop.is_dma:
                    continue
                if dop.eng == op.eng and (dop.eng in NO_SELF_SYNC or not SAME_ENGINE_SYNC) and not op.is_dma:
                    continue
                dop.needs_sig = True
        counters = {}
        for op in ops:
            if op.is_dma or not op.needs_sig:
                continue
            c = counters.get(op.eng, 0) + 1
            counters[op.eng] = c
            op.sig = c
        eng_sems = {}
        for eng, c in counters.items():
            n_ep = (c + SEM_EPOCH - 1) // SEM_EPOCH
            eng_sems[eng] = [self.stack.enter_context(nc.semaphore(f"s_{eng}_{i}")) for i in range(n_ep)]
        for i, ph in enumerate(self.phys):
            ph[0] = self.stack.enter_context(nc.semaphore(f"d{i}"))
        self.n_sems = sum(len(v) for v in eng_sems.values()) + len(self.phys)
        self.max_dma_total = max([ph[1] for ph in self.phys] + [0])
        allk = dict(getattr(self, "retired", {}))
        allk.update(self.dma_sems)

        def dsem(key):
            return self.phys[allk[key][0]][0]

        def sem_of(eng, sig):
            ep = (sig - 1) // SEM_EPOCH
            return eng_sems[eng][ep], sig - ep * SEM_EPOCH

        per_eng = {}
        for op in ops:
            per_eng.setdefault(op.eng, []).append(op)
        handles = {"pe": "tensor", "act": "scalar", "dve": "vector", "pool": "gpsimd", "sp": "sync"}
        final_dma = [(ph[0], ph[1]) for ph in self.phys]
        stats = {"waits": 0, "instr": 0}

        def run_engine(engname, e):
            seen_eng = {}
            seen_dma = {}
            for op in per_eng.get(engname, []):
                need_eng = {}
                need_dma = {}
                for d in op.deps:
                    dop = ops[d]
                    if dop.is_dma:
                        k = dop.dma_phys
                        if dop.dma_val > need_dma.get(k, 0):
                            need_dma[k] = dop.dma_val
                    else:
                        if dop.sig is None:
                            continue
                        if dop.eng == engname and not op.is_dma and (engname in NO_SELF_SYNC or not SAME_ENGINE_SYNC):
                            continue
                        if dop.sig > need_eng.get(dop.eng, 0):
                            need_eng[dop.eng] = dop.sig
                wl = []
                for se, sig in need_eng.items():
                    if seen_eng.get(se, 0) >= sig:
                        continue
                    seen_eng[se] = sig
                    wl.append(sem_of(se, sig))
                for k, val in need_dma.items():
                    if seen_dma.get(k, 0) >= val:
                        continue
                    seen_dma[k] = val
                    wl.append((self.phys[k][0], val))
                emb = None
                if wl and EMBED_WAIT and not op.is_dma:
                    emb = wl.pop()
                for s_, v_ in wl:
                    e.wait_ge(s_, v_)
                    stats["waits"] += 1
                if op.is_dma:
                    s = self.phys[op.dma_phys][0]
                    for f in op.fn:
                        f(e).then_inc(s, 16)
                        stats["instr"] += 1
                else:
                    ins = op.fn(e)
                    stats["instr"] += 1
                    if emb is not None:
                        ins._wait_ge(emb[0], emb[1])
                    if op.sig is not None:
                        s, _ = sem_of(op.eng, op.sig)
                        ins.then_inc(s, 1)
            if engname == "sp":
                for s, v in final_dma:
                    e.wait_ge(s, v)

        with nc.Block() as block:
            for engname in ("sp", "pe", "act", "dve", "pool"):
                if engname not in per_eng and engname != "sp":
                    continue
                deco = getattr(block, handles[engname])

                def body(e, engname=engname):
                    run_engine(engname, e)

                deco(body)
        self.stats = stats
        return stats


D = 1024
SEQ = 8192
NB_ = 2
DEPTH = 2
HD = 64
MEMT = 256
EPS = 1e-6
O_QA, O_KA, O_VA, O_GA = 0, 512, 640, 768
O_QB, O_KB, O_VB, O_FB, O_GB = 1280, 1792, 2304, 2816, 2824
O_Z, O_XBC, O_DT = 3336, 4360, 5896
FM_GROUPS = [("qa", 128), ("qas", 128), ("ka", 128), ("kas", 128), ("ga0", 64), ("ga1", 64),
             ("qb0", 64), ("qb1", 64), ("kb0", 64), ("kb1", 64), ("gb0", 64), ("gb1", 64), ("fb", 2),
             ("z0", 128), ("z1", 128), ("x0", 128), ("x1", 128), ("Bm", 128), ("Cm", 128)]
FM_OFF = {}
_o = 0
for _n, _m in FM_GROUPS:
    FM_OFF[_n] = (_o, _m)
    _o += _m
NFM = _o
NTM = 196
NPV = 72
TT = 512
FOX_D = int(os.environ.get("MK_FOXD", "3"))


def fm_cols(hg):
    h0, h1 = 2 * hg, 2 * hg + 1
    kv = hg // 2
    g = hg // 2
    r = np.arange(64)
    sw = np.concatenate([np.arange(32, 64), np.arange(0, 32)])
    a128 = np.arange(128)
    cols = {
        "qa": np.concatenate([O_QA + h0 * 64 + r, O_QA + h1 * 64 + r]),
        "qas": np.concatenate([O_QA + h0 * 64 + sw, O_QA + h1 * 64 + sw]),
        "ka": np.concatenate([O_KA + kv * 64 + r, O_KA + kv * 64 + r]),
        "kas": np.concatenate([O_KA + kv * 64 + sw, O_KA + kv * 64 + sw]),
        "ga0": O_GA + h0 * 64 + r, "ga1": O_GA + h1 * 64 + r,
        "qb0": O_QB + h0 * 64 + r, "qb1": O_QB + h1 * 64 + r,
        "kb0": O_KB + h0 * 64 + r, "kb1": O_KB + h1 * 64 + r,
        "gb0": O_GB + h0 * 64 + r, "gb1": O_GB + h1 * 64 + r,
        "fb": np.array([O_FB + h0, O_FB + h1]),
        "z0": O_Z + 256 * hg + a128, "z1": O_Z + 256 * hg + 128 + a128,
        "x0": O_XBC + 256 * hg + a128, "x1": O_XBC + 256 * hg + 128 + a128,
        "Bm": O_XBC + 1024 + 128 * g + a128, "Cm": O_XBC + 1280 + 128 * g + a128,
    }
    fm = np.concatenate([cols[n] for n, _ in FM_GROUPS])
    tm = np.concatenate([O_VB + h0 * 64 + r, O_VB + h1 * 64 + r, O_VA + kv * 64 + r, O_DT + 4 * hg + np.arange(4)])
    return fm, tm


def host_pvec(inp, l, hg):
    pv = np.zeros((128, NPV), np.float32)
    pv[:, 0:8] = inp["norm_mix_w"][l].reshape(8, 128).T
    g = hg // 2
    a128 = np.arange(128)
    chans = [256 * hg + a128, 256 * hg + 128 + a128, 1024 + 128 * g + a128, 1280 + 128 * g + a128]
    for c, ch in enumerate(chans):
        pv[:, 8 + 4 * c:12 + 4 * c] = inp["conv_w"][l][:, ch].T
        pv[:, 24 + c] = inp["conv_b"][l][ch]
    for j in range(2):
        pv[:, 28 + j] = np.repeat(inp["d_skip"][l][4 * hg + 2 * j:4 * hg + 2 * j + 2], 64)
    pv[:, 30:46] = np.tile(inp["dt_bias"][l][4 * hg:4 * hg + 4], 4)[None, :]
    pv[:, 46:62] = np.tile(inp["a_log"][l][4 * hg:4 * hg + 4], 4)[None, :]
    pv[0:2, 62] = inp["b_forget"][l][2 * hg:2 * hg + 2]
    pv[:, 63] = inp["swa_sinks"][l][2 * hg]
    pv[:, 64] = inp["swa_sinks"][l][2 * hg + 1]
    return pv


def host_consts(T):
    c = {}
    c["ident"] = np.eye(128, dtype=np.float32)
    s = np.arange(128)[:, None]
    t = np.arange(128)[None, :]
    triu = (s <= t).astype(np.float32)
    c["triu"] = triu
    c["maskS"] = (np.concatenate([triu, 1.0 - triu], axis=1) * np.float32(-240000.0)).astype(np.float32)
    pos = np.arange(T, dtype=np.float32)
    inv = (1.0 / (np.float32(10000.0) ** (np.arange(0, HD, 2, dtype=np.float32) / np.float32(HD)))).astype(np.float32)
    ang = (pos[:, None] * inv[None, :]).astype(np.float32)
    cosv = np.cos(ang).astype(np.float32).T
    sinv = np.sin(ang).astype(np.float32).T
    c["cos2"] = np.concatenate([cosv, cosv, cosv, cosv], axis=0)
    c["sin2"] = np.concatenate([-sinv, sinv, -sinv, sinv], axis=0)
    return c


class Ctx:
    pass


def alloc_common(P, T):
    C = Ctx()
    C.T = T
    C.ident_d = P.dram("ident", [128, 128], F32, "ExternalInput")
    C.triu_d = P.dram("triu", [128, 128], F32, "ExternalInput")
    C.identf = P.sbuf("identf", [128, 128], F32)
    C.identb = P.sbuf("identb", [128, 128], BF16)
    C.triuf = P.sbuf("triuf", [128, 128], F32)
    C.triub = P.sbuf("triub", [128, 128], BF16)
    C.onesf = P.sbuf("onesf", [128, 512], F32)
    C.onesb = P.sbuf("onesb", [128, 128], BF16)
    P.dma("sp", lambda e: e.dma_start(out=C.identf[:], in_=C.ident_d[:]), writes=[C.identf])
    P.dma("sp", lambda e: e.dma_start(out=C.triuf[:], in_=C.triu_d[:]), writes=[C.triuf])
    P.op("dve", lambda e: e.tensor_copy(out=C.identb[:], in_=C.identf[:]), reads=[C.identf], writes=[C.identb])
    P.op("dve", lambda e: e.tensor_copy(out=C.triub[:], in_=C.triuf[:]), reads=[C.triuf], writes=[C.triub])
    P.op("pool", lambda e: e.memset(C.onesf[:], 1.0), writes=[C.onesf])
    P.op("pool", lambda e: e.memset(C.onesb[:], 1.0), writes=[C.onesb])
    C.PJ = [P.psum(f"PJ{i}", [128, 512], F32) for i in range(2)]
    C.SC = [P.psum(f"SC{i}", [128, 512], F32) for i in range(2)]
    C.OT = [P.psum(f"OT{i}", [128, 512], F32) for i in range(2)]
    C.MS = [P.psum(f"MS{i}", [128, 512], F32) for i in range(2)]
    return C


import os
PARTS = set(os.environ.get("MK_PARTS", "swa,fox,ssd,f").split(","))


def emit_phaseA(P, C, hT_d, wfm_d, wtm_d, pvec_d, cos_d, sin_d, maskS_d, yT_d, tag="A"):
    T = C.T
    NT = T // TT
    NBLK = T // 128
    PJ, SC, OT = C.PJ, C.SC, C.OT
    MS0, MS1 = C.MS
    CBp = Tile(MS0[:, 0:128], tag + "CBp")
    XTp = Tile(MS0[:, 128:384], tag + "XTp")
    ACp = Tile(MS0[:, 384:400], tag + "ACp")
    STp = Tile(MS1[:, 0:256], tag + "STp")
    BTp = Tile(MS1[:, 256:320].bitcast(BF16), tag + "BTp")
    CBp.r = XTp.r = ACp.r = MS0.r
    XTs = [Tile(PJ[0][:, 0:256], tag + "XT0"), Tile(PJ[1][:, 0:256], tag + "XT1")]
    XTs[0].r = PJ[0].r
    XTs[1].r = PJ[1].r
    STp.r = BTp.r = MS1.r

    pv = P.sbuf(tag + "pv", [128, NPV], F32)
    P.dma("sp", lambda e: e.dma_start(out=pv[:], in_=pvec_d[:]), writes=[pv])
    aneg = P.sbuf(tag + "aneg", [128, 16], F32)
    P.op("act", lambda e: e.activation(out=aneg[:], in_=pv[:, 46:62], func=AF.Exp), reads=[pv], writes=[aneg])
    P.op("dve", lambda e: e.tensor_scalar(out=aneg[:], in0=aneg[:], scalar1=-1.0, scalar2=None, op0=ALU.mult), reads=[aneg], writes=[aneg])
    negb = P.sbuf(tag + "negb", [2, 1], F32)
    P.op("dve", lambda e: e.tensor_scalar(out=negb[:], in0=pv[0:2, 62:63], scalar1=-1.0, scalar2=None, op0=ALU.mult), reads=[pv], writes=[negb])
    esink = P.sbuf(tag + "esink", [128, 2], F32)
    P.op("act", lambda e: e.activation(out=esink[:], in_=pv[:, 63:65], func=AF.Exp), reads=[pv], writes=[esink])
    maskS = P.sbuf(tag + "maskS", [128, 256], BF16)
    maskSf = P.sbuf(tag + "maskSf", [128, 256], F32)
    P.dma("sp", lambda e: e.dma_start(out=maskSf[:], in_=maskS_d[:]), writes=[maskSf])
    P.op("dve", lambda e: e.tensor_copy(out=maskS[:], in_=maskSf[:]), reads=[maskSf], writes=[maskS])

    Wb = P.sbuf(tag + "Wb", [128, 8, NFM], BF16)
    Wt = P.sbuf(tag + "Wt", [128, 8, NTM], BF16)
    stg = [P.sbuf(tag + "stg", [128, NFM], F32)] * 2
    wfm_v = wfm_d.t.rearrange("(k p) n -> p k n", p=128)
    for k in range(8):
        s_ = stg[k % 2]
        P.dma("sp", lambda e, k=k, s_=s_: e.dma_start(out=s_[:], in_=wfm_v[:, k, :]), writes=[s_])
        if k % 2 == 0:
            P.op("dve", lambda e, k=k, s_=s_: e.tensor_scalar(out=Wb[:, k, :], in0=s_[:], scalar1=pv[:, k:k + 1], scalar2=None, op0=ALU.mult),
                 reads=[s_, pv], writes=[Wb])
        else:
            P.op("act", lambda e, k=k, s_=s_: e.activation(out=Wb[:, k, :], in_=s_[:], func=AF.Copy, scale=pv[:, k:k + 1]),
                 reads=[s_, pv], writes=[Wb])
    s_ = stg[0]
    P.dma("sp", lambda e: e.dma_start(out=s_[:, 0:8 * NTM].rearrange("p (k n) -> p k n", k=8), in_=wtm_d.t.rearrange("(k p) n -> p k n", p=128)), writes=[s_])
    for k in range(8):
        P.op("dve", lambda e, k=k: e.tensor_scalar(out=Wt[:, k, :], in0=s_[:, k * NTM:(k + 1) * NTM], scalar1=pv[:, k:k + 1], scalar2=None, op0=ALU.mult),
             reads=[s_, pv], writes=[Wt])

    KA = [P.sbuf(tag + f"KA{h}", [70, T], BF16) for h in range(2)]
    KAq = [[Res(f"KAq{h}_{t}") for t in range(NT)] for h in range(2)]
    KAc = [[Res(f"KAc{h}_{t}") for t in range(NT)] for h in range(2)]
    VF = P.sbuf(tag + "VF", [128, NBLK, 2, 72], BF16)
    VFr = [Res(f"VF_{t}") for t in range(NT)]
    KR = P.sbuf(tag + "KR", [128, T], BF16)
    KRr = [Res(f"KR_{t}") for t in range(NT)]
    VS = P.sbuf(tag + "VS", [128, NBLK, 72], BF16)
    VSr = [Res(f"VS_{t}") for t in range(NT)]
    QA = [P.sbuf(tag + f"QA{h}", [70, TT], BF16) for h in range(2)]
    QAc = [Res(f"QAc{h}") for h in range(2)]
    ST = P.sbuf(tag + "ST", [128, 4, 64], F32)
    STb = P.sbuf(tag + "STb", [128, 4, 64], BF16)
    ub = [P.sbuf(tag + f"ub{c}", [128, 3 + TT], F32) for c in range(4)]
    carry = P.sbuf(tag + "carry", [2, 1], F32)

    initr = Res("init")
    for h in range(2):
        P.op("pool", lambda e, h=h: e.memset(KA[h][64:70, :], 1.0), writes=[KAc[h][t] for t in range(NT)])
        P.op("pool", lambda e, h=h: e.memset(QA[h][64:70, :], 1.0), writes=[QAc[h]])
    P.op("pool", lambda e: e.memset(VF[:], 1.0), writes=VFr)
    P.op("pool", lambda e: e.memset(VS[:], 1.0), writes=VSr)
    P.op("pool", lambda e: e.memset(ST[:], 0.0), writes=[ST])
    P.op("pool", lambda e: e.memset(STb[:], 0.0), writes=[STb])
    for c in range(4):
        P.op("pool", lambda e, c=c: e.memset(ub[c][:, 0:3], 0.0), writes=[ub[c]])
    P.op("pool", lambda e: e.memset(carry[:], 0.0), writes=[carry])

    hTt = [P.sbuf(tag + f"hTt{i}", [128, 8, TT], BF16) for i in range(2)]
    cosT = [P.sbuf(tag + "cosT", [128, TT], F32)] * 2
    sinT = [P.sbuf(tag + "sinT", [128, TT], F32)] * 2
    QR = P.sbuf(tag + "QR", [128, TT], BF16)
    sg = {n: P.sbuf(tag + "sg_" + n, [m, TT], F32) for n, m in (("ga0", 64), ("ga1", 64), ("gb0", 64), ("gb1", 64), ("z0", 128), ("z1", 128))}
    xc = [P.sbuf(tag + f"xc{j}", [128, TT], F32) for j in range(2)]
    acc = P.sbuf(tag + "acc", [128, TT], F32)
    BcT = P.sbuf(tag + "BcT", [128, TT], BF16)
    CcT = P.sbuf(tag + "CcT", [128, TT], BF16)
    fe = P.sbuf(tag + "fe", [2, TT], F32)
    cc = P.sbuf(tag + "cc", [2, TT], F32)
    cr = fe
    csp = [P.sbuf(tag + f"csp{i}", [2, TT], BF16) for i in range(3)]
    cspn = [P.sbuf(tag + "cspn", [2, TT], BF16)] * 3
    dtraw = P.sbuf(tag + "dtraw", [128, 16], F32)
    dtt = P.sbuf(tag + "dtt", [128, 16], F32)
    dA = P.sbuf(tag + "dA", [128, 16], F32)
    acsc = P.sbuf(tag + "acsc", [128, 16], F32)
    PT = [P.sbuf(tag + f"PT{i}", [128, TT], BF16) for i in range(5)]
    denr = P.sbuf(tag + "denr", [65, TT], F32)
    rbc = P.sbuf(tag + "rbc", [64, TT], F32)
    yo = [P.sbuf(tag + f"yo{i}", [64, TT], BF16) for i in range(4)]
    vo = [P.sbuf(tag + f"vo{j}", [128, TT], BF16) for j in range(2)]
    Dabs = P.sbuf(tag + "Dabs", [128, 4, 128], F32)
    Ee = Dabs
    eA = P.sbuf(tag + "eA", [128, 4, 128], F32)
    CBm = P.sbuf(tag + "CBm", [128, 128], F32)
    MT2 = [P.sbuf(tag + f"MT{i}", [128, 4, 128], BF16) for i in range(2)]
    Cs2 = [P.sbuf(tag + f"Cs{i}", [128, 4, 128], BF16) for i in range(2)]
    Bc2 = [P.sbuf(tag + f"Bc{i}", [128, 128], BF16) for i in range(2)]
    cd2 = [P.sbuf(tag + f"cd{i}", [128, 4], F32) for i in range(2)]
    d4 = P.sbuf(tag + "d4", [128, 4], F32)
    dtd = P.sbuf(tag + "dtd", [128, 4], F32)
    xs2 = [P.sbuf(tag + f"xs{i}", [128, 4, 64], BF16) for i in range(2)]
    xsd2 = [P.sbuf(tag + f"xsd{i}", [128, 4, 64], BF16) for i in range(2)]
    yf = P.sbuf(tag + "yf", [128, TT], F32)
    t1, t2 = yf, acc

    hT_v = hT_d.t.rearrange("(k p) t -> p k t", p=128)
    PJB = [PJ[0], PJ[1], C.MS[0], C.MS[1]] if os.environ.get("MK_PJ4", "1") == "1" else [PJ[0], PJ[1]]
    pj_i = [0]
    pt_i = [0]

    def load_tile(tt):
        b_ = tt % 2
        sl = slice(tt * TT, (tt + 1) * TT)
        P.dma("sp", lambda e: e.dma_start(out=hTt[b_][:], in_=hT_v[:, :, sl]), writes=[hTt[b_]])

    def load_tables(tt):
        sl = slice(tt * TT, (tt + 1) * TT)
        P.dma("sp", lambda e: e.dma_start(out=cosT[0][:], in_=cos_d[:, sl]), writes=[cosT[0]])
        P.dma("sp", lambda e: e.dma_start(out=sinT[0][:], in_=sin_d[:, sl]), writes=[sinT[0]])

    def proj_fm(tt, name):
        off, m = FM_OFF[name]
        bank = PJB[pj_i[0] % len(PJB)]
        pj_i[0] += 1
        h_ = hTt[tt % 2]
        for k in range(8):
            P.op("pe", lambda e, k=k: e.matmul(bank[0:m, :], lhsT=Wb[:, k, off:off + m], rhs=h_[:, k, :], start=(k == 0), stop=(k == 7)),
                 reads=[Wb, h_], writes=[bank])
        return bank, m

    def normalize(otb, sgt, sink_col, yout, gate_part):
        if sink_col is None:
            P.op("act", lambda e: e.activation(out=denr[64:65, :], in_=otb[64:65, :], func=AF.Ln), reads=[otb], writes=[denr])
        else:
            P.op("act", lambda e: e.activation(out=denr[64:65, :], in_=otb[64:65, :], func=AF.Ln, bias=esink[64:65, sink_col:sink_col + 1]),
                 reads=[otb, esink], writes=[denr])
        P.op("act", lambda e: e.activation(out=denr[64:65, :], in_=denr[64:65, :], func=AF.Exp, scale=-1.0), reads=[denr], writes=[denr])
        bcb = SC[0]
        P.op("pe", lambda e: e.matmul(bcb[0:64, :], lhsT=C.onesf[64:65, 0:64], rhs=denr[64:65, :], start=True, stop=True),
             reads=[C.onesf, denr], writes=[bcb])
        P.op("dve", lambda e: e.tensor_tensor(out=rbc[:], in0=bcb[0:64, :], in1=sgt[gate_part], op=ALU.mult), reads=[bcb, sgt], writes=[rbc])
        P.op("dve", lambda e: e.tensor_tensor(out=yout[:], in0=otb[0:64, :], in1=rbc[:], op=ALU.mult), reads=[otb, rbc], writes=[yout])

    for tt in range(NT):
        if tt == 0:
            load_tile(0)
            load_tables(0)
        if tt + 1 < NT:
            load_tile(tt + 1)
        b_ = tt % 2
        sl = slice(tt * TT, (tt + 1) * TT)
        cs_, sn_ = cosT[b_], sinT[b_]

        for (nm, nms, dst, dres) in (("qa", "qas", QR, QR), ("ka", "kas", KR, KRr[tt])):
            bank, _ = proj_fm(tt, nm)
            P.op("dve", lambda e, bank=bank: e.tensor_tensor(out=t1[:], in0=bank[:], in1=cs_[:], op=ALU.mult), reads=[bank, cs_], writes=[t1])
            bank2, _ = proj_fm(tt, nms)
            P.op("dve", lambda e, bank2=bank2: e.tensor_tensor(out=t2[:], in0=bank2[:], in1=sn_[:], op=ALU.mult), reads=[bank2, sn_], writes=[t2])
            if dst is QR:
                P.op("pool", lambda e: e.tensor_tensor(out=QR[:], in0=t1[:], in1=t2[:], op=ALU.add), reads=[t1, t2], writes=[QR])
            else:
                P.op("pool", lambda e: e.tensor_tensor(out=KR[:, sl], in0=t1[:], in1=t2[:], op=ALU.add), reads=[t1, t2], writes=[KRr[tt]])
        if tt + 1 < NT:
            load_tables(tt + 1)
        for nm in ("ga0", "ga1", "gb0", "gb1", "z0", "z1"):
            bank, m = proj_fm(tt, nm)
            P.op("act", lambda e, bank=bank, m=m, nm=nm: e.activation(out=sg[nm][:], in_=bank[0:m, :], func=AF.Silu), reads=[bank], writes=[sg[nm]])
        for h in range(2):
            bank, _ = proj_fm(tt, f"qb{h}")
            P.op("act", lambda e, bank=bank, h=h: e.activation(out=QA[h][0:64, :], in_=bank[0:64, :], func=AF.Copy, scale=0.125), reads=[bank], writes=[QA[h]])
            bank, _ = proj_fm(tt, f"kb{h}")
            P.op("act", lambda e, bank=bank, h=h: e.copy(out=KA[h][0:64, sl], in_=bank[0:64, :]), reads=[bank], writes=[KAq[h][tt]])
        bank, _ = proj_fm(tt, "fb")
        P.op("act", lambda e, bank=bank: e.activation(out=fe[:], in_=bank[0:2, :], func=AF.Exp, scale=-1.0, bias=negb[:]), reads=[bank, negb], writes=[fe])
        P.op("act", lambda e: e.activation(out=fe[:], in_=fe[:], func=AF.Ln, bias=1.0), reads=[fe], writes=[fe])
        P.op("dve", lambda e: e.tensor_tensor_scan(out=cc[:], data0=C.onesf[0:2, 0:TT], data1=fe[:], initial=carry[:], op0=ALU.mult, op1=ALU.subtract),
             reads=[fe, carry, C.onesf], writes=[cc])
        P.op("dve", lambda e: e.tensor_copy(out=carry[:], in_=cc[:, TT - 1:TT]), reads=[cc], writes=[carry])
        P.op("act", lambda e: e.copy(out=csp[0][:], in_=cc[:]), reads=[cc], writes=[csp[0]])
        P.op("dve", lambda e: e.tensor_tensor(out=cr[:], in0=cc[:], in1=csp[0][:], op=ALU.subtract), reads=[cc, csp[0]], writes=[cr])
        P.op("act", lambda e: e.copy(out=csp[1][:], in_=cr[:]), reads=[cr], writes=[csp[1]])
        P.op("dve", lambda e: e.tensor_tensor(out=cr[:], in0=cr[:], in1=csp[1][:], op=ALU.subtract), reads=[cr, csp[1]], writes=[cr])
        P.op("act", lambda e: e.copy(out=csp[2][:], in_=cr[:]), reads=[cr], writes=[csp[2]])
        for h in range(2):
            P.dma("sp", [lambda e, h=h, i=i: e.dma_start(out=QA[h][64 + i:65 + i, :], in_=csp[i][h:h + 1, :]) for i in range(3)],
                  reads=csp, writes=[QAc[h]])
        for i in range(3):
            P.op("act", lambda e, i=i: e.activation(out=cspn[i][:], in_=csp[i][:], func=AF.Copy, scale=-1.0), reads=[csp[i]], writes=[cspn[i]])
            P.dma("sp", [lambda e, h=h, i=i: e.dma_start(out=KA[h][67 + i:68 + i, sl], in_=cspn[i][h:h + 1, :]) for h in range(2)],
                  reads=[cspn[i]], writes=[KAc[0][tt], KAc[1][tt]])
        for c, nm in enumerate(("x0", "x1", "Bm", "Cm")):
            bank, _ = proj_fm(tt, nm)
            u = ub[c]
            P.op("act", lambda e, bank=bank, u=u: e.copy(out=u[:, 3:3 + TT], in_=bank[:]), reads=[bank], writes=[u])
        for i in range(4):
            blk = tt * 4 + i
            bank = PJB[pj_i[0] % len(PJB)]
            pj_i[0] += 1
            h_ = hTt[b_]
            for k in range(8):
                P.op("pe", lambda e, k=k, i=i, bank=bank: e.matmul(bank[:, 0:NTM], lhsT=h_[:, k, i * 128:(i + 1) * 128], rhs=Wt[:, k, :], start=(k == 0), stop=(k == 7)),
                     reads=[Wt, h_], writes=[bank])
            P.op("act", lambda e, blk=blk, bank=bank: e.copy(out=VF[:, blk, :, 0:64], in_=bank[:, 0:128].rearrange("p (h d) -> p h d", h=2)), reads=[bank], writes=[VFr[tt]])
            if os.environ.get("MK_DBG") == "1" and i == 2:
                P.op("dve", lambda e, blk=blk, bank=bank: e.tensor_copy(out=t1[:, 0:64], in_=bank[:, 128:192]), reads=[bank], writes=[VSr[tt]])
            elif os.environ.get("MK_DBG") == "2" and i == 2:
                P.op("dve", lambda e, blk=blk, bank=bank: e.tensor_copy(out=VS[:, blk, 0:64], in_=t2[:, 128:192]), reads=[bank], writes=[VSr[tt]])
            else:
                P.op("dve", lambda e, blk=blk, bank=bank: e.tensor_copy(out=VS[:, blk, 0:64], in_=bank[:, 128:192]), reads=[bank], writes=[VSr[tt]])
            P.op("dve", lambda e, i=i, bank=bank: e.tensor_copy(out=dtraw[:, 4 * i:4 * i + 4], in_=bank[:, 192:196]), reads=[bank], writes=[dtraw])

        accs = [acc, yf]
        for c in range(4):
            u = ub[c]
            ac = accs[c % 2]
            P.op("act", lambda e, u=u, c=c, ac=ac: e.activation(out=ac[:], in_=u[:, 3:3 + TT], func=AF.Identity, scale=pv[:, 8 + 4 * c + 3:8 + 4 * c + 4], bias=pv[:, 24 + c:25 + c]),
                 reads=[u, pv], writes=[ac])
            for k in range(3):
                P.op("dve", lambda e, u=u, c=c, k=k, ac=ac: e.scalar_tensor_tensor(out=ac[:], in0=u[:, k:k + TT], scalar=pv[:, 8 + 4 * c + k:8 + 4 * c + k + 1], in1=ac[:], op0=ALU.mult, op1=ALU.add),
                     reads=[u, pv, ac], writes=[ac])
            P.op("pool", lambda e, u=u: e.tensor_copy(out=u[:, 0:3], in_=u[:, TT:TT + 3]), reads=[u], writes=[u])
            dst = xc[c] if c < 2 else (BcT if c == 2 else CcT)
            P.op("act", lambda e, dst=dst, ac=ac: e.activation(out=dst[:], in_=ac[:], func=AF.Silu), reads=[ac], writes=[dst])

        if "swa" in PARTS:
            sitems = [(hh, i) for i in range(4) for hh in range(2)]
            sbanks = [SC[0], SC[1], PJ[0], PJ[1]]
            spts = {}

            def swa_s1(idx):
                hh, i = sitems[idx]
                pb = 64 * hh
                qb = tt * 4 + i
                scb = sbanks[idx % 4]
                pt = PT[pt_i[0] % len(PT)]
                pt_i[0] += 1
                spts[idx] = pt
                kres = [KRr[tt]] if i > 0 else ([KRr[tt], KRr[tt - 1]] if tt > 0 else [KRr[tt]])
                qcols = slice(i * 128, (i + 1) * 128)
                if qb > 0:
                    P.op("pe", lambda e: e.matmul(scb[:, 0:128], lhsT=KR[pb:pb + 64, (qb - 1) * 128:qb * 128], rhs=QR[pb:pb + 64, qcols], start=True, stop=False),
                         reads=kres + [QR], writes=[scb])
                    P.op("pe", lambda e: e.matmul(scb[:, 0:128], lhsT=C.identb[:], rhs=maskS[:, 0:128], start=False, stop=True),
                         reads=[C.identb, maskS], writes=[scb])
                P.op("pe", lambda e: e.matmul(scb[:, 128:256], lhsT=KR[pb:pb + 64, qb * 128:(qb + 1) * 128], rhs=QR[pb:pb + 64, qcols], start=True, stop=False),
                     reads=kres + [QR], writes=[scb])
                P.op("pe", lambda e: e.matmul(scb[:, 128:256], lhsT=C.identb[:], rhs=maskS[:, 128:256], start=False, stop=True),
                     reads=[C.identb, maskS], writes=[scb])
                lo = 0 if qb > 0 else 128
                P.op("act", lambda e: e.activation(out=pt[:, lo:256], in_=scb[:, lo:256], func=AF.Exp, scale=0.125), reads=[scb], writes=[pt])

            def swa_s2(idx):
                hh, i = sitems[idx]
                qb = tt * 4 + i
                pt = spts[idx]
                otb = OT[hh]
                vres = [VSr[tt]] if i > 0 else ([VSr[tt], VSr[tt - 1]] if tt > 0 else [VSr[tt]])
                qcols = slice(i * 128, (i + 1) * 128)
                if qb > 0:
                    P.op("pe", lambda e: e.matmul(otb[0:65, qcols], lhsT=VS[:, qb - 1, 0:65], rhs=pt[:, 0:128], start=True, stop=False),
                         reads=vres + [pt], writes=[otb])
                P.op("pe", lambda e: e.matmul(otb[0:65, qcols], lhsT=VS[:, qb, 0:65], rhs=pt[:, 128:256], start=(qb == 0), stop=True),
                     reads=vres + [pt], writes=[otb])

            SD = 3
            for idx in range(len(sitems) + SD):
                if idx < len(sitems):
                    swa_s1(idx)
                if idx >= SD:
                    swa_s2(idx - SD)
            for hh in range(2):
                normalize(OT[hh], sg[f"ga{hh}"], hh, yo[hh], slice(None))

        if "fox" in PARTS:
            nkb = 4 * tt + 4
            items = [(h, j) for j in range(nkb) for h in range(2)]
            fbanks = [SC[0], SC[1], PJ[0], PJ[1]]
            NPT = len(PT)

            def fox_s1(idx):
                h, j = items[idx]
                i = j - 4 * tt
                c0 = 0 if i <= 0 else i * 128
                cols = slice(c0, TT)
                scb = fbanks[idx % 4]
                pt = PT[idx % NPT]
                jt = j // 4
                P.op("pe", lambda e: e.matmul(scb[:, cols], lhsT=KA[h][0:70, j * 128:(j + 1) * 128], rhs=QA[h][0:70, cols], start=True, stop=(i < 0)),
                     reads=[KAq[h][jt], KAc[h][jt], QA[h], QAc[h]], writes=[scb])
                if i >= 0:
                    dcol = slice(i * 128, (i + 1) * 128)
                    P.op("pe", lambda e: e.matmul(scb[:, dcol], lhsT=C.identb[:], rhs=maskS[:, 128:256], start=False, stop=True),
                         reads=[C.identb, maskS], writes=[scb])
                P.op("act", lambda e: e.activation(out=pt[:, cols], in_=scb[:, cols], func=AF.Exp), reads=[scb], writes=[pt])

            def fox_s2(idx):
                h, j = items[idx]
                i = j - 4 * tt
                c0 = 0 if i <= 0 else i * 128
                cols = slice(c0, TT)
                pt = PT[idx % NPT]
                jt = j // 4
                otb = OT[h]
                P.op("pe", lambda e: e.matmul(otb[0:65, cols], lhsT=VF[:, j, h, 0:65], rhs=pt[:, cols], start=(j == 0), stop=(j == nkb - 1)),
                     reads=[VFr[jt], pt], writes=[otb])

            for idx in range(len(items) + FOX_D):
                if idx < len(items):
                    fox_s1(idx)
                if idx >= FOX_D:
                    fox_s2(idx - FOX_D)
            for h in range(2):
                normalize(OT[h], sg[f"gb{h}"], None, yo[2 + h], slice(None))

        P.op("dve", lambda e: e.tensor_tensor(out=dtt[:], in0=dtraw[:], in1=pv[:, 30:46], op=ALU.add), reads=[dtraw, pv], writes=[dtt])
        P.op("act", lambda e: e.activation(out=dtt[:], in_=dtt[:], func=AF.Exp), reads=[dtt], writes=[dtt])
        P.op("act", lambda e: e.activation(out=dtt[:], in_=dtt[:], func=AF.Ln, bias=1.0), reads=[dtt], writes=[dtt])
        P.op("dve", lambda e: e.tensor_tensor(out=dA[:], in0=dtt[:], in1=aneg[:], op=ALU.mult), reads=[dtt, aneg], writes=[dA])
        P.op("pe", lambda e: e.matmul(ACp[:], lhsT=C.triuf[:], rhs=dA[:], start=True, stop=True), reads=[C.triuf, dA], writes=[ACp])
        P.op("dve", lambda e: e.tensor_copy(out=acsc[:], in_=ACp[:]), reads=[ACp], writes=[acsc])
        def ssd_A(c):
            ccols = slice(c * 128, (c + 1) * 128)
            par = c % 2
            AB = SC[par]
            MT_, Cs_, xs_, xsd_, Bc_, cd_ = MT2[par], Cs2[par], xs2[par], xsd2[par], Bc2[par], cd2[par]
            for h in range(4):
                ch = 4 * c + h
                P.op("pe", lambda e, h=h, ch=ch: e.matmul(AB[:, h * 128:(h + 1) * 128], lhsT=dA[:, ch:ch + 1].to_broadcast([128, 128]), rhs=C.triuf[:], start=True, stop=True),
                     reads=[dA, C.triuf], writes=[AB])
            for h in range(4):
                ch = 4 * c + h
                P.op("dve", lambda e, h=h, ch=ch: e.tensor_scalar(out=Dabs[:, h, :], in0=AB[:, h * 128:(h + 1) * 128], scalar1=acsc[:, ch:ch + 1], scalar2=0.0, op0=ALU.subtract, op1=ALU.min),
                     reads=[AB, acsc], writes=[Dabs])
            P.op("act", lambda e: e.activation(out=Ee[:].rearrange("p h l -> p (h l)"), in_=Dabs[:].rearrange("p h l -> p (h l)"), func=AF.Exp), reads=[Dabs], writes=[Ee])
            P.op("act", lambda e: e.activation(out=eA[:].rearrange("p h l -> p (h l)"), in_=AB[:], func=AF.Exp), reads=[AB], writes=[eA])
            P.op("dve", lambda e: e.tensor_tensor(out=d4[:], in0=AB[:].rearrange("p (h l) -> p h l", h=4)[:, :, 127], in1=acsc[:, 4 * c:4 * c + 4], op=ALU.subtract),
                 reads=[AB, acsc], writes=[d4])
            P.op("act", lambda e: e.activation(out=d4[:], in_=d4[:], func=AF.Exp), reads=[d4], writes=[d4])
            P.op("dve", lambda e: e.tensor_tensor(out=dtd[:], in0=d4[:], in1=dtt[:, 4 * c:4 * c + 4], op=ALU.mult), reads=[d4, dtt], writes=[dtd])
            P.op("dve", lambda e: e.tensor_copy(out=cd_[:], in_=eA[:, :, 127]), reads=[eA], writes=[cd_])
            P.op("pe", lambda e: e.matmul(CBp[:], lhsT=BcT[:, ccols], rhs=CcT[:, ccols], start=True, stop=True), reads=[BcT, CcT], writes=[CBp])
            P.op("dve", lambda e: e.tensor_tensor(out=CBm[:], in0=CBp[:], in1=C.triuf[:], op=ALU.mult), reads=[CBp, C.triuf], writes=[CBm])
            P.op("dve", lambda e: e.tensor_tensor(out=MT_[:], in0=Ee[:], in1=CBm[:].unsqueeze(1).to_broadcast([128, 4, 128]), op=ALU.mult), reads=[Ee, CBm], writes=[MT_])
            P.op("pool", lambda e: e.tensor_tensor(out=Cs_[:], in0=eA[:], in1=CcT[:, ccols].unsqueeze(1).to_broadcast([128, 4, 128]), op=ALU.mult), reads=[eA, CcT], writes=[Cs_])
            XTp = XTs[par]
            for j in range(2):
                P.op("pe", lambda e, j=j: e.transpose(out=XTp[:, j * 128:(j + 1) * 128], in_=xc[j][:, ccols], identity=C.identf[:]), reads=[xc[j], C.identf], writes=[XTp])
            P.op("pe", lambda e: e.transpose(out=BTp[:], in_=BcT[:, ccols], identity=C.identb[:]), reads=[BcT, C.identb], writes=[BTp])
            P.op("act", lambda e: e.copy(out=Bc_[:], in_=BTp[:]), reads=[BTp], writes=[Bc_])
            P.op("dve", lambda e: e.tensor_tensor(out=xs_[:], in0=XTp[:].rearrange("p (h d) -> p h d", h=4), in1=dtt[:, 4 * c:4 * c + 4].unsqueeze(2).to_broadcast([128, 4, 64]), op=ALU.mult),
                 reads=[XTp, dtt], writes=[xs_])
            P.op("dve", lambda e: e.tensor_tensor(out=xsd_[:], in0=XTp[:].rearrange("p (h d) -> p h d", h=4), in1=dtd[:].unsqueeze(2).to_broadcast([128, 4, 64]), op=ALU.mult),
                 reads=[XTp, dtd], writes=[xsd_])

        def ssd_B(c):
            ccols = slice(c * 128, (c + 1) * 128)
            par = c % 2
            MT_, Cs_, xs_, xsd_, Bc_, cd_ = MT2[par], Cs2[par], xs2[par], xsd2[par], Bc2[par], cd2[par]
            for h in range(4):
                yb_ = OT[h // 2]
                po = (h % 2) * 64
                P.op("pe", lambda e, h=h: e.matmul(yb_[po:po + 64, ccols], lhsT=xs_[:, h, :], rhs=MT_[:, h, :], start=True, stop=False),
                     reads=[xs_, MT_], writes=[yb_])
                P.op("pe", lambda e, h=h: e.matmul(yb_[po:po + 64, ccols], lhsT=STb[:, h, :], rhs=Cs_[:, h, :], start=False, stop=True),
                     reads=[STb, Cs_], writes=[yb_])
            P.op("pe", lambda e: e.matmul(STp[:], lhsT=Bc_[:], rhs=xsd_[:].rearrange("p h d -> p (h d)"), start=True, stop=True), reads=[Bc_, xsd_], writes=[STp])
            P.op("dve", lambda e: e.tensor_tensor(out=ST[:], in0=ST[:], in1=cd_[:].unsqueeze(2).to_broadcast([128, 4, 64]), op=ALU.mult), reads=[ST, cd_], writes=[ST])
            P.op("dve", lambda e: e.tensor_tensor(out=ST[:], in0=ST[:], in1=STp[:].rearrange("p (h d) -> p h d", h=4), op=ALU.add), reads=[ST, STp], writes=[ST])
            P.op("act", lambda e: e.copy(out=STb[:], in_=ST[:]), reads=[ST], writes=[STb])

        if "ssd" in PARTS:
            ssd_A(0)
            for c in range(4):
                if c + 1 < 4:
                    ssd_A(c + 1)
                ssd_B(c)
        for j in range(2):
            P.op("dve", lambda e, j=j: e.scalar_tensor_tensor(out=yf[:], in0=xc[j][:], scalar=pv[:, 28 + j:29 + j], in1=OT[j][:], op0=ALU.mult, op1=ALU.add),
                 reads=[xc[j], pv, OT[j]], writes=[yf])
            P.op("pool", lambda e, j=j: e.tensor_tensor(out=vo[j][:], in0=yf[:], in1=sg[f"z{j}"][:], op=ALU.mult), reads=[yf, sg[f"z{j}"]], writes=[vo[j]])

        for i in range(4):
            P.dma("pool", lambda e, i=i: e.dma_start(out=yT_d[i * 64:(i + 1) * 64, sl], in_=yo[i][:]), reads=[yo[i]])
        for j in range(2):
            P.dma("pool", lambda e, j=j: e.dma_start(out=yT_d[256 + j * 128:256 + (j + 1) * 128, sl], in_=vo[j][:]), reads=[vo[j]])


def build_A(T):
    nc = bass.Bass("TRN2", target_bir_lowering=False)
    with ExitStack() as st:
        P = Prog(nc, st)
        C = alloc_common(P, T)
        hT_d = P.dram("hT", [D, T], BF16, "ExternalInput")
        wfm_d = P.dram("wfm", [D, NFM], F32, "ExternalInput")
        wtm_d = P.dram("wtm", [D, NTM], F32, "ExternalInput")
        pvec_d = P.dram("pvec", [128, NPV], F32, "ExternalInput")
        cos_d = P.dram("cos2", [128, T], F32, "ExternalInput")
        sin_d = P.dram("sin2", [128, T], F32, "ExternalInput")
        maskS_d = P.dram("maskS", [128, 256], F32, "ExternalInput")
        yT_d = P.dram("yT", [512, T], BF16, "ExternalOutput")
        emit_phaseA(P, C, hT_d, wfm_d, wtm_d, pvec_d, cos_d, sin_d, maskS_d, yT_d)
        stats = P.emit()
        print("phaseA ops", len(P.ops), stats, "sems", P.n_sems)
    return nc


NPVB = 32


def host_pvecB(inp, l):
    pv = np.zeros((128, NPVB), np.float32)
    pv[:, 0:8] = inp["norm_xq_w"][l].reshape(8, 128).T
    pv[:, 8:16] = inp["norm_mem_w"][l].reshape(8, 128).T
    nw = inp["ssm_norm_w"][l]
    for hg in range(4):
        pv[:, 16 + 4 * hg + 0] = 1.0
        pv[:, 16 + 4 * hg + 1] = 1.0
        pv[:, 16 + 4 * hg + 2] = nw[256 * hg:256 * hg + 128]
        pv[:, 16 + 4 * hg + 3] = nw[256 * hg + 128:256 * hg + 256]
    return pv


def wout_row_perm():
    rows = []
    for hg in range(4):
        rows.append(np.arange(2 * hg * 64, (2 * hg + 2) * 64))
        rows.append(512 + np.arange(2 * hg * 64, (2 * hg + 2) * 64))
        rows.append(1024 + np.arange(256 * hg, 256 * (hg + 1)))
    return np.concatenate(rows)


def load_weight(P, stg, dst, src_d, KC, N, scale_fn, eng_cycle=("dve", "act")):
    src_v = src_d.t.rearrange("(k p) n -> p k n", p=128)
    for k in range(KC):
        s_ = stg[k % 2]
        P.dma("sp", lambda e: e.dma_start(out=s_[:, 0:N], in_=src_v[:, k, :]), writes=[s_])
        sc = scale_fn(k) if scale_fn is not None else None
        eng = eng_cycle[k % len(eng_cycle)]
        if sc is None:
            if eng == "act":
                P.op(eng, lambda e: e.copy(out=dst[:, k, :], in_=s_[:, 0:N]), reads=[s_], writes=[dst])
            else:
                P.op(eng, lambda e: e.tensor_copy(out=dst[:, k, :], in_=s_[:, 0:N]), reads=[s_], writes=[dst])
        else:
            tl, ap = sc
            if eng == "act":
                P.op(eng, lambda e: e.activation(out=dst[:, k, :], in_=s_[:, 0:N], func=AF.Copy, scale=ap), reads=[s_, tl], writes=[dst])
            else:
                P.op(eng, lambda e: e.tensor_scalar(out=dst[:, k, :], in0=s_[:, 0:N], scalar1=ap, scalar2=None, op0=ALU.mult), reads=[s_, tl], writes=[dst])


def emit_norm_T(P, C, xin, xin_res, hT_out, col0, sq, ss, hb, want_T=True):
    P.op("act", lambda e: e.activation(out=sq[:], in_=xin, func=AF.Square, accum_out=ss[:]), reads=[xin_res], writes=[sq, ss])
    P.op("act", lambda e: e.activation(out=ss[:], in_=ss[:], func=AF.Ln, scale=1.0 / D, bias=EPS), reads=[ss], writes=[ss])
    P.op("act", lambda e: e.activation(out=ss[:], in_=ss[:], func=AF.Exp, scale=-0.5), reads=[ss], writes=[ss])
    if not want_T:
        return
    P.op("dve", lambda e: e.tensor_scalar(out=hb[:], in0=xin, scalar1=ss[:], scalar2=None, op0=ALU.mult), reads=[xin_res, ss], writes=[hb])
    tp = C.TPb
    for k in range(8):
        P.op("pe", lambda e, k=k: e.transpose(out=tp[:, k * 128:(k + 1) * 128], in_=hb[:, k * 128:(k + 1) * 128], identity=C.identb[:]), reads=[hb, C.identb], writes=[tp])
    P.op("act", lambda e: e.copy(out=hT_out[:, :, col0:col0 + 128], in_=tp[:].rearrange("p (k t) -> p k t", k=8)), reads=[tp], writes=[hT_out])


def emit_phaseB(P, C, TQ, x_d, yT_d, wout_d, pvB_d, wmq_d, wmk_d, wmv_d, wmo_d, mem_d, xout_d, hTn_d, out_d, fnw_d, final, tag="B"):
    NTB = TQ // TT
    PJ, SC, OT, MS = C.PJ, C.SC, C.OT, C.MS
    C.TPb = Tile(MS[1][:].bitcast(BF16), "TPb")
    C.TPb.r = MS[1].r
    pvB = P.sbuf(tag + "pvB", [128, NPVB], F32)
    P.dma("sp", lambda e: e.dma_start(out=pvB[:], in_=pvB_d[:]), writes=[pvB])
    stg = [P.sbuf(tag + f"stg{i}", [128, 1024], F32) for i in range(2)]
    WO = P.sbuf(tag + "WO", [128, 16, 1024], BF16)
    WQ = P.sbuf(tag + "WQ", [128, 8, 1024], BF16)
    WMO = P.sbuf(tag + "WMO", [128, 8, 1024], BF16)
    WT = P.sbuf(tag + "WT", [128, 8, 1024], BF16)
    KmT = P.sbuf(tag + "KmT", [128, 8, 256], BF16)
    Vm = P.sbuf(tag + "Vm", [128, 2, 1024], BF16)
    memnT = P.sbuf(tag + "memnT", [128, 8, 256], BF16)
    xt = P.sbuf(tag + "xt", [128, 4, 1024], F32)
    yt = P.sbuf(tag + "yt", [128, 16, TT], BF16)
    sq8 = P.sbuf(tag + "sq8", [128, 8, TT], BF16)
    sq = P.sbuf(tag + "sq", [128, 1024], F32)
    ss = P.sbuf(tag + "ss", [128, 1], F32)
    rs = P.sbuf(tag + "rs", [128, 4], F32)
    hb = P.sbuf(tag + "hb", [128, 1024], BF16)
    hqT = P.sbuf(tag + "hqT", [128, 8, TT], BF16)
    qT = P.sbuf(tag + "qT", [128, 8, TT], BF16)
    oT = P.sbuf(tag + "oT", [128, 8, TT], BF16)
    hTn = P.sbuf(tag + "hTn", [128, 8, TT], BF16)
    ptm = [P.sbuf(tag + f"ptm{i}", [128, TT], BF16) for i in range(2)]
    rden = P.sbuf(tag + "rden", [128, TT], F32)
    if final:
        fnw = P.sbuf(tag + "fnw", [128, 1024], F32)
        P.dma("sp", lambda e: e.dma_start(out=fnw[:], in_=fnw_d[:]), writes=[fnw])
        ob = P.sbuf(tag + "ob", [128, 1024], F32)

    for mb in range(2):
        P.dma("sp", lambda e: e.dma_start(out=xt[:, mb, :], in_=mem_d[mb * 128:(mb + 1) * 128, :]), writes=[xt])
        emit_norm_T(P, C, xt[:, mb, :], xt, memnT, mb * 128, sq, ss, hb)
    load_weight(P, stg, WT, wmk_d, 8, 1024, lambda k: (pvB, pvB[:, 8 + k:9 + k]))
    for c in range(8):
        bank = PJ[c % 2]
        for k in range(8):
            P.op("pe", lambda e, k=k: e.matmul(bank[:, 0:256], lhsT=WT[:, k, c * 128:(c + 1) * 128], rhs=memnT[:, k, :], start=(k == 0), stop=(k == 7)), reads=[WT, memnT], writes=[bank])
        P.op("act", lambda e: e.copy(out=KmT[:, c, :], in_=bank[:, 0:256]), reads=[bank], writes=[KmT])
    load_weight(P, stg, WT, wmv_d, 8, 1024, lambda k: (pvB, pvB[:, 8 + k:9 + k]))
    for mc in range(2):
        for half in range(2):
            bank = PJ[(mc * 2 + half) % 2]
            for k in range(8):
                P.op("pe", lambda e, k=k: e.matmul(bank[:], lhsT=memnT[:, k, mc * 128:(mc + 1) * 128], rhs=WT[:, k, half * 512:(half + 1) * 512], start=(k == 0), stop=(k == 7)), reads=[WT, memnT], writes=[bank])
            P.op("act", lambda e: e.copy(out=Vm[:, mc, half * 512:(half + 1) * 512], in_=bank[:]), reads=[bank], writes=[Vm])
    load_weight(P, stg, WO, wout_d, 16, 1024, lambda k: (pvB, pvB[:, 16 + k:17 + k]))
    load_weight(P, stg, WQ, wmq_d, 8, 1024, lambda k: (pvB, pvB[:, k:k + 1]))
    load_weight(P, stg, WMO, wmo_d, 8, 1024, None)

    yT_v = yT_d.t.rearrange("(c p) t -> p c t", p=128)
    x_v = x_d.t.rearrange("(n p) d -> p n d", p=128)
    xo_v = xout_d.t.rearrange("(n p) d -> p n d", p=128) if xout_d is not None else None
    out_v = out_d.t.rearrange("(n p) d -> p n d", p=128) if out_d is not None else None
    attn_ch = [c for c in range(16) if c % 4 < 2]
    ssm_ch = [c for c in range(16) if c % 4 >= 2]
    SSp = Tile(MS[0][:, 0:4], "SSp")
    SSp.r = MS[0].r
    for tb in range(NTB):
        sl = slice(tb * TT, (tb + 1) * TT)
        P.dma("sp", lambda e: e.dma_start(out=yt[:], in_=yT_v[:, :, sl]), writes=[yt])
        P.dma("sp", lambda e: e.dma_start(out=xt[:], in_=x_v[:, tb * 4:(tb + 1) * 4, :]), writes=[xt])
        for j, c in enumerate(ssm_ch):
            P.op("act", lambda e: e.activation(out=sq8[:, j, :], in_=yt[:, c, :], func=AF.Square), reads=[yt], writes=[sq8])
        for i in range(4):
            for j in range(8):
                P.op("pe", lambda e: e.matmul(SSp[:, i:i + 1], lhsT=sq8[:, j, i * 128:(i + 1) * 128], rhs=C.onesb[:, 0:1], start=(j == 0), stop=(j == 7)), reads=[sq8, C.onesb], writes=[SSp])
        P.op("act", lambda e: e.activation(out=rs[:], in_=SSp[:], func=AF.Ln, scale=1.0 / 1024, bias=EPS), reads=[SSp], writes=[rs])
        P.op("act", lambda e: e.activation(out=rs[:], in_=rs[:], func=AF.Exp, scale=-0.5), reads=[rs], writes=[rs])
        for i in range(4):
            for half in range(2):
                hs = slice(half * 512, (half + 1) * 512)
                A, S = PJ[0], PJ[1]
                for n_, c in enumerate(attn_ch):
                    P.op("pe", lambda e: e.matmul(A[:], lhsT=yt[:, c, i * 128:(i + 1) * 128], rhs=WO[:, c, hs], start=(n_ == 0), stop=(n_ == 7)), reads=[yt, WO], writes=[A])
                for n_, c in enumerate(ssm_ch):
                    P.op("pe", lambda e: e.matmul(S[:], lhsT=yt[:, c, i * 128:(i + 1) * 128], rhs=WO[:, c, hs], start=(n_ == 0), stop=(n_ == 7)), reads=[yt, WO], writes=[S])
                P.op("dve", lambda e: e.tensor_tensor(out=xt[:, i, hs], in0=A[:], in1=xt[:, i, hs], op=ALU.add), reads=[A, xt], writes=[xt])
                P.op("dve", lambda e: e.scalar_tensor_tensor(out=xt[:, i, hs], in0=S[:], scalar=rs[:, i:i + 1], in1=xt[:, i, hs], op0=ALU.mult, op1=ALU.add), reads=[S, rs, xt], writes=[xt])
            emit_norm_T(P, C, xt[:, i, :], xt, hqT, i * 128, sq, ss, hb)
        for c in range(8):
            bank = PJ[c % 2]
            for k in range(8):
                P.op("pe", lambda e, k=k: e.matmul(bank[:], lhsT=WQ[:, k, c * 128:(c + 1) * 128], rhs=hqT[:, k, :], start=(k == 0), stop=(k == 7)), reads=[WQ, hqT], writes=[bank])
            P.op("act", lambda e: e.activation(out=qT[:, c, :], in_=bank[:], func=AF.Copy, scale=1.0 / 16), reads=[bank], writes=[qT])
        for h in range(4):
            for mc in range(2):
                scb = SC[mc]
                for dc in range(2):
                    P.op("pe", lambda e: e.matmul(scb[:], lhsT=KmT[:, 2 * h + dc, mc * 128:(mc + 1) * 128], rhs=qT[:, 2 * h + dc, :], start=(dc == 0), stop=(dc == 1)), reads=[KmT, qT], writes=[scb])
                P.op("act", lambda e: e.activation(out=ptm[mc][:], in_=scb[:], func=AF.Exp), reads=[scb], writes=[ptm[mc]])
            den = MS[0]
            for mc in range(2):
                P.op("pe", lambda e: e.matmul(den[:], lhsT=C.onesb[:], rhs=ptm[mc][:], start=(mc == 0), stop=(mc == 1)), reads=[C.onesb, ptm[mc]], writes=[den])
            P.op("act", lambda e: e.activation(out=rden[:], in_=den[:], func=AF.Ln), reads=[den], writes=[rden])
            P.op("act", lambda e: e.activation(out=rden[:], in_=rden[:], func=AF.Exp, scale=-1.0), reads=[rden], writes=[rden])
            for dc in range(2):
                ob_ = OT[dc]
                for mc in range(2):
                    P.op("pe", lambda e: e.matmul(ob_[:], lhsT=Vm[:, mc, (2 * h + dc) * 128:(2 * h + dc + 1) * 128], rhs=ptm[mc][:], start=(mc == 0), stop=(mc == 1)), reads=[Vm, ptm[mc]], writes=[ob_])
                P.op("dve", lambda e: e.tensor_tensor(out=oT[:, 2 * h + dc, :], in0=ob_[:], in1=rden[:], op=ALU.mult), reads=[ob_, rden], writes=[oT])
        for i in range(4):
            for half in range(2):
                hs = slice(half * 512, (half + 1) * 512)
                bank = PJ[half]
                for k in range(8):
                    P.op("pe", lambda e, k=k: e.matmul(bank[:], lhsT=oT[:, k, i * 128:(i + 1) * 128], rhs=WMO[:, k, hs], start=(k == 0), stop=(k == 7)), reads=[oT, WMO], writes=[bank])
                P.op("dve", lambda e: e.tensor_tensor(out=xt[:, i, hs], in0=bank[:], in1=xt[:, i, hs], op=ALU.add), reads=[bank, xt], writes=[xt])
            if final:
                emit_norm_T(P, C, xt[:, i, :], xt, None, 0, sq, ss, hb, want_T=False)
                P.op("dve", lambda e: e.scalar_tensor_tensor(out=ob[:], in0=xt[:, i, :], scalar=ss[:], in1=fnw[:], op0=ALU.mult, op1=ALU.mult), reads=[xt, ss, fnw], writes=[ob])
                P.dma("pool", lambda e: e.dma_start(out=out_v[:, tb * 4 + i, :], in_=ob[:]), reads=[ob])
            else:
                emit_norm_T(P, C, xt[:, i, :], xt, hTn, i * 128, sq, ss, hb)
        if not final:
            P.dma("pool", lambda e: e.dma_start(out=xo_v[:, tb * 4:(tb + 1) * 4, :], in_=xt[:]), reads=[xt])
            P.dma("pool", lambda e: e.dma_start(out=hTn_d.t.rearrange("(k p) t -> p k t", p=128)[:, :, sl], in_=hTn[:]), reads=[hTn])


def build_B(TQ, final):
    nc = bass.Bass("TRN2", target_bir_lowering=False)
    with ExitStack() as st:
        P = Prog(nc, st)
        C = alloc_common(P, TQ)
        x_d = P.dram("x", [TQ, D], F32, "ExternalInput")
        yT_d = P.dram("yT", [2048, TQ], BF16, "ExternalInput")
        wout_d = P.dram("wout", [2048, D], F32, "ExternalInput")
        pvB_d = P.dram("pvB", [128, NPVB], F32, "ExternalInput")
        wmq_d = P.dram("wmq", [D, D], F32, "ExternalInput")
        wmk_d = P.dram("wmk", [D, D], F32, "ExternalInput")
        wmv_d = P.dram("wmv", [D, D], F32, "ExternalInput")
        wmo_d = P.dram("wmo", [D, D], F32, "ExternalInput")
        mem_d = P.dram("mem", [MEMT, D], F32, "ExternalInput")
        if final:
            fnw_d = P.dram("fnw", [128, D], F32, "ExternalInput")
            out_d = P.dram("out", [TQ, D], F32, "ExternalOutput")
            xout_d = hTn_d = None
        else:
            fnw_d = out_d = None
            xout_d = P.dram("xout", [TQ, D], F32, "ExternalOutput")
            hTn_d = P.dram("hTn", [D, TQ], BF16, "ExternalOutput")
        emit_phaseB(P, C, TQ, x_d, yT_d, wout_d, pvB_d, wmq_d, wmk_d, wmv_d, wmo_d, mem_d, xout_d, hTn_d, out_d, fnw_d, final)
        stats = P.emit()
        print("phaseB ops", len(P.ops), stats, "sems", P.n_sems)
    return nc


def build_N(TQ):
    nc = bass.Bass("TRN2", target_bir_lowering=False)
    with ExitStack() as st:
        P = Prog(nc, st)
        C = alloc_common(P, TQ)
        C.TPb = Tile(C.MS[1][:].bitcast(BF16), "TPb")
        C.TPb.r = C.MS[1].r
        x_d = P.dram("x", [TQ, D], F32, "ExternalInput")
        hTn_d = P.dram("hTn", [D, TQ], BF16, "ExternalOutput")
        xt = P.sbuf("Nxt", [128, 4, 1024], F32)
        sq = P.sbuf("Nsq", [128, 1024], F32)
        ss = P.sbuf("Nss", [128, 1], F32)
        hb = P.sbuf("Nhb", [128, 1024], BF16)
        hTn = P.sbuf("NhTn", [128, 8, TT], BF16)
        x_v = x_d.t.rearrange("(n p) d -> p n d", p=128)
        for tb in range(TQ // TT):
            sl = slice(tb * TT, (tb + 1) * TT)
            P.dma("sp", lambda e: e.dma_start(out=xt[:], in_=x_v[:, tb * 4:(tb + 1) * 4, :]), writes=[xt])
            for i in range(4):
                emit_norm_T(P, C, xt[:, i, :], xt, hTn, i * 128, sq, ss, hb)
            P.dma("pool", lambda e: e.dma_start(out=hTn_d.t.rearrange("(k p) t -> p k t", p=128)[:, :, sl], in_=hTn[:]), reads=[hTn])
        stats = P.emit()
        print("phaseN ops", len(P.ops), stats, "sems", P.n_sems)
    return nc


def _run(nc, in_maps):
    res = run_bass_kernel_spmd(nc, in_maps, core_ids=list(range(8)))
    return res.results


def kernel_unfused(**inputs):
    inp = {k: np.asarray(v) for k, v in inputs.items()}
    T = SEQ
    TQ = T // 4
    x = inp["x"]
    consts = host_consts(T)
    common = {"ident": consts["ident"], "triu": consts["triu"]}
    cores = [(c // 4, c % 4) for c in range(8)]
    ncN = build_N(TQ)
    r = _run(ncN, [dict(x=np.ascontiguousarray(x[b, q * TQ:(q + 1) * TQ]), **common) for b, q in cores])
    hT_full = [np.concatenate([np.asarray(r[b * 4 + q]["hTn"]) for q in range(4)], axis=1) for b in range(NB_)]
    x_cur = [np.ascontiguousarray(x[b, q * TQ:(q + 1) * TQ]) for b, q in cores]
    ncA = build_A(T)
    perm = wout_row_perm()
    out = None
    for l in range(DEPTH):
        maps = []
        for b, hg in cores:
            fm, tm = fm_cols(hg)
            maps.append(dict(hT=hT_full[b], wfm=np.ascontiguousarray(inp["w_in"][l][:, fm]), wtm=np.ascontiguousarray(inp["w_in"][l][:, tm]),
                             pvec=host_pvec(inp, l, hg), cos2=consts["cos2"], sin2=consts["sin2"], maskS=consts["maskS"], **common))
        rA = _run(ncA, maps)
        final = (l == DEPTH - 1)
        ncB = build_B(TQ, final)
        maps = []
        wout_p = np.ascontiguousarray(inp["w_out"][l][perm])
        pvB = host_pvecB(inp, l)
        for ci, (b, q) in enumerate(cores):
            yT_own = np.concatenate([np.asarray(rA[b * 4 + hg]["yT"])[:, q * TQ:(q + 1) * TQ] for hg in range(4)], axis=0)
            m = dict(x=x_cur[ci], yT=np.ascontiguousarray(yT_own), wout=wout_p, pvB=pvB, wmq=inp["w_mq"][l], wmk=inp["w_mk"][l],
                     wmv=inp["w_mv"][l], wmo=inp["w_mo"][l], mem=np.ascontiguousarray(inp["mem"][b]), **common)
            if final:
                m["fnw"] = np.ascontiguousarray(np.broadcast_to(inp["final_norm_w"][None, :], (128, D)))
            maps.append(m)
        rB = _run(ncB, maps)
        if final:
            out = np.stack([np.concatenate([np.asarray(rB[b * 4 + q]["out"]) for q in range(4)], axis=0) for b in range(NB_)], axis=0)
        else:
            x_cur = [np.asarray(rB[ci]["xout"]) for ci in range(8)]
            hT_full = [np.concatenate([np.asarray(rB[b * 4 + q]["hTn"]) for q in range(4)], axis=1) for b in range(NB_)]
    return out.astype(np.float32)


def _sub(tile_, ap, name):
    t = Tile(ap, name)
    return t


def build_fused(T, depth=DEPTH):
    nc = bass.Bass("TRN2", target_bir_lowering=False)
    with ExitStack() as st:
        P = Prog(nc, st)
        C = alloc_common(P, T)
        C.TPb = Tile(C.MS[1][:].bitcast(BF16), "TPb")
        C.TPb.r = C.MS[1].r
        x_d = P.dram("x", [T, D], F32, "ExternalInput")
        mem_d = P.dram("mem", [MEMT, D], F32, "ExternalInput")
        wfm_d = P.dram("wfm", [depth * 4 * D, NFM], F32, "ExternalInput")
        wtm_d = P.dram("wtm", [depth * 4 * D, NTM], F32, "ExternalInput")
        pvec_d = P.dram("pvec", [depth * 4 * 128, NPV], F32, "ExternalInput")
        wout_d = P.dram("wout", [depth * 2048, D], F32, "ExternalInput")
        pvB_d = P.dram("pvB", [depth * 128, NPVB], F32, "ExternalInput")
        wm_d = {n: P.dram(n, [depth * D, D], F32, "ExternalInput") for n in ("wmq", "wmk", "wmv", "wmo")}
        cos_d = P.dram("cos2", [128, T], F32, "ExternalInput")
        sin_d = P.dram("sin2", [128, T], F32, "ExternalInput")
        maskS_d = P.dram("maskS", [128, 256], F32, "ExternalInput")
        fnw_d = P.dram("fnw", [128, D], F32, "ExternalInput")
        out_d = P.dram("out", [T, D], F32, "ExternalOutput")
        hT_s = P.dram("hT_s", [D, T], BF16, "Internal")
        yT_s = P.dram("yT_s", [2048, T], BF16, "Internal")
        x1_s = P.dram("x1_s", [T, D], F32, "Internal")
        outer = P.stack

        def phase(fn):
            with ExitStack() as ph:
                P.stack = ph
                fn()
                P.barrier()
            P.stack = outer

        P.barrier()

        def phN():
            xt = P.sbuf("Nxt", [128, 4, 1024], F32)
            sq = P.sbuf("Nsq", [128, 1024], F32)
            ss = P.sbuf("Nss", [128, 1], F32)
            hb = P.sbuf("Nhb", [128, 1024], BF16)
            hTn = P.sbuf("NhTn", [128, 8, TT], BF16)
            x_v = x_d.t.rearrange("(n p) d -> p n d", p=128)
            for tb in range(T // TT):
                sl = slice(tb * TT, (tb + 1) * TT)
                P.dma("sp", lambda e: e.dma_start(out=xt[:], in_=x_v[:, tb * 4:(tb + 1) * 4, :]), writes=[xt])
                for i in range(4):
                    emit_norm_T(P, C, xt[:, i, :], xt, hTn, i * 128, sq, ss, hb)
                P.dma("pool", lambda e: e.dma_start(out=hT_s.t.rearrange("(k p) t -> p k t", p=128)[:, :, sl], in_=hTn[:]), reads=[hTn])
        phase(phN)
        for l in range(depth):
            for hg in range(4):
                i_ = l * 4 + hg
                phase(lambda: emit_phaseA(P, C, hT_s, Tile(wfm_d[i_ * D:(i_ + 1) * D, :], "wfm_s"), Tile(wtm_d[i_ * D:(i_ + 1) * D, :], "wtm_s"),
                                          Tile(pvec_d[i_ * 128:(i_ + 1) * 128, :], "pv_s"), cos_d, sin_d, maskS_d,
                                          Tile(yT_s[hg * 512:(hg + 1) * 512, :], "yT_sub"), tag=f"A{l}{hg}"))
            final = (l == depth - 1)
            wsl = lambda n: Tile(wm_d[n][l * D:(l + 1) * D, :], n + "_s")
            phase(lambda: emit_phaseB(P, C, T, x_d if l == 0 else x1_s, yT_s, Tile(wout_d[l * 2048:(l + 1) * 2048, :], "wo_s"),
                                      Tile(pvB_d[l * 128:(l + 1) * 128, :], "pvB_s"), wsl("wmq"), wsl("wmk"), wsl("wmv"), wsl("wmo"), mem_d,
                                      None if final else x1_s, None if final else hT_s, out_d if final else None, fnw_d if final else None, final, tag=f"B{l}"))
        stats = P.emit()
        print("fused ops", len(P.ops), stats, "sems", P.n_sems)
    return nc


def fused_inputs(inp, T, depth=DEPTH):
    consts = host_consts(T)
    perm = wout_row_perm()
    cols = [fm_cols(hg) for hg in range(4)]
    shared = dict(
        wfm=np.concatenate([inp["w_in"][l][:, cols[hg][0]] for l in range(depth) for hg in range(4)], axis=0),
        wtm=np.concatenate([inp["w_in"][l][:, cols[hg][1]] for l in range(depth) for hg in range(4)], axis=0),
        pvec=np.concatenate([host_pvec(inp, l, hg) for l in range(depth) for hg in range(4)], axis=0),
        wout=np.concatenate([inp["w_out"][l][perm] for l in range(depth)], axis=0),
        pvB=np.concatenate([host_pvecB(inp, l) for l in range(depth)], axis=0),
        wmq=np.concatenate([inp["w_mq"][l] for l in range(depth)], axis=0),
        wmk=np.concatenate([inp["w_mk"][l] for l in range(depth)], axis=0),
        wmv=np.concatenate([inp["w_mv"][l] for l in range(depth)], axis=0),
        wmo=np.concatenate([inp["w_mo"][l] for l in range(depth)], axis=0),
        cos2=consts["cos2"], sin2=consts["sin2"], maskS=consts["maskS"], ident=consts["ident"], triu=consts["triu"],
        fnw=np.ascontiguousarray(np.broadcast_to(inp["final_norm_w"][None, :], (128, D))),
    )
    return shared


def kernel_fused(**inputs):
    inp = {k: np.asarray(v) for k, v in inputs.items()}
    T = inp["x"].shape[1]
    shared = fused_inputs(inp, T)
    nc = build_fused(T)
    maps = []
    for c in range(8):
        b = c // 4
        maps.append(dict(x=np.ascontiguousarray(inp["x"][b]), mem=np.ascontiguousarray(inp["mem"][b]), **shared))
    r = _run(nc, maps)
    return np.stack([np.asarray(r[4 * b]["out"]) for b in range(NB_)], axis=0).astype(np.float32)


def kernel(**inputs):
    return kernel_unfused(**inputs)
```
